# Optimizing a Trainium2 kernel written in Bass

```python
import jax
import jax.numpy as jnp
from jax import lax
import numpy as np

D_MODEL = 1024
BATCH = 4
SEQ = 4096
DEPTH = 2

GRID_W = 64
CTX_LEN = 256
N_BRANCH = 4
BRANCH_W = D_MODEL // 4
HEAD_DIM = 64
MLA_HEADS = 4
MLA_NOPE = 64
MLA_ROPE = 32
MLA_V = 64
MLA_Q_RANK = 256
MLA_KV_RANK = 128
SWA_Q_HEADS = 4
SWA_KV_HEADS = 2
WINDOW = 128
SWA_BLOCK = 128
AXA_Q_HEADS = 4
AXA_KV_HEADS = 2
LRU_WIDTH = BRANCH_W
LRU_BLOCKS = 4
LRU_BLOCK = LRU_WIDTH // LRU_BLOCKS
LRU_C = 8.0
CONV_W = 4
CONV_LEFT = 2
Q_BLOCK = 128
ROPE_THETA = 10000.0
RMS_EPS = 1e-6
NEG_INF = -1e30

IN_SIZES = (MLA_Q_RANK, MLA_KV_RANK, MLA_ROPE,
            SWA_Q_HEADS * HEAD_DIM, SWA_KV_HEADS * HEAD_DIM, SWA_KV_HEADS * HEAD_DIM,
            AXA_Q_HEADS * HEAD_DIM, AXA_KV_HEADS * HEAD_DIM, AXA_KV_HEADS * HEAD_DIM,
            LRU_WIDTH,
            N_BRANCH * BRANCH_W,
            N_BRANCH * D_MODEL)
IN_COLS = sum(IN_SIZES)

kernel_name = 'hybrid_parallel_mla_swa_axial_rglru_dit'


def _rms_norm(x, gain):
    x32 = x.astype(jnp.float32)
    y = x32 * lax.rsqrt(jnp.mean(x32 * x32, axis=-1, keepdims=True) + RMS_EPS)
    return (y * gain.astype(jnp.float32)).astype(x.dtype)


def _split_in(z):
    idx, acc = [], 0
    for s in IN_SIZES[:-1]:
        acc += s
        idx.append(acc)
    return jnp.split(z, idx, axis=-1)


def _axial_rope(n, rot_dim):
    n_rows = n // GRID_W
    rows = jnp.repeat(jnp.arange(n_rows, dtype=jnp.float32), GRID_W)
    cols = jnp.tile(jnp.arange(GRID_W, dtype=jnp.float32), n_rows)
    quarter = rot_dim // 4
    freqs = ROPE_THETA ** (-jnp.arange(quarter, dtype=jnp.float32) / quarter)
    ang = jnp.concatenate([rows[:, None] * freqs, cols[:, None] * freqs], axis=-1)
    return jnp.cos(ang), jnp.sin(ang)


def _apply_rope(x, cos, sin):
    half = x.shape[-1] // 2
    cos = cos[None, :, None, :].astype(x.dtype)
    sin = sin[None, :, None, :].astype(x.dtype)
    x1, x2 = x[..., :half], x[..., half:]
    return jnp.concatenate([x1 * cos - x2 * sin, x2 * cos + x1 * sin], axis=-1)


def _rope_tail(x, cos, sin):
    r = 2 * cos.shape[-1]
    return jnp.concatenate([x[..., :-r], _apply_rope(x[..., -r:], cos, sin)], axis=-1)


def _short_conv(u, w, b):
    n = u.shape[1]
    up = jnp.pad(u, ((0, 0), (CONV_LEFT, CONV_W - 1 - CONV_LEFT), (0, 0)))
    y = b
    for j in range(CONV_W):
        y = y + up[:, j:j + n] * w[j]
    return y


def _softmax_with_sink(s, sink):
    m = jnp.maximum(jnp.max(s, axis=-1, keepdims=True), sink)
    p = jnp.exp(s - m)
    return p / (jnp.sum(p, axis=-1, keepdims=True) + jnp.exp(sink - m))


def _dense_attention(q, k, v):
    b, n, hkv, g, d = q.shape
    nb = n // Q_BLOCK
    scale = d ** -0.5
    qb = jnp.moveaxis(q.reshape(b, nb, Q_BLOCK, hkv, g, d), 1, 0)

    def block(qblk):
        s = jnp.einsum('bqhgd,bkhd->bhgqk', qblk, k).astype(jnp.float32) * scale
        p = jax.nn.softmax(s, axis=-1).astype(v.dtype)
        return jnp.einsum('bhgqk,bkhd->bqhgd', p, v)

    out = lax.map(block, qb)
    return jnp.moveaxis(out, 0, 1).reshape(b, n, -1)


def _window_attention(q, k, v, k_ctx, v_ctx, sink):
    b, n, hkv, g, d = q.shape
    w = SWA_BLOCK
    nb = n // w
    scale = d ** -0.5
    pad = ((0, 0), (w, w), (0, 0), (0, 0))
    kp = jnp.pad(k, pad).reshape(b, nb + 2, w, hkv, d)
    vp = jnp.pad(v, pad).reshape(b, nb + 2, w, hkv, v.shape[-1])
    kb = jnp.concatenate([kp[:, :-2], kp[:, 1:-1], kp[:, 2:]], axis=2)
    vb = jnp.concatenate([vp[:, :-2], vp[:, 1:-1], vp[:, 2:]], axis=2)
    qb = q.reshape(b, nb, w, hkv, g, d)
    s_loc = jnp.einsum('bnqhgd,bnkhd->bnhgqk', qb, kb).astype(jnp.float32) * scale
    s_ctx = jnp.einsum('bnqhgd,bkhd->bnhgqk', qb, k_ctx).astype(jnp.float32) * scale
    qi = jnp.arange(w)[:, None]
    ki = jnp.arange(3 * w)[None, :]
    key_pos = (jnp.arange(nb)[:, None, None] - 1) * w + ki[None]
    rel = ki - qi
    valid = ((rel >= w - WINDOW) & (rel <= w + WINDOW))[None] & (key_pos >= 0) & (key_pos < n)
    s_loc = jnp.where(valid[None, :, None, None], s_loc, NEG_INF)
    p = _softmax_with_sink(jnp.concatenate([s_loc, s_ctx], axis=-1),
                           sink[None, None, :, :, None, None])
    p_loc = p[..., :3 * w].astype(v.dtype)
    p_ctx = p[..., 3 * w:].astype(v.dtype)
    out = (jnp.einsum('bnhgqk,bnkhd->bnqhgd', p_loc, vb)
           + jnp.einsum('bnhgqk,bkhd->bnqhgd', p_ctx, v_ctx))
    return out.reshape(b, n, -1)


def _sink_attention(q, k, v, sink):
    b, n = q.shape[:2]
    scale = q.shape[-1] ** -0.5
    s = jnp.einsum('bqhgd,bkhd->bhgqk', q, k).astype(jnp.float32) * scale
    p = _softmax_with_sink(s, sink[None, :, :, None, None]).astype(v.dtype)
    return jnp.einsum('bhgqk,bkhd->bqhgd', p, v).reshape(b, n, -1)


def _scan_combine(left, right):
    a_l, b_l = left
    a_r, b_r = right
    return a_l * a_r, a_r * b_l + b_r


def _rglru_dir(u, w_r, b_r, w_i, b_i, lam, h0, reverse):
    u32 = u.astype(jnp.float32)
    shp = u32.shape
    ub = u32.reshape(shp[:-1] + (LRU_BLOCKS, LRU_BLOCK))
    r = jax.nn.sigmoid(jnp.einsum('bnki,kij->bnkj', ub, w_r.astype(jnp.float32)).reshape(shp)
                       + b_r.astype(jnp.float32))
    i = jax.nn.sigmoid(jnp.einsum('bnki,kij->bnkj', ub, w_i.astype(jnp.float32)).reshape(shp)
                       + b_i.astype(jnp.float32))
    log_a = -LRU_C * r * jax.nn.softplus(-lam.astype(jnp.float32))
    a = jnp.exp(log_a)
    bx = jnp.sqrt(-jnp.expm1(2.0 * log_a)) * (i * u32)
    a_cum, b_cum = lax.associative_scan(_scan_combine, (a, bx), axis=1, reverse=reverse)
    return a_cum * h0[:, None, :] + b_cum


def _rglru_branch(u_c, u_x, lp, want_ctx):
    bsz = u_x.shape[0]
    ys_x, ys_c = [], []
    for d, reverse in enumerate((False, True)):
        args = (lp['lru_w_r'][d], lp['lru_b_r'][d], lp['lru_w_i'][d], lp['lru_b_i'][d],
                lp['lru_lambda'][d])
        h_c = _rglru_dir(u_c, *args, jnp.zeros((bsz, LRU_WIDTH), jnp.float32), reverse)
        h_last = h_c[:, 0] if reverse else h_c[:, -1]
        ys_x.append(_rglru_dir(u_x, *args, h_last, reverse))
        ys_c.append(h_c)
    y_x = (ys_x[0] + ys_x[1]).astype(u_x.dtype)
    y_c = (ys_c[0] + ys_c[1]).astype(u_c.dtype) if want_ctx else None
    return y_x, y_c


def _project_stream(h, lp, rope, need_queries):
    b, n = h.shape[:2]
    (z_cq, z_ckv, z_kr, z_sq, z_sk, z_sv, z_aq, z_ak, z_av,
     z_lru, z_gate, z_merge) = _split_in(h @ lp['w_in'])
    out = {}
    ckv = _rms_norm(z_ckv, lp['mla_ckv_norm'])
    kv = (ckv @ lp['mla_w_ukv']).reshape(b, n, MLA_HEADS, MLA_NOPE + MLA_V)
    k_rope = jnp.broadcast_to(z_kr[:, :, None, :], (b, n, MLA_HEADS, MLA_ROPE))
    mla_k = _rms_norm(jnp.concatenate([kv[..., :MLA_NOPE], k_rope], axis=-1), lp['mla_k_norm'])
    out['mla_v'] = kv[..., MLA_NOPE:]
    swa_k = _rms_norm(z_sk.reshape(b, n, SWA_KV_HEADS, HEAD_DIM), lp['swa_k_norm'])
    out['swa_v'] = z_sv.reshape(b, n, SWA_KV_HEADS, HEAD_DIM)
    axa_k = _rms_norm(z_ak.reshape(b, n, AXA_KV_HEADS, HEAD_DIM), lp['axa_k_norm'])
    out['axa_v'] = z_av.reshape(b, n, AXA_KV_HEADS, HEAD_DIM)
    if rope is not None:
        mla_k = _rope_tail(mla_k, *rope['mla'])
        swa_k = _apply_rope(swa_k, *rope['head'])
        axa_k = _apply_rope(axa_k, *rope['head'])
    out['mla_k'] = mla_k
    out['swa_k'] = swa_k
    out['axa_k'] = axa_k
    out['lru_u'] = _short_conv(z_lru, lp['lru_conv_w'], lp['lru_conv_b'])
    if need_queries:
        cq = _rms_norm(z_cq, lp['mla_cq_norm'])
        mla_q = _rms_norm((cq @ lp['mla_w_uq']).reshape(b, n, MLA_HEADS, MLA_NOPE + MLA_ROPE),
                          lp['mla_q_norm'])
        swa_q = _rms_norm(z_sq.reshape(b, n, SWA_Q_HEADS, HEAD_DIM), lp['swa_q_norm'])
        axa_q = _rms_norm(z_aq.reshape(b, n, AXA_Q_HEADS, HEAD_DIM), lp['axa_q_norm'])
        if rope is not None:
            mla_q = _rope_tail(mla_q, *rope['mla'])
            swa_q = _apply_rope(swa_q, *rope['head'])
            axa_q = _apply_rope(axa_q, *rope['head'])
        out['mla_q'] = mla_q[:, :, :, None, :]
        out['swa_q'] = swa_q.reshape(b, n, SWA_KV_HEADS, SWA_Q_HEADS // SWA_KV_HEADS, HEAD_DIM)
        out['axa_q'] = axa_q.reshape(b, n, AXA_KV_HEADS, AXA_Q_HEADS // AXA_KV_HEADS, HEAD_DIM)
        out['gate'] = z_gate
        out['merge'] = z_merge
    return out


def _merge_branches(branches, z_gate, z_merge, lp):
    b, n = z_gate.shape[:2]
    y = jnp.stack(branches, axis=2) * jax.nn.silu(z_gate.reshape(b, n, N_BRANCH, BRANCH_W))
    proj = jnp.einsum('bnkw,kwd->bnkd', y, lp['w_branch'])
    mix = jnp.sum(jax.nn.sigmoid(z_merge.reshape(b, n, N_BRANCH, D_MODEL)) * proj, axis=2)
    return mix @ lp['w_out']


def _layer(x, ctx, mod_x, mod_c, lp, rope, update_ctx):
    shift_x, scale_x, gate_x = jnp.split(mod_x[:, None, :], 3, axis=-1)
    shift_c, scale_c, gate_c = jnp.split(mod_c, 3, axis=-1)
    hx = _rms_norm(x, lp['norm_w']) * (1 + scale_x) + shift_x
    hc = _rms_norm(ctx, lp['norm_w']) * (1 + scale_c) + shift_c
    px = _project_stream(hx, lp, rope, True)
    pc = _project_stream(hc, lp, None, update_ctx)
    sink = lp['swa_sink'].astype(jnp.float32).reshape(SWA_KV_HEADS, SWA_Q_HEADS // SWA_KV_HEADS)
    ya = _dense_attention(px['mla_q'], jnp.concatenate([pc['mla_k'], px['mla_k']], axis=1),
                          jnp.concatenate([pc['mla_v'], px['mla_v']], axis=1))
    yb = _window_attention(px['swa_q'], px['swa_k'], px['swa_v'], pc['swa_k'], pc['swa_v'], sink)
    yc = _dense_attention(px['axa_q'], jnp.concatenate([pc['axa_k'], px['axa_k']], axis=1),
                          jnp.concatenate([pc['axa_v'], px['axa_v']], axis=1))
    yd_x, yd_c = _rglru_branch(pc['lru_u'], px['lru_u'], lp, update_ctx)
    x = x + gate_x * _merge_branches((ya, yb, yc, yd_x), px['gate'], px['merge'], lp)
    if update_ctx:
        ya_c = _dense_attention(pc['mla_q'], pc['mla_k'], pc['mla_v'])
        yb_c = _sink_attention(pc['swa_q'], pc['swa_k'], pc['swa_v'], sink)
        yc_c = _dense_attention(pc['axa_q'], pc['axa_k'], pc['axa_v'])
        ctx = ctx + gate_c * _merge_branches((ya_c, yb_c, yc_c, yd_c), pc['gate'], pc['merge'], lp)
    return x, ctx


def setup_inputs(seed: int = 0) -> dict:
    key = jax.random.key(seed)
    ks = jax.random.split(key, 32)
    D = D_MODEL

    def nrm(k, shape, scale):
        return jax.random.normal(k, shape, jnp.float32) * scale

    def gain(k, shape):
        return 1.0 + 0.05 * jax.random.normal(k, shape, jnp.float32)

    a_target = jax.random.uniform(ks[25], (DEPTH, 2, LRU_WIDTH), jnp.float32, 0.9, 0.999)
    s_lam = a_target ** (1.0 / LRU_C)
    return {
        'x': nrm(ks[0], (BATCH, SEQ, D), 1.0),
        'c': nrm(ks[1], (BATCH, D), 1.0),
        'ctx': nrm(ks[2], (BATCH, CTX_LEN, D), 1.0),
        'c_ctx': nrm(ks[3], (D,), 1.0),
        'w_mod': nrm(ks[4], (DEPTH, D, 3 * D), 0.5 * D ** -0.5),
        'b_mod': nrm(ks[5], (DEPTH, 3 * D), 0.02),
        'norm_w': gain(ks[6], (DEPTH, D)),
        'w_in': nrm(ks[7], (DEPTH, D, IN_COLS), D ** -0.5),
        'mla_cq_norm': gain(ks[8], (DEPTH, MLA_Q_RANK)),
        'mla_ckv_norm': gain(ks[9], (DEPTH, MLA_KV_RANK)),
        'mla_w_uq': nrm(ks[10], (DEPTH, MLA_Q_RANK, MLA_HEADS * (MLA_NOPE + MLA_ROPE)), MLA_Q_RANK ** -0.5),
        'mla_w_ukv': nrm(ks[11], (DEPTH, MLA_KV_RANK, MLA_HEADS * (MLA_NOPE + MLA_V)), MLA_KV_RANK ** -0.5),
        'mla_q_norm': gain(ks[12], (DEPTH, MLA_NOPE + MLA_ROPE)),
        'mla_k_norm': gain(ks[13], (DEPTH, MLA_NOPE + MLA_ROPE)),
        'swa_q_norm': gain(ks[14], (DEPTH, HEAD_DIM)),
        'swa_k_norm': gain(ks[15], (DEPTH, HEAD_DIM)),
        'swa_sink': nrm(ks[16], (DEPTH, SWA_Q_HEADS), 0.5),
        'axa_q_norm': gain(ks[17], (DEPTH, HEAD_DIM)),
        'axa_k_norm': gain(ks[18], (DEPTH, HEAD_DIM)),
        'lru_conv_w': nrm(ks[19], (DEPTH, CONV_W, LRU_WIDTH), CONV_W ** -0.5),
        'lru_conv_b': nrm(ks[20], (DEPTH, LRU_WIDTH), 0.02),
        'lru_w_r': nrm(ks[21], (DEPTH, 2, LRU_BLOCKS, LRU_BLOCK, LRU_BLOCK), LRU_BLOCK ** -0.5),
        'lru_b_r': nrm(ks[22], (DEPTH, 2, LRU_WIDTH), 0.02),
        'lru_w_i': nrm(ks[23], (DEPTH, 2, LRU_BLOCKS, LRU_BLOCK, LRU_BLOCK), LRU_BLOCK ** -0.5),
        'lru_b_i': nrm(ks[24], (DEPTH, 2, LRU_WIDTH), 0.02),
        'lru_lambda': jnp.log(s_lam) - jnp.log1p(-s_lam),
        'w_branch': nrm(ks[26], (DEPTH, N_BRANCH, BRANCH_W, D), BRANCH_W ** -0.5),
        'w_out': nrm(ks[27], (DEPTH, D, D), D ** -0.5),
    }


def reference(x, c, ctx, c_ctx, w_mod, b_mod, norm_w, w_in, mla_cq_norm, mla_ckv_norm,
              mla_w_uq, mla_w_ukv, mla_q_norm, mla_k_norm, swa_q_norm, swa_k_norm, swa_sink,
              axa_q_norm, axa_k_norm, lru_conv_w, lru_conv_b, lru_w_r, lru_b_r, lru_w_i,
              lru_b_i, lru_lambda, w_branch, w_out):
    n = x.shape[1]
    rope = {'mla': _axial_rope(n, MLA_ROPE), 'head': _axial_rope(n, HEAD_DIM)}
    sc = jax.nn.silu(c)
    scc = jax.nn.silu(c_ctx)
    for l in range(DEPTH):
        lp = {
            'norm_w': norm_w[l], 'w_in': w_in[l],
            'mla_cq_norm': mla_cq_norm[l], 'mla_ckv_norm': mla_ckv_norm[l],
            'mla_w_uq': mla_w_uq[l], 'mla_w_ukv': mla_w_ukv[l],
            'mla_q_norm': mla_q_norm[l], 'mla_k_norm': mla_k_norm[l],
            'swa_q_norm': swa_q_norm[l], 'swa_k_norm': swa_k_norm[l], 'swa_sink': swa_sink[l],
            'axa_q_norm': axa_q_norm[l], 'axa_k_norm': axa_k_norm[l],
            'lru_conv_w': lru_conv_w[l], 'lru_conv_b': lru_conv_b[l],
            'lru_w_r': lru_w_r[l], 'lru_b_r': lru_b_r[l],
            'lru_w_i': lru_w_i[l], 'lru_b_i': lru_b_i[l], 'lru_lambda': lru_lambda[l],
            'w_branch': w_branch[l], 'w_out': w_out[l],
        }
        mod_x = sc @ w_mod[l] + b_mod[l]
        mod_c = scc @ w_mod[l] + b_mod[l]
        x, ctx = _layer(x, ctx, mod_x, mod_c, lp, rope, l < DEPTH - 1)
    return x
```

```python
import os
import numpy as np
from contextlib import ExitStack
import concourse.bass as bass
import concourse.mybir as mybir
from concourse.bass_utils import run_bass_kernel_spmd

F32 = mybir.dt.float32
BF16 = mybir.dt.bfloat16
AF = mybir.ActivationFunctionType
ALU = mybir.AluOpType

D = 1024
SEQ = 4096
HALF = 2048
CTX = 256
TF = CTX + SEQ
EPS = 1e-6
C_CQ, C_CKV, C_KR, C_SQ, C_SK, C_SV, C_AQ, C_AK, C_AV, C_LRU, C_GATE, C_MERGE = (
    0, 256, 384, 416, 672, 800, 928, 1184, 1312, 1440, 1696, 2720)
IN_COLS = 6816
V_NORMW, V_BSHIFT, V_BSCALE, V_GCQ, V_GCKV, V_GMQ, V_GMK, V_GSQ, V_GSK, V_GAQ, V_GAK = (
    0, 8, 16, 24, 26, 27, 28, 29, 30, 31, 32)
V_SINK, V_CONVW, V_CONVB, V_BR, V_BI, V_LAM, V_SEL = 33, 37, 45, 47, 51, 55, 59
NV = 64
M_ID, M_ONES64, M_ONES96, M_ONES128, M_SWAP64, M_SWAPM, M_EKR = range(7)


class Sched:
    ENG = {'pe': 'tensor', 'act': 'scalar', 'dve': 'vector', 'pool': 'gpsimd', 'sp': 'sync'}
    ROLL = 30000

    def __init__(self, nc, es):
        self.nc = nc
        self.es = es
        self.eng = {k: getattr(nc, v) for k, v in self.ENG.items()}
        self.nsem = 0
        self.sem = {}
        self.cnt = {}
        self.allsems = []
        for k in self.eng:
            self._roll(k)
        self.last_w = {}
        self.readers = {}
        self.seen = {k: {} for k in self.eng}
        self.dma_sems = {}
        self.nops = 0

    def _alloc_sem(self):
        self.nsem += 1
        return self.es.enter_context(self.nc.semaphore("s%d" % self.nsem))

    def _roll(self, e):
        self.sem[e] = self._alloc_sem()
        self.cnt[e] = 0

    def op(self, e, fn, reads=(), writes=(), dma=None):
        deps = []
        for k in reads:
            t = self.last_w.get(k)
            if t is not None:
                deps.append((t, 0))
            if isinstance(k, tuple) and k[0] in ('ps', 'pt'):
                for t in self.readers.get(k, ()):
                    deps.append((t, 2))
        for k in writes:
            t = self.last_w.get(k)
            if t is not None:
                deps.append((t, 1))
            for t in self.readers.get(k, ()):
                deps.append((t, 2))
        eng = self.eng[e]
        need = {}
        for (sem, val, pe, is_dma), kind in deps:
            if pe == e and (not is_dma) and dma is None and (kind == 2 or (kind == 1 and e == 'pe')):
                continue
            sid = id(sem)
            if self.seen[e].get(sid, 0) >= val:
                continue
            if sid not in need or need[sid][1] < val:
                need[sid] = (sem, val)
        for sem, val in need.values():
            eng.wait_ge(sem, val)
            self.seen[e][id(sem)] = val
        ins = fn(eng)
        self.nops += 1
        if dma is not None:
            s = self.dma_sems.get(dma)
            if s is None:
                s = self.dma_sems[dma] = [self._alloc_sem(), 0]
            s[1] += 16
            ins.then_inc(s[0], 16)
            tok = (s[0], s[1], e, True)
        else:
            if self.cnt[e] >= self.ROLL:
                self._roll(e)
            self.cnt[e] += 1
            ins.then_inc(self.sem[e], 1)
            tok = (self.sem[e], self.cnt[e], e, False)
        for k in reads:
            self.readers.setdefault(k, []).append(tok)
        for k in writes:
            self.last_w[k] = tok
            self.readers[k] = []
        return tok

    def wait_all(self, e, toks):
        eng = self.eng[e]
        for (sem, val, pe, is_dma) in toks:
            if self.seen[e].get(id(sem), 0) >= val:
                continue
            eng.wait_ge(sem, val)
            self.seen[e][id(sem)] = val

    def barrier(self):
        toks = [(self.sem[p], self.cnt[p], p, False) for p in self.eng if self.cnt[p] > 0]
        toks += [(s[0], s[1], None, True) for s in self.dma_sems.values()]
        for e in self.eng:
            self.wait_all(e, toks)


class Prog:
    def __init__(self, update_ctx):
        self.update_ctx = update_ctx
        self.nc = bass.Bass("TRN2", target_bir_lowering=False, num_devices=8)
        self.din = {}
        self.dout = {}

    def inp(self, name, shape, dt=F32):
        t = self.nc.dram_tensor(name, list(shape), dt, kind="ExternalInput").ap()
        self.din[name] = t
        return t

    def outp(self, name, shape, dt=F32):
        t = self.nc.dram_tensor(name, list(shape), dt, kind="ExternalOutput").ap()
        self.dout[name] = t
        return t


def build():
    P = Prog(True)
    nc = P.nc
    xf_in = P.inp("xf", [SEQ, D])
    xo_in = P.inp("xo", [HALF, D])
    cx_in = P.inp("cx", [CTX, D])
    cc = P.inp("cc", [128, 8, 2])
    wmod_a = P.inp("wmod", [2, D, 3 * D])
    win_a = P.inp("win", [2, D, IN_COLS])
    wuq_a = P.inp("wuq", [2, 256, 384])
    wukv_a = P.inp("wukv", [2, 128, 512])
    wbr_a = P.inp("wbr", [2, 4, 256, D])
    wout_a = P.inp("wout", [2, D, D])
    lruw_a = P.inp("lruw", [2, 2, 2, 2, 64, 64])
    wlru_a = P.inp("wlru", [2, D, 128])
    vecs_a = P.inp("vecs", [2, 128, NV])
    rows_a = P.inp("rows", [2, 128, D])
    mats_d = P.inp("mats", [7, 128, 128])
    masks_d = P.inp("masks", [128, 640])
    c64f = P.inp("c64f", [128, SEQ])
    s64f = P.inp("s64f", [128, SEQ])
    c64e = P.inp("c64e", [128, 256 + HALF])
    s64e = P.inp("s64e", [128, 256 + HALF])
    cMf = P.inp("cMf", [128, SEQ])
    sMf = P.inp("sMf", [128, SEQ])
    cMo = P.inp("cMo", [128, HALF])
    sMo = P.inp("sMo", [128, HALF])
    y_out = P.outp("xn", [HALF, D])
    hxf = nc.dram_tensor("hxf", [D, TF], BF16, kind="Internal").ap()
    hxo = nc.dram_tensor("hxo", [D, HALF], BF16, kind="Internal").ap()
    x1o = nc.dram_tensor("x1o", [HALF, D], F32, kind="Internal").ap()
    c1 = nc.dram_tensor("c1", [CTX, D], F32, kind="Internal").ap()
    x1f = nc.dram_tensor("x1f", [SEQ, D], F32, kind="Internal").ap()
    HTF = TF // 2
    ydm = [nc.dram_tensor("ydm%d" % i, [128, HTF], BF16, kind="Internal").ap() for i in range(2)]
    ydg = [nc.dram_tensor("ydg%d" % i, [256, HTF], BF16, kind="Internal").ap() for i in range(2)]
    hxf_v = hxf.rearrange("(c p) t -> p c t", p=128)
    hxo_v = hxo.rearrange("(c p) t -> p c t", p=128)


    with ExitStack() as es:
        S = Sched(nc, es)

        uniq = [0]

        def sbuf(st, name, shape, dt):
            uniq[0] += 1
            return st.enter_context(nc.sbuf_tensor("sb%d_%s" % (uniq[0], name), list(shape), dt))

        def dma(q, out, in_, reads, writes, slot):
            return S.op(q, lambda e: e.dma_start(out=out, in_=in_), reads, writes, dma=slot)

        def mm(out, lhsT, rhs, start, stop, reads, writes):
            return S.op('pe', lambda e: e.matmul(out, lhsT=lhsT, rhs=rhs, start=start, stop=stop), reads, writes)

        def act(out, in_, func, reads, writes, **kw):
            return S.op('act', lambda e: e.activation(out=out, in_=in_, func=func, **kw), reads, writes)

        def tt(en, out, in0, in1, op, reads, writes):
            return S.op(en, lambda e: e.tensor_tensor(out=out, in0=in0, in1=in1, op=op), reads, writes)

        def ts(en, out, in0, s1, s2, op0, op1, reads, writes):
            if s2 is None:
                return S.op(en, lambda e: e.tensor_scalar(out=out, in0=in0, scalar1=s1, scalar2=None, op0=op0),
                            reads, writes)
            return S.op(en, lambda e: e.tensor_scalar(out=out, in0=in0, scalar1=s1, scalar2=s2, op0=op0, op1=op1),
                        reads, writes)

        def stt(out, in0, scalar, in1, op0, op1, reads, writes):
            return S.op('dve', lambda e: e.scalar_tensor_tensor(out=out, in0=in0, scalar=scalar, in1=in1,
                                                                op0=op0, op1=op1), reads, writes)

        def cp(en, out, in_, reads, writes):
            return S.op(en, lambda e: e.tensor_copy(out=out, in_=in_), reads, writes)

        def memset(en, ap, val, writes):
            return S.op(en, lambda e: e.memset(ap, val), (), writes)

        PT = [es.enter_context(nc.psum_tensor("pt%d" % i, [128, 2, 512], BF16)) for i in range(2)]
        PS = [es.enter_context(nc.psum_tensor("ps%d" % i, [128, 512], F32)) for i in range(6)]
        rot = [0]

        def nps(pool=(0, 1, 2)):
            i = pool[rot[0] % len(pool)]
            rot[0] += 1
            return PS[i], ('ps', i)

        matsf = sbuf(es, "matsf", [128, 7, 128], F32)
        mats = sbuf(es, "mats", [128, 7, 128], BF16)
        onesf = sbuf(es, "onesf", [128, 128], F32)
        masks = sbuf(es, "masks", [128, 640], BF16)
        masksf = sbuf(es, "masksf", [128, 640], F32)
        dma('sp', matsf[:], mats_d.rearrange("m p n -> p m n"), (), ['matsf'], 'matsf')
        dma('sp', masksf[:], masks_d, (), ['masksf'], 'masksf')
        cp('dve', mats[:], matsf[:], ['matsf'], ['mats'])
        cp('dve', masks[:], masksf[:], ['masksf'], ['masks'])
        memset('pool', onesf[:], 1.0, ['onesf'])

        def M(i, k=128, m=128):
            return mats[0:k, i, 0:m]

        ccy = es.enter_context(nc.semaphore("ccy"))

        def emit_layer(l, update_ctx, xf, xo, cx, xn_out, cn_out, final, after_p0=None):
            NQ = HALF + (CTX if update_ctx else 0)
            wmod_v = wmod_a[l].rearrange("(c p) n -> p c n", p=128)
            win_v = win_a[l].rearrange("(c p) n -> p c n", p=128)
            wuq_d, wukv_d, wbr_d, wout_d, lruw_d = wuq_a[l], wukv_a[l], wbr_a[l], wout_a[l], lruw_a[l]
            wlru_v = wlru_a[l].rearrange("(c p) n -> p c n", p=128)
            vecs_d, rows_d = vecs_a[l], rows_a[l]
            with ExitStack() as esl:
                emit_layer_body(l, update_ctx, xf, xo, cx, xn_out, cn_out, final, NQ, wmod_v, win_v, wuq_d, wukv_d,
                                wbr_d, wout_d, lruw_d, vecs_d, rows_d, esl, after_p0, wlru_v)
            S.barrier()

        def emit_layer_body(l, update_ctx, xf, xo, cx, xn_out, cn_out, final, NQ, wmod_v, win_v, wuq_d, wukv_d,
                            wbr_d, wout_d, lruw_d, vecs_d, rows_d, esl, after_p0, wlru_v):
            vecs = sbuf(esl, "vecs", [128, NV], F32)
            AB = sbuf(esl, "AB", [128, 2, 2, 8], F32)
            G = sbuf(esl, "G", [128, 2, D], F32)
            yT = sbuf(esl, "yT", [128, 8, NQ], BF16)
            esink = sbuf(esl, "esink", [128, 4], F32)
            cs = sbuf(esl, "cs", [128, 4], F32)
            nbr = sbuf(esl, "nbr", [128, 8], F32)
            dma('sp', vecs[:], vecs_d, (), ['vecs'], 'vecs')
            act(esink[:], vecs[:, V_SINK:V_SINK + 4], AF.Exp, ['vecs'], ['esink'])
            act(cs[:], vecs[:, V_LAM:V_LAM + 4], AF.Exp, ['vecs'], ['cs'], scale=-1.0)
            act(cs[:], cs[:], AF.Ln, ['cs'], ['cs'], bias=1.0, scale=1.0)
            ts('dve', cs[:], cs[:], -8.0, None, ALU.mult, None, ['cs'], ['cs'])
            ts('dve', nbr[:], vecs[:, V_BR:V_BR + 8], -1.0, None, ALU.mult, None, ['vecs'], ['nbr'])

            with ExitStack() as st:
                cct = sbuf(st, "cct", [128, 8, 2], F32)
                sct = sbuf(st, "sct", [128, 8, 2], F32)
                scb = sbuf(st, "scb", [128, 2, 8, 128], F32)
                wm = [sbuf(st, "wm%d" % i, [128, 8, 512], F32) for i in range(2)]
                modT = sbuf(st, "modT", [128, 2, 16], F32)
                rowsb = sbuf(st, "rowsb", [128, D], F32)
                dma('sp', cct[:], cc, (), ['cct'], 'cct')
                dma('sp', rowsb[:], rows_d, (), ['rowsb'], 'rowsb')
                act(sct[:], cct[:], AF.Exp, ['cct'], ['sct'], scale=-1.0)
                ts('dve', sct[:], sct[:], 1.0, None, ALU.add, None, ['sct'], ['sct'])
                S.op('dve', lambda e: e.reciprocal(out=sct[:], in_=sct[:]), ['sct'], ['sct'])
                tt('dve', sct[:], sct[:], cct[:], ALU.mult, ['sct', 'cct'], ['sct'])
                for j in range(2):
                    for kc in range(8):
                        cp('pool', scb[:, j, kc, :], sct[:, kc, j:j + 1].to_broadcast([128, 128]), ['sct'], ['scb'])
                pm, pmk = PS[5], ('ps', 5)
                for cb in range(6):
                    b = cb % 2
                    dma('sp', wm[b][:], wmod_v[:, :, cb * 512:(cb + 1) * 512], (), [('wm', b)], ('wm', b))
                    if cb < 4:
                        for fc in range(4):
                            f = cb * 4 + fc
                            for kc in range(8):
                                mm(pm[:, f * 2:f * 2 + 2], wm[b][:, kc, fc * 128:(fc + 1) * 128], sct[:, kc, :],
                                   kc == 0, kc == 7, [('wm', b), 'sct'], [pmk])
                    else:
                        nb = cb - 4
                        for j in range(2 if update_ctx else 1):
                            pg, pgk = nps()
                            for kc in range(8):
                                mm(pg[:], scb[:, j, kc, :], wm[b][:, kc, :], kc == 0, kc == 7, [('wm', b), 'scb'], [pgk])
                            tt('dve', G[:, j, nb * 512:(nb + 1) * 512], pg[:], rowsb[:, nb * 512:(nb + 1) * 512], ALU.add,
                               [pgk, 'rowsb'], ['G'])
                    if cb == 3:
                        pmv = pm[:, 0:32].rearrange("p (f j) -> p j f", j=2)
                        for j in range(2):
                            tt('dve', modT[:, j, :], pmv[:, j, :], vecs[:, V_BSHIFT:V_BSHIFT + 16], ALU.add,
                               [pmk, 'vecs'], ['modT'])
                            stt(AB[:, j, 0, :], modT[:, j, 8:16], 1.0, vecs[:, V_NORMW:V_NORMW + 8], ALU.add, ALU.mult,
                                ['modT', 'vecs'], ['AB'])
                            cp('dve', AB[:, j, 1, :], modT[:, j, 0:8], ['modT'], ['AB'])
            S.barrier()
            if after_p0 is not None:
                after_p0()

            with ExitStack() as st:
                xt = [sbuf(st, "xt%d" % i, [128, 4, D], F32) for i in range(2)]
                sqj = sbuf(st, "sqj", [128, D], BF16)
                ssq = [sbuf(st, "ssq%d" % i, [128, 4], F32) for i in range(2)]
                rs = [sbuf(st, "rs%d" % i, [128, 4], F32) for i in range(2)]
                xnb = [sbuf(st, "xnb%d" % i, [128, 4, D], BF16) for i in range(2)]
                hblk = [sbuf(st, "hblk%d" % i, [128, 8, 512], BF16) for i in range(2)]
                groups = [(1, cx, 0, 256, hxf_v, 0, ('hxf', 0))]
                groups += [(0, xf, i * 512, 512, hxf_v, 256 + i * 512, ('hxf', i + 1)) for i in range(8)]
                groups += [(0, xo, i * 512, 512, hxo_v, i * 512, ('hxo', i)) for i in range(4)]

                def stage_a(gi):
                    mj, src, r0, n, dst, c0, dkey = groups[gi]
                    b = gi % 2
                    ns = n // 128
                    for s_ in range(ns):
                        t0_ = r0 + s_ * 128
                        src_rows = src(t0_) if callable(src) else src[t0_:t0_ + 128, :]
                        dma('sp', xt[b][:, s_, :], src_rows, (), [('xt', b, s_)], ('xt', b))
                    xk = [('xt', b, s_) for s_ in range(ns)]
                    for s_ in range(ns):
                        act(sqj[:], xt[b][:, s_, :], AF.Square, xk, ['sqj', ('ssq', b)], accum_out=ssq[b][:, s_:s_ + 1])
                    act(rs[b][:, 0:ns], ssq[b][:, 0:ns], AF.Ln, [('ssq', b)], [('rs', b)], bias=EPS, scale=1.0 / D)
                    act(rs[b][:, 0:ns], rs[b][:, 0:ns], AF.Exp, [('rs', b)], [('rs', b)], scale=-0.5)
                    for s_ in range(ns):
                        ts('dve', xnb[b][:, s_, :], xt[b][:, s_, :], rs[b][:, s_:s_ + 1], None,
                           ALU.mult, None, xk + [('rs', b)], [('xnb', b, s_)])

                def stage_b(gi):
                    mj, src, r0, n, dst, c0, dkey = groups[gi]
                    b = gi % 2
                    ns = n // 128
                    for cp_ in range(4):
                        pv_, pk_ = PT[cp_ % 2], ('pt', cp_ % 2)
                        for s_ in range(ns):
                            for cc_ in range(2):
                                c = cp_ * 2 + cc_
                                S.op('pe', lambda e, c=c, cc_=cc_, s_=s_, pv_=pv_: e.transpose(
                                    out=pv_[:, cc_, s_ * 128:(s_ + 1) * 128], in_=xnb[b][:, s_, c * 128:(c + 1) * 128],
                                    identity=M(M_ID)), [('xnb', b, s_), 'mats'], [pk_])
                        for cc_ in range(2):
                            c = cp_ * 2 + cc_
                            o = hblk[b][:, c, 0:n]
                            if cp_ % 2 == 0:
                                ts('dve', o, pv_[:, cc_, 0:n], AB[:, mj, 0, c:c + 1], AB[:, mj, 1, c:c + 1], ALU.mult,
                                   ALU.add, [pk_, 'AB'], [('hblk', b, c)])
                            else:
                                act(o, pv_[:, cc_, 0:n], AF.Identity, [pk_, 'AB'], [('hblk', b, c)],
                                    scale=AB[:, mj, 0, c:c + 1], bias=AB[:, mj, 1, c:c + 1])
                    dma('sp', dst[:, :, c0:c0 + n], hblk[b][:, :, 0:n], [('hblk', b, c) for c in range(8)], [dkey],
                        ('hst', b))

                for gi in range(len(groups) + 1):
                    if gi < len(groups):
                        stage_a(gi)
                    if gi >= 1:
                        stage_b(gi - 1)
            S.barrier()

            if os.environ.get("KSTOP") == "p1":
                return
            FB = [(0, 256, ('hxf', 0), True)] + [(256 + i * 512, 512, ('hxf', i + 1), False) for i in range(8)]

            nr_ctr = [0]

            def norm_rope_g(st_tiles, src_ps, src_key, rows, n, ones_i, inv_d, gcol, out_ap, out_keys, rope=None, post=None):
                si = nr_ctr[0] % 4
                nr_ctr[0] += 1
                sq_t, rstd_t, kn_t, t1_t = st_tiles[si]
                ksq, krs, kkn, kt1 = ('nr_sq', si), ('nr_rstd', si), ('nr_kn', si), ('nr_t1', si)
                act(sq_t[0:rows, 0:n], src_ps[0:rows, 0:n], AF.Square, [src_key], [ksq])
                yield
                pq, pqk = nps((3, 4))
                mm(pq[0:rows, 0:n], M(ones_i, rows, rows), sq_t[0:rows, 0:n], True, True, [ksq, 'mats'], [pqk])
                act(rstd_t[0:rows, 0:n], pq[0:rows, 0:n], AF.Ln, [pqk], [krs], bias=EPS, scale=inv_d)
                act(rstd_t[0:rows, 0:n], rstd_t[0:rows, 0:n], AF.Exp, [krs], [krs], scale=-0.5)
                if rope is None:
                    stt(out_ap, src_ps[0:rows, 0:n], vecs[0:rows, gcol:gcol + 1], rstd_t[0:rows, 0:n], ALU.mult, ALU.mult,
                        [src_key, krs, 'vecs'], out_keys)
                    if post is not None:
                        post()
                    return
                swap_i, cos_ap, sin_ap, tab_keys = rope
                stt(kn_t[0:rows, 0:n], src_ps[0:rows, 0:n], vecs[0:rows, gcol:gcol + 1], rstd_t[0:rows, 0:n], ALU.mult,
                    ALU.mult, [src_key, krs, 'vecs'], [kkn])
                yield
                pw, pwk = nps((3, 4))
                mm(pw[0:rows, 0:n], M(swap_i, rows, rows), kn_t[0:rows, 0:n], True, True, [kkn, 'mats'], [pwk])
                tt('pool', t1_t[0:rows, 0:n], kn_t[0:rows, 0:n], cos_ap, ALU.mult, [kkn] + tab_keys, [kt1])
                tt('dve', rstd_t[0:rows, 0:n], pw[0:rows, 0:n], sin_ap, ALU.mult, [pwk] + tab_keys, [krs])
                yield
                tt('dve', out_ap, t1_t[0:rows, 0:n], rstd_t[0:rows, 0:n], ALU.add, [kt1, krs], out_keys)
                if post is not None:
                    post()

            def run_staged(gens):
                gens = list(gens)
                while gens:
                    nxt = []
                    for g_ in gens:
                        try:
                            next(g_)
                            nxt.append(g_)
                        except StopIteration:
                            pass
                    gens = nxt

            def norm_rope(*a, **k):
                run_staged([norm_rope_g(*a, **k)])

            def alloc_nr(st):
                return [(sbuf(st, "nr_sq", [128, 512], BF16), sbuf(st, "nr_rstd", [128, 512], F32),
                         sbuf(st, "nr_kn", [128, 512], BF16), sbuf(st, "nr_t1", [128, 512], F32)) for _ in range(4)]

            ZW = 4358
            with ExitStack() as st:
                wl = sbuf(st, "wl", [128, 8, 128], BF16)
                bdf = sbuf(st, "bdf", [128, 4, 128], F32)
                bd = sbuf(st, "bd", [128, 4, 128], BF16)
                hb_t = [sbuf(st, "lhb%d" % i, [128, 8, 512], BF16) for i in range(2)]
                zl = sbuf(st, "zl", [128, ZW], F32)
                ul = sbuf(st, "ul", [128, TF], F32)
                ub = sbuf(st, "ub", [128, TF], BF16)
                ltmp = [tuple(sbuf(st, "l%s%d" % (nm, i), [128, 512], F32) for nm in "AT") for i in range(2)]
                lh = [sbuf(st, "lH%d" % i, [128, 512], F32) for i in range(3)]
                lctr = [0]
                Rall = sbuf(st, "Rall", [128, TF], F32)
                Iall = sbuf(st, "Iall", [128, TF], F32)
                Yall = sbuf(st, "Yall", [128, TF], F32)
                Yb = sbuf(st, "Yb", [128, TF], BF16)
                dma('pool', wl[:], wlru_v, (), ['wl'], 'wl')
                memset('pool', bdf[:], 0.0, ['bdf'])
                for g in range(2):
                    for d in range(2):
                        for hh in range(2):
                            dma('sp', bdf[hh * 64:(hh + 1) * 64, g * 2 + d, hh * 64:(hh + 1) * 64],
                                lruw_d[g, d, hh], ['bdf'], [('bdfq', g, d, hh)], 'bdf')
                cp('dve', bd[:], bdf[:], ['bdf'] + [('bdfq', g, d, hh) for g in range(2) for d in range(2) for hh in range(2)],
                   ['bd'])
                c = 0
                memset('pool', zl[:], 0.0, ['zl'])
                for bi, (c0, n, hk, isc) in enumerate(FB):
                    b = bi % 2
                    dma('sp', hb_t[b][:, :, 0:n], hxf_v[:, :, c0:c0 + n], [hk], [('lhb', b)], ('lhb', b))
                    pz, pzk = nps()
                    for kc in range(8):
                        mm(pz[:, 0:n], wl[:, kc, :], hb_t[b][:, kc, 0:n], kc == 0, kc == 7, ['wl', ('lhb', b)], [pzk])
                    zc0 = 2 if isc else 261 + (c0 - 256)
                    act(zl[:, zc0:zc0 + n], pz[:, 0:n], AF.Copy, [pzk], ['zl'])
                for (u0, z0, n) in ((0, 2, 256), (256, 261, SEQ)):
                    for j in range(4):
                        wj = vecs[:, V_CONVW + c * 4 + j:V_CONVW + c * 4 + j + 1]
                        zin = zl[:, z0 + j - 2:z0 + j - 2 + n]
                        if j == 0:
                            ts('dve', ul[:, u0:u0 + n], zin, wj, vecs[:, V_CONVB + c:V_CONVB + c + 1], ALU.mult, ALU.add,
                               ['zl', 'vecs'], ['ul'])
                        else:
                            stt(ul[:, u0:u0 + n], zin, wj, ul[:, u0:u0 + n], ALU.mult, ALU.add, ['zl', 'vecs', 'ul'],
                                ['ul'])
                for d in range(2):
                    order = list(range(9)) if d == 0 else [0] + list(range(8, 0, -1))
                    for bi in order:
                        c0, n, hk, isc = FB[bi]
                        if d == 0:
                            cp('pool', ub[:, c0:c0 + n], ul[:, c0:c0 + n], ['ul'], [('ub', bi)])
                        pr, prk = nps((0, 1, 2))
                        mm(pr[:, 0:n], bd[:, 0 * 2 + d, :], ub[:, c0:c0 + n], True, True, ['bd', ('ub', bi)], [prk])
                        pi_, pik = nps((3, 4, 5))
                        mm(pi_[:, 0:n], bd[:, 1 * 2 + d, :], ub[:, c0:c0 + n], True, True, ['bd', ('ub', bi)], [pik])
                        bcr = V_BR + d * 2 + c
                        bci = V_BI + d * 2 + c
                        act(Rall[:, c0:c0 + n], pr[:, 0:n], AF.Sigmoid, [prk, 'vecs'], [('Rall', bi)], scale=1.0,
                            bias=vecs[:, bcr:bcr + 1])
                        act(Iall[:, c0:c0 + n], pi_[:, 0:n], AF.Sigmoid, [pik, 'vecs'], [('Iall', bi)], scale=1.0,
                            bias=vecs[:, bci:bci + 1])
                    prev_h = None
                    for oi, bi in enumerate(order):
                        c0, n, hk, isc = FB[bi]
                        j = lctr[0] % 2
                        j3 = lctr[0] % 3
                        lctr[0] += 1
                        At, Tt = ltmp[j]
                        Rt = Rall[:, c0:c0 + n]
                        It = Iall[:, c0:c0 + n]
                        kR, kI, kA, kT = ('Rall', bi), ('Iall', bi), ('lA', j), ('lT', j)
                        if d == 0:
                            Ht, kH = Yall[:, c0:c0 + n], ('Yall', bi)
                        else:
                            Ht, kH = lh[j3][:, 0:n], ('lH', j3)
                        act(At[:, 0:n], Rt, AF.Exp, [kR, 'cs'], [kA], scale=cs[:, d * 2 + c:d * 2 + c + 1])
                        act(Tt[:, 0:n], At[:, 0:n], AF.Square, [kA], [kT])
                        act(Tt[:, 0:n], Tt[:, 0:n], AF.Ln, [kT], [kT], scale=-1.0, bias=1.0)
                        act(Tt[:, 0:n], Tt[:, 0:n], AF.Exp, [kT], [kT], scale=0.5)
                        tt('pool', It, It, ul[:, c0:c0 + n], ALU.mult, [kI, 'ul'], [kI])
                        tt('dve', It, It, Tt[:, 0:n], ALU.mult, [kI, kT], [kI])
                        if d == 0:
                            o_, da_, db_ = Ht, At[:, 0:n], It
                            init = 0.0 if prev_h is None else prev_h[0][:, prev_h[1] - 1:prev_h[1]]
                        else:
                            o_, da_, db_ = Ht[:, ::-1], At[:, 0:n][:, ::-1], It[:, ::-1]
                            init = 0.0 if prev_h is None else prev_h[0][:, 0:1]
                        rk = [kA, kI] + ([] if prev_h is None else [prev_h[2]])
                        S.op('dve', lambda e, o_=o_, da_=da_, db_=db_, init=init: e.tensor_tensor_scan(
                            out=o_, data0=da_, data1=db_, initial=init, op0=ALU.mult, op1=ALU.add), rk, [kH])
                        prev_h = (Ht, n, kH)
                        if d == 1:
                            S.op('dve', lambda e, c0=c0, n=n, Ht=Ht: e.tensor_tensor(
                                out=Yb[:, c0:c0 + n], in0=Yall[:, c0:c0 + n], in1=Ht, op=ALU.add),
                                [kH, ('Yall', bi)], [('Yb', bi)])
                ybk = [('Yb', bi) for bi in range(9)]
                t0_ = dma('sp', ydm[0], Yb[:, 0:HTF], ybk, [('ydm', l, 0)], 'ydm')
                t1_ = dma('sp', ydm[1], Yb[:, HTF:TF], ybk, [('ydm', l, 1)], 'ydm')
                S.wait_all('pool', [t0_, t1_])
                for i in range(2):
                    nc.gpsimd.collective_compute("AllGather", ALU.bypass, replica_groups=[[0, 1], [2, 3], [4, 5], [6, 7]],
                                                 ins=[ydm[i]], outs=[ydg[i]]).then_inc(ccy, 1)
            S.barrier()

            if os.environ.get("KSTOP") == "pA":
                return
            QB = [(i * 512, 512, hxo_v, i * 512, ('hxo', i), False) for i in range(4)]
            if update_ctx:
                QB.append((HALF, 256, hxf_v, 0, ('hxf', 0), True))

            fin_ctr = [0]
            fin_pend = []

            def finish_a(o_ps, o_key, odd, sink_col, ych, q0, n, scrs):
                si = fin_ctr[0] % 2
                fin_ctr[0] += 1
                osb, rden = scrs[si]
                ko, kr_ = ('osb', si), ('rden', si)
                if not odd:
                    drow, r0, r1 = 64, 0, 64
                    cp('dve', osb[0:65, 0:n], o_ps[0:65, 0:n], [o_key], [ko])
                else:
                    drow, r0, r1 = 0, 64, 128
                    cp('dve', osb[:, 0:n], o_ps[:, 0:n], [o_key], [ko])
                if sink_col is not None:
                    ts('dve', rden[drow:drow + 1, 0:n], osb[drow:drow + 1, 0:n], esink[drow:drow + 1, sink_col:sink_col + 1],
                       None, ALU.add, None, [ko, 'esink'], [kr_])
                    S.op('dve', lambda e: e.reciprocal(out=rden[drow:drow + 1, 0:n], in_=rden[drow:drow + 1, 0:n]),
                         [kr_], [kr_])
                else:
                    S.op('dve', lambda e: e.reciprocal(out=rden[drow:drow + 1, 0:n], in_=osb[drow:drow + 1, 0:n]),
                         [ko], [kr_])
                fin_pend.append((osb, rden, ko, kr_, drow, r0, r1, ych, q0, n))

            def finish_b():
                osb, rden, ko, kr_, drow, r0, r1, ych, q0, n = fin_pend.pop(0)
                pb, pbk = PS[5], ('ps', 5)
                mm(pb[0:r1, 0:n], onesf[drow:drow + 1, 0:r1], rden[drow:drow + 1, 0:n], True, True, [kr_, 'onesf'], [pbk])
                tt('dve', yT[r0:r1, ych, q0:q0 + n], osb[r0:r1, 0:n], pb[r0:r1, 0:n], ALU.mult, [ko, pbk], [('yT', ych, r0)])

            def finish_head(o_ps, o_key, odd, sink_col, ych, q0, n, scrs):
                finish_a(o_ps, o_key, odd, sink_col, ych, q0, n, scrs)
                while len(fin_pend) > 1:
                    finish_b()

            def finish_flush():
                while fin_pend:
                    finish_b()

            def attend(jobs, n, scale, scr_p):
                o_ps, o_key = nps((3, 4))
                pend = []
                first = [True]

                left = [len(jobs)]

                def flush_one():
                    (pt_t, ptk, vl, qlo, qhi, mrows, rd) = pend.pop(0)
                    left[0] -= 1
                    mm(o_ps[0:mrows, qlo:qhi], vl, pt_t[:, qlo:qhi], first[0], left[0] == 0, [ptk] + rd, [o_key])
                    first[0] = False

                for ji, (kl, rq, vl, mask, qlo, qhi, mrows, rd) in enumerate(jobs):
                    sp_t, spk = nps((0, 1, 2))
                    mm(sp_t[:, qlo:qhi], kl, rq, True, True, rd, [spk])
                    pi = ji % len(scr_p)
                    pt_t, ptk = scr_p[pi], ('pT', pi)
                    act(pt_t[:, qlo:qhi], sp_t[:, qlo:qhi], AF.Exp, [spk], [ptk], scale=scale)
                    if mask is not None:
                        tt('pool', pt_t[:, qlo:qhi], pt_t[:, qlo:qhi], mask, ALU.mult, [ptk, 'masks'], [ptk])
                    pend.append((pt_t, ptk, vl, qlo, qhi, mrows, rd))
                    if len(pend) > 2:
                        flush_one()
                while pend:
                    flush_one()
                return o_ps, o_key

            with ExitStack() as st:
                nr = alloc_nr(st)
                KmT = sbuf(st, "KmT", [128, 4, TF], BF16)
                Vm = sbuf(st, "Vm", [128, 34, 386], BF16)
                wkv1 = sbuf(st, "wkv1", [128, 8, 160], BF16)
                wkn = sbuf(st, "wkn", [128, 4, 96], BF16)
                wv = sbuf(st, "wv", [128, 4, 64], BF16)
                wcq = sbuf(st, "wcq", [128, 8, 256], BF16)
                wuq = sbuf(st, "wuq", [128, 2, 384], BF16)
                hb_t = [sbuf(st, "mhb%d" % i, [128, 8, 512], BF16) for i in range(2)]
                ckvn2 = [sbuf(st, "ckvn%d" % i, [128, 512], BF16) for i in range(2)]
                krT2 = [sbuf(st, "krT%d" % i, [32, 512], BF16) for i in range(2)]
                tabs = [sbuf(st, "mtab%d" % i, [128, 2, 512], F32) for i in range(2)]
                cqn = sbuf(st, "cqn", [128, 2, 512], BF16)
                QmT = sbuf(st, "QmT", [128, 4, 512], BF16)
                pTs = [sbuf(st, "mpT%d" % i, [128, 512], BF16) for i in range(4)]
                fscr = [(sbuf(st, "mosb", [128, 512], F32), sbuf(st, "mrden", [128, 512], F32)) for _ in range(2)]
                dma('pool', wkv1[:], win_v[:, :, C_CKV:C_CKV + 160], (), ['wkv1'], 'wkv1')
                memset('pool', wkn[:], 0.0, ['wkn'])
                wukv_h = wukv_d.rearrange("p (h n) -> p h n", h=4)
                dma('pool', wkn[:, :, 0:64], wukv_h[:, :, 0:64], ['wkn'], ['wkn2'], 'wkn')
                dma('pool', wv[:], wukv_h[:, :, 64:128], (), ['wv'], 'wv')
                dma('pool', wcq[:], win_v[:, :, C_CQ:C_CQ + 256], (), ['wcq'], 'wcq')
                dma('pool', wuq[:], wuq_d.rearrange("(c p) n -> p c n", p=128), (), ['wuq'], 'wuq')
                memset('pool', Vm[:], 0.0, ['Vm0'])
                for oc in (64, 65, 257, 258):
                    memset('pool', Vm[:, :, oc:oc + 1], 1.0, ['Vm0'])
                VMV = {0: (0, 65), 1: (65, 193), 2: (193, 258), 3: (258, 386)}
                def b1_front(bi):
                    c0, n, hk, isc = FB[bi]
                    b = bi % 2
                    dma('sp', hb_t[b][:, :, 0:n], hxf_v[:, :, c0:c0 + n], [hk], [('mhb', b)], ('mhb', b))
                    if not isc:
                        dma('sp', tabs[b][0:96, 0, :], cMf[0:96, c0 - 256:c0 - 256 + 512], (), [('mtab', b)], ('mtab', b))
                        dma('sp', tabs[b][0:96, 1, :], sMf[0:96, c0 - 256:c0 - 256 + 512], (), [('mtab', b, 1)], ('mtab', b))
                    pa, pak = nps()
                    for kc in range(8):
                        mm(pa[:, 0:n], wkv1[:, kc, 0:128], hb_t[b][:, kc, 0:n], kc == 0, kc == 7, ['wkv1', ('mhb', b)], [pak])
                    pk, pkk = nps()
                    for kc in range(8):
                        mm(pk[0:32, 0:n], wkv1[:, kc, 128:160], hb_t[b][:, kc, 0:n], kc == 0, kc == 7,
                           ['wkv1', ('mhb', b)], [pkk])
                    norm_rope(nr, pa, pak, 128, n, M_ONES128, 1.0 / 128, V_GCKV, ckvn2[b][:, 0:n], [('ckvn', b)])
                    cp('dve', krT2[b][:, 0:n], pk[0:32, 0:n], [pkk], [('krT', b)])

                def b1_back(bi):
                    c0, n, hk, isc = FB[bi]
                    b = bi % 2
                    ckvn, krT = ckvn2[b], krT2[b]
                    gens = []
                    for h in range(4):
                        pd, pdk = PS[(0, 1, 2, 5)[h]], ('ps', (0, 1, 2, 5)[h])
                        mm(pd[0:96, 0:n], wkn[:, h, :], ckvn[:, 0:n], True, False, ['wkn', 'wkn2', ('ckvn', b)], [pdk])
                        mm(pd[0:96, 0:n], M(M_EKR, 32, 96), krT[:, 0:n], False, True, [('krT', b), 'mats'], [pdk])
                        rope = None if isc else (M_SWAPM, tabs[b][0:96, 0, 0:n], tabs[b][0:96, 1, 0:n],
                                                 [('mtab', b), ('mtab', b, 1)])
                        gens.append(norm_rope_g(nr, pd, pdk, 96, n, M_ONES96, 1.0 / 96, V_GMK, KmT[0:96, h, c0:c0 + n],
                                                [('KmT', bi)], rope))
                    run_staged(gens)
                    for s in range(n // 128):
                        kt = c0 // 128 + s
                        pvv, pvk = nps()
                        mm(pvv[:, 0:256], ckvn[:, s * 128:(s + 1) * 128], wv[:].rearrange("p h n -> p (h n)"), True, True,
                           [('ckvn', b), 'wv'], [pvk])
                        vsrc = pvv[:, 0:256].rearrange("p (a b n) -> p a b n", a=2, b=2)
                        vdst = Vm[:, kt, :].rearrange("p (a c) -> p a c", a=2)
                        cp('dve', vdst[:, :, 0:64], vsrc[:, :, 0, :], [pvk, 'Vm0'], [('Vm', bi)])
                        cp('dve', vdst[:, :, 129:193], vsrc[:, :, 1, :], [pvk, 'Vm0'], [('Vm', bi, 1)])

                for bi in range(len(FB) + 1):
                    if bi < len(FB):
                        b1_front(bi)
                    if bi >= 1:
                        b1_back(bi - 1)
                allK = [('KmT', bi) for bi in range(9)] + [('Vm', bi) for bi in range(9)] + [('Vm', bi, 1) for bi in range(9)] + ['Vm0']
                for qi, (q0, n, hv, h0, hk, isc) in enumerate(QB):
                    b = qi % 2
                    dma('sp', hb_t[b][:, :, 0:n], hv[:, :, h0:h0 + n], [hk], [('mhb', b)], ('mhb', b))
                    if not isc:
                        dma('sp', tabs[b][0:96, 0, :], cMo[0:96, q0:q0 + 512], (), [('mtab', b)], ('mtab', b))
                        dma('sp', tabs[b][0:96, 1, :], sMo[0:96, q0:q0 + 512], (), [('mtab', b, 1)], ('mtab', b))
                    pcs = []
                    for c in range(2):
                        pc_, pck = nps((0, 1))
                        for kc in range(8):
                            mm(pc_[:, 0:n], wcq[:, kc, c * 128:(c + 1) * 128], hb_t[b][:, kc, 0:n], kc == 0, kc == 7,
                               ['wcq', ('mhb', b)], [pck])
                        pcs.append((pc_, pck))
                    sq_t, rstd_t, kn_t, t1_t = nr[0]
                    pq, pqk = PS[2], ('ps', 2)
                    for c in range(2):
                        act(sq_t[:, 0:n], pcs[c][0][:, 0:n], AF.Square, [pcs[c][1]], [('nr_sq', 0)])
                        mm(pq[:, 0:n], M(M_ONES128), sq_t[:, 0:n], c == 0, c == 1, [('nr_sq', 0), 'mats'], [pqk])
                    act(rstd_t[:, 0:n], pq[:, 0:n], AF.Ln, [pqk], [('nr_rstd', 0)], bias=EPS, scale=1.0 / 256)
                    act(rstd_t[:, 0:n], rstd_t[:, 0:n], AF.Exp, [('nr_rstd', 0)], [('nr_rstd', 0)], scale=-0.5)
                    for c in range(2):
                        stt(cqn[:, c, 0:n], pcs[c][0][:, 0:n], vecs[:, V_GCQ + c:V_GCQ + c + 1], rstd_t[:, 0:n], ALU.mult,
                            ALU.mult, [pcs[c][1], ('nr_rstd', 0), 'vecs'], ['cqn'])
                    gens = []
                    for h in range(4):
                        pd, pdk = PS[(0, 1, 2, 5)[h]], ('ps', (0, 1, 2, 5)[h])
                        for c in range(2):
                            mm(pd[0:96, 0:n], wuq[:, c, h * 96:(h + 1) * 96], cqn[:, c, 0:n], c == 0, c == 1, ['wuq', 'cqn'],
                               [pdk])
                        rope = None if isc else (M_SWAPM, tabs[b][0:96, 0, 0:n], tabs[b][0:96, 1, 0:n],
                                                 [('mtab', b), ('mtab', b, 1)])
                        gens.append(norm_rope_g(nr, pd, pdk, 96, n, M_ONES96, 1.0 / 96, V_GMQ, QmT[0:96, h, 0:n],
                                                [('QmT', h)], rope))
                    run_staged(gens)
                    kts = range(2) if isc else range(34)
                    for h in range(4):
                        odd = h % 2 == 1
                        jobs = []
                        for kt in kts:
                            vl = Vm[:, kt, VMV[h][0]:VMV[h][1]]
                            jobs.append((KmT[0:96, h, kt * 128:(kt + 1) * 128], QmT[0:96, h, 0:n], vl, None, 0, n,
                                         128 if odd else 65, allK + [('QmT', h)]))
                        o_ps, o_key = attend(jobs, n, 96.0 ** -0.5, pTs)
                        finish_head(o_ps, o_key, odd, None, h // 2, q0, n, fscr)
                finish_flush()
            S.barrier()

            if os.environ.get("KSTOP") == "pB1":
                return
            with ExitStack() as st:
                nr = alloc_nr(st)
                KaT = sbuf(st, "KaT", [128, TF], BF16)
                Va = sbuf(st, "Va", [128, 34, 2, 129], BF16)
                NSK = CTX + 256 + HALF
                KsT = sbuf(st, "KsT", [128, NSK], BF16)
                Vs = sbuf(st, "Vs", [128, 20, 2, 129], BF16)
                wk_ = sbuf(st, "wk_", [128, 8, 4, 128], BF16)
                wq_ = sbuf(st, "wq_", [128, 8, 2, 256], BF16)
                hb_t = [sbuf(st, "ahb%d" % i, [128, 8, 512], BF16) for i in range(2)]
                tabs = [sbuf(st, "atab%d" % i, [128, 2, 512], F32) for i in range(2)]
                QaT = sbuf(st, "QaT", [128, 2, 512], BF16)
                QsT = sbuf(st, "QsT", [128, 2, 512], BF16)
                Qz = sbuf(st, "Qz", [128, 2, 4, 512], BF16)
                memset('pool', Qz[:], 0.0, ['Qz0'])
                pTs = [sbuf(st, "apT%d" % i, [128, 512], BF16) for i in range(4)]
                fscr = [(sbuf(st, "aosb", [128, 512], F32), sbuf(st, "arden", [128, 512], F32)) for _ in range(2)]
                for i, cb in enumerate((C_AK, C_AV, C_SK, C_SV)):
                    dma('pool', wk_[:, :, i, :], win_v[:, :, cb:cb + 128], (), [('wk_', i)], 'wk_')
                for i, cb in enumerate((C_AQ, C_SQ)):
                    for pos, hq in enumerate((0, 2, 1, 3)):
                        dma('pool', wq_[:, :, i, pos * 64:(pos + 1) * 64], win_v[:, :, cb + hq * 64:cb + (hq + 1) * 64], (),
                            [('wq_', i, pos)], 'wq_')
                wkk = [('wk_', i) for i in range(4)]
                wqk = [('wq_', i, pos) for i in range(2) for pos in range(4)]
                for V_ in (Va, Vs):
                    memset('pool', V_[:], 0.0, ['V0'])
                    memset('pool', V_[:, :, :, 0:1], 1.0, ['V0'])
                    memset('pool', V_[:, :, :, 128:129], 1.0, ['V0'])

                def kv_block(b, n, wi_k, wi_v, KT_ap, kkeys, V_t, kt0, vkeys, rope):
                    pa, pak = nps()
                    for kc in range(8):
                        mm(pa[:, 0:n], wk_[:, kc, wi_k, :], hb_t[b][:, kc, 0:n], kc == 0, kc == 7, wkk + [('ahb', b)], [pak])
                    pvs = []
                    for s in range(n // 128):
                        pvv, pvk = nps((1, 2, 5) if s % 2 == 0 else (0, 2, 5))
                        if pvk == pak:
                            pvv, pvk = nps((1, 2, 5) if s % 2 == 0 else (0, 2, 5))
                        for kc in range(8):
                            mm(pvv[:, 0:128], hb_t[b][:, kc, s * 128:(s + 1) * 128], wk_[:, kc, wi_v, :], kc == 0, kc == 7,
                               wkk + [('ahb', b)], [pvk])
                        cp('dve', V_t[:, kt0 + s, :, 64:128], pvv[:, 0:128].rearrange("p (h n) -> p h n", h=2),
                           [pvk, 'V0'], vkeys)
                    norm_rope(nr, pa, pak, 128, n, M_ONES64, 1.0 / 64, V_GAK if wi_k == 0 else V_GSK, KT_ap, kkeys, rope)

                blk = 0
                for bi, (c0, n, hk, isc) in enumerate(FB):
                    b = blk % 2
                    blk += 1
                    dma('sp', hb_t[b][:, :, 0:n], hxf_v[:, :, c0:c0 + n], [hk], [('ahb', b)], ('ahb', b))
                    rope = None
                    if not isc:
                        dma('sp', tabs[b][:, 0, :], c64f[:, c0 - 256:c0 - 256 + 512], (), [('atab', b)], ('atab', b))
                        dma('sp', tabs[b][:, 1, :], s64f[:, c0 - 256:c0 - 256 + 512], (), [('atab', b, 1)], ('atab', b))
                        rope = (M_SWAP64, tabs[b][:, 0, 0:n], tabs[b][:, 1, 0:n], [('atab', b), ('atab', b, 1)])
                    kv_block(b, n, 0, 1, KaT[:, c0:c0 + n], [('KaT', bi)], Va, c0 // 128, [('Va', bi)], rope)
                    if isc:
                        kv_block(b, n, 2, 3, KsT[:, 0:256], [('KsT', 0)], Vs, 0, [('Vs', 0)], None)
                b = blk % 2
                blk += 1
                dma('sp', hb_t[b][:, :, 0:128], hxf_v[:, :, 256 + 1920:256 + 2048], [('hxf', 4)], [('ahb', b)], ('ahb', b))
                dma('sp', hb_t[b][:, :, 128:256], hxf_v[:, :, 256 + 2048:256 + 2176], [('hxf', 5)], [('ahb', b)], ('ahb', b))
                dma('sp', tabs[b][:, 0, 0:256], c64e[:, 0:256], (), [('atab', b)], ('atab', b))
                dma('sp', tabs[b][:, 1, 0:256], s64e[:, 0:256], (), [('atab', b, 1)], ('atab', b))
                kv_block(b, 256, 2, 3, KsT[:, 256:512], [('KsT', 1)], Vs, 2, [('Vs', 1)],
                         (M_SWAP64, tabs[b][:, 0, 0:256], tabs[b][:, 1, 0:256], [('atab', b), ('atab', b, 1)]))
                for i in range(4):
                    b = blk % 2
                    blk += 1
                    dma('sp', hb_t[b][:], hxo_v[:, :, i * 512:(i + 1) * 512], [('hxo', i)], [('ahb', b)], ('ahb', b))
                    dma('sp', tabs[b][:, 0, :], c64e[:, 256 + i * 512:256 + (i + 1) * 512], (), [('atab', b)], ('atab', b))
                    dma('sp', tabs[b][:, 1, :], s64e[:, 256 + i * 512:256 + (i + 1) * 512], (), [('atab', b, 1)], ('atab', b))
                    kv_block(b, 512, 2, 3, KsT[:, 512 + i * 512:512 + (i + 1) * 512], [('KsT', 2 + i)], Vs, 4 + i * 4,
                             [('Vs', 2 + i)], (M_SWAP64, tabs[b][:, 0, :], tabs[b][:, 1, :], [('atab', b), ('atab', b, 1)]))
                allKa = [('KaT', bi) for bi in range(9)] + [('Va', bi) for bi in range(9)] + ['V0']
                allKs = [('KsT', i) for i in range(6)] + [('Vs', i) for i in range(6)] + ['V0']
                for qi, (q0, n, hv, h0, hk, isc) in enumerate(QB):
                    b = blk % 2
                    blk += 1
                    dma('sp', hb_t[b][:, :, 0:n], hv[:, :, h0:h0 + n], [hk], [('ahb', b)], ('ahb', b))
                    rope = None
                    if not isc:
                        dma('sp', tabs[b][:, 0, :], c64e[:, 256 + q0:256 + q0 + 512], (), [('atab', b)], ('atab', b))
                        dma('sp', tabs[b][:, 1, :], s64e[:, 256 + q0:256 + q0 + 512], (), [('atab', b, 1)], ('atab', b))
                        rope = (M_SWAP64, tabs[b][:, 0, 0:n], tabs[b][:, 1, 0:n], [('atab', b), ('atab', b, 1)])
                    gens = []
                    for wi, (QT, gcol, qn) in enumerate(((QaT, V_GAQ, 'QaT'), (QsT, V_GSQ, 'QsT'))):
                        for tl in range(2):
                            bk = (0, 1, 2, 5)[wi * 2 + tl]
                            pa, pak = PS[bk], ('ps', bk)
                            for kc in range(8):
                                mm(pa[:, 0:n], wq_[:, kc, wi, tl * 128:(tl + 1) * 128], hb_t[b][:, kc, 0:n], kc == 0, kc == 7,
                                   wqk + [('ahb', b)], [pak])

                            def post(QT=QT, qn=qn, wi=wi, tl=tl):
                                cp('pool', Qz[0:64, wi, tl, 0:n], QT[0:64, tl, 0:n], [(qn, tl), 'Qz0'], [('Qz', wi, tl)])
                                cp('pool', Qz[64:128, wi, 2 + tl, 0:n], QT[64:128, tl, 0:n], [(qn, tl), 'Qz0'],
                                   [('Qz', wi, 2 + tl)])
                            gens.append(norm_rope_g(nr, pa, pak, 128, n, M_ONES64, 1.0 / 64, gcol, QT[:, tl, 0:n], [(qn, tl)],
                                                    rope, post))
                    run_staged(gens)
                    for hq in range(4):
                        hkv, g = hq // 2, hq % 2
                        odd = hq % 2 == 1
                        r0 = hkv * 64
                        kts = range(2) if isc else range(34)
                        jobs = []
                        for kt in kts:
                            vl = Va[:, kt, hkv, 0:128] if odd else Va[:, kt, hkv, 64:129]
                            jobs.append((KaT[:, kt * 128:(kt + 1) * 128], Qz[:, 0, hq, 0:n], vl, None, 0, n,
                                         128 if odd else 65, allKa + [('Qz', 0, hq), 'Qz0']))
                        o_ps, o_key = attend(jobs, n, 0.125, pTs)
                        finish_head(o_ps, o_key, odd, None, 4 + hq // 2, q0, n, fscr)
                        jobs = []
                        for kt in range(2):
                            vl = Vs[:, kt, hkv, 0:128] if odd else Vs[:, kt, hkv, 64:129]
                            jobs.append((KsT[:, kt * 128:(kt + 1) * 128], Qz[:, 1, hq, 0:n], vl, None, 0, n,
                                         128 if odd else 65, allKs + [('Qz', 1, hq), 'Qz0']))
                        if not isc:
                            jj0 = q0 // 128
                            for m in range(jj0, jj0 + 6):
                                lo, hi = max(m - 2, jj0), min(m, jj0 + 3)
                                if m == 0:
                                    kt, mask = 2, masks[:, 384:512]
                                elif m == 17:
                                    kt, mask = 3, masks[:, 512:640]
                                else:
                                    kt = 4 + (m - 1)
                                    mask = masks[:, (lo - (m - 2)) * 128:(hi - (m - 2) + 1) * 128]
                                qlo, qhi = (lo - jj0) * 128, (hi - jj0 + 1) * 128
                                vl = Vs[:, kt, hkv, 0:128] if odd else Vs[:, kt, hkv, 64:129]
                                jobs.append((KsT[:, kt * 128:(kt + 1) * 128], Qz[:, 1, hq, qlo:qhi], vl, mask,
                                             qlo, qhi, 128 if odd else 65, allKs + [('Qz', 1, hq), 'Qz0']))
                        o_ps, o_key = attend(jobs, n, 0.125, pTs)
                        finish_head(o_ps, o_key, odd, hq, 2 + hq // 2, q0, n, fscr)
                finish_flush()
            S.barrier()

            if os.environ.get("KSTOP") == "pB2":
                return
            yk = [('yT', c) for c in (6, 7)] + [('yT', c, 'c') for c in (6, 7)] + [('yT', c, r) for c in range(6) for r in (0, 64)]
            for e_ in S.eng.values():
                e_.wait_ge(ccy, 2 * (l + 1))
            with ExitStack() as st:
                YA = sbuf(st, "YA", [128, 2, HALF], BF16)
                YB = sbuf(st, "YB", [128, 2, HALF], BF16)
                for j in range(2):
                    rows_ = slice(j * 128, (j + 1) * 128)
                    dma('sp', YA[:, j, 0:HTF - 256], ydg[0][rows_, 256:HTF], (), [('YA', j)], ('YA', j))
                    dma('sp', YA[:, j, HTF - 256:HALF], ydg[1][rows_, 0:HALF - (HTF - 256)], (), [('YA', j, 1)], ('YA', j))
                    dma('sp', YB[:, j, :], ydg[1][rows_, HTF - HALF:HTF], (), [('YB', j)], ('YB', j))
                    if update_ctx:
                        dma('sp', yT[:, 6 + j, HALF:NQ], ydg[0][rows_, 0:CTX], (), [('yT', 6 + j, 'c')], ('yTc', j))
                    ts('dve', YA[:, j, :], YA[:, j, :], vecs[:, V_SEL:V_SEL + 1], None, ALU.mult, None,
                       [('YA', j), ('YA', j, 1), 'vecs'], [('YA', j), ('YA', j, 1)])
                    stt(yT[:, 6 + j, 0:HALF], YB[:, j, :], vecs[:, V_SEL + 1:V_SEL + 2], YA[:, j, :], ALU.mult, ALU.add,
                        [('YA', j), ('YA', j, 1), ('YB', j), 'vecs'], [('yT', 6 + j)])
            S.barrier()
            with ExitStack() as st0:
              mixT = sbuf(st0, "mixT", [128, 8, NQ], BF16)
              with ExitStack() as st:
                hxq = sbuf(st, "hxq", [128, 8, NQ], BF16)
                wg = [sbuf(st, "wg%d" % i, [128, 8, 128], BF16) for i in range(2)]
                wmg = [sbuf(st, "wmg%d" % i, [128, 8, 4, 128], BF16) for i in range(2)]
                wbr = [sbuf(st, "wbr%d" % i, [128, 2, 4, 128], BF16) for i in range(2)]
                sg = [sbuf(st, "sg%d" % i, [128, 512], BF16) for i in range(2)]
                sig = [sbuf(st, "sig%d" % i, [128, 512], F32) for i in range(2)]
                acc = [sbuf(st, "acc%d" % i, [128, 512], F32) for i in range(2)]
                for i in range(4):
                    dma('sp', hxq[:, :, i * 512:(i + 1) * 512], hxo_v[:, :, i * 512:(i + 1) * 512], [('hxo', i)], ['hxq'], 'hxq')
                if update_ctx:
                    dma('sp', hxq[:, :, HALF:NQ], hxf_v[:, :, 0:CTX], [('hxf', 0)], ['hxq'], 'hxq')
                TB = [(i * 512, 512) for i in range(4)] + ([(HALF, 256)] if update_ctx else [])
                for gc in range(8):
                    b = gc % 2
                    dma('pool', wg[b][:], win_v[:, :, C_GATE + gc * 128:C_GATE + (gc + 1) * 128], (), [('wg', b)], ('wg', b))
                    for ti_, (t0, n) in enumerate(TB):
                        pa, pak = nps()
                        for kc in range(8):
                            mm(pa[:, 0:n], wg[b][:, kc, :], hxq[:, kc, t0:t0 + n], kc == 0, kc == 7, [('wg', b), 'hxq'], [pak])
                        sb_ = ti_ % 2
                        act(sg[sb_][:, 0:n], pa[:, 0:n], AF.Silu, [pak], [('sg', sb_)])
                        tt('dve', yT[:, gc, t0:t0 + n], yT[:, gc, t0:t0 + n], sg[sb_][:, 0:n],
                           ALU.mult, yk + [('sg', sb_)], [('yg', gc, ti_)])
                ygk = [('yg', gc, ti_) for gc in range(8) for ti_ in range(len(TB))]
                it = 0
                for f in range(8):
                    b = f % 2
                    for k in range(4):
                        dma('pool', wmg[b][:, :, k, :], win_v[:, :, C_MERGE + k * 1024 + f * 128:C_MERGE + k * 1024 + (f + 1) * 128],
                            (), [('wmg', b, k)], ('wmg', b))
                        dma('pool', wbr[b][:, :, k, :], wbr_d[k].rearrange("(c p) n -> p c n", p=128)[:, :, f * 128:(f + 1) * 128],
                            (), [('wbr', b, k)], ('wbr', b))
                    wk4 = [('wmg', b, k) for k in range(4)] + [('wbr', b, k) for k in range(4)]
                    for (t0, n) in TB:
                        ab = it % 2
                        it += 1
                        for k in range(4):
                            pz, pzk = nps((0, 1, 2))
                            for kc in range(8):
                                mm(pz[:, 0:n], wmg[b][:, kc, k, :], hxq[:, kc, t0:t0 + n], kc == 0, kc == 7, wk4 + ['hxq'], [pzk])
                            pj, pjk = nps((3, 4, 5))
                            for kc in range(2):
                                mm(pj[:, 0:n], wbr[b][:, kc, k, :], yT[:, 2 * k + kc, t0:t0 + n], kc == 0, kc == 1, wk4 + ygk,
                                   [pjk])
                            sb_ = k % 2
                            act(sig[sb_][:, 0:n], pz[:, 0:n], AF.Sigmoid, [pzk], [('sig', sb_)])
                            if k == 0:
                                tt('dve', acc[ab][:, 0:n], sig[sb_][:, 0:n], pj[:, 0:n], ALU.mult, [('sig', sb_), pjk],
                                   [('acc', ab)])
                            else:
                                tt('dve', sig[sb_][:, 0:n], sig[sb_][:, 0:n], pj[:, 0:n], ALU.mult, [('sig', sb_), pjk],
                                   [('sig', sb_)])
                                if k < 3:
                                    tt('dve', acc[ab][:, 0:n], acc[ab][:, 0:n], sig[sb_][:, 0:n], ALU.add,
                                       [('sig', sb_), ('acc', ab)], [('acc', ab)])
                                else:
                                    tt('dve', mixT[:, f, t0:t0 + n], acc[ab][:, 0:n], sig[sb_][:, 0:n], ALU.add,
                                       [('sig', sb_), ('acc', ab)], [('mixT', f)])
              S.barrier()
              with ExitStack() as st:
                wo = sbuf(st, "wo", [128, 8, D], BF16)
                xt = [sbuf(st, "oxt%d" % i, [128, D], F32) for i in range(2)]
                t1 = [sbuf(st, "ot%d" % i, [128, D], F32) for i in range(2)]
                dma('pool', wo[:], wout_d.rearrange("(c p) n -> p c n", p=128), (), ['wo'], 'wo')
                mk = [('mixT', f) for f in range(8)]
                tiles = [(0, xo, i * 128, xn_out, i * 128, i * 128) for i in range(16)]
                if update_ctx:
                    tiles += [(1, cx, i * 128, cn_out, i * 128, HALF + i * 128) for i in range(2)]
                outs = []
                for ti_, (mj, src, r0, dst, d0, t0) in enumerate(tiles):
                    b = ti_ % 2
                    dma('sp', xt[b][:], src[r0:r0 + 128, :], (), [('oxt', b)], ('oxt', b))
                    for nb in range(2):
                        po, pok = nps((0, 1, 2, 3))
                        for kc in range(8):
                            mm(po[:], mixT[:, kc, t0:t0 + 128], wo[:, kc, nb * 512:(nb + 1) * 512], kc == 0, kc == 7,
                               mk + ['wo'], [pok])
                        tt('dve', t1[b][:, nb * 512:(nb + 1) * 512], po[:], G[:, mj, nb * 512:(nb + 1) * 512], ALU.mult,
                           [pok, 'G'], [('ot', b, nb)])
                        tt('pool', t1[b][:, nb * 512:(nb + 1) * 512], t1[b][:, nb * 512:(nb + 1) * 512],
                           xt[b][:, nb * 512:(nb + 1) * 512], ALU.add, [('ot', b, nb), ('oxt', b)], [('ot', b, nb)])
                    outs.append(dma('sp', dst[d0:d0 + 128, :], t1[b][:], [('ot', b, 0), ('ot', b, 1)], [('out', ti_)], ('ost', b)))
                if final:
                    S.wait_all('sp', outs)
        emit_layer(0, True, xf_in, xo_in, cx_in, x1o, c1, False)
        if os.environ.get("KSTOP"):
            P.nops = S.nops
            return P
        ccs = es.enter_context(nc.semaphore("ccs"))
        CH = 256
        nch = HALF // CH
        x1g = x1f.rearrange("(k q) n -> k q n", k=nch)
        for k in range(nch):
            nc.gpsimd.collective_compute("AllGather", ALU.bypass, replica_groups=[[0, 1], [2, 3], [4, 5], [6, 7]],
                                         ins=[x1o[k * CH:(k + 1) * CH]], outs=[x1g[k]]).then_inc(ccs, 1)

        def wait_cc():
            for e in S.eng.values():
                e.wait_ge(ccs, nch)

        def x1_rows(t0):
            hh, rem = t0 // HALF, t0 % HALF
            r = (rem // CH) * (2 * CH) + hh * CH + rem % CH
            return x1f[r:r + 128, :]

        emit_layer(1, False, x1_rows, x1o, c1, y_out, None, True, after_p0=wait_cc)
        P.nops = S.nops
    return P


def _rope_tables():
    rows = np.repeat(np.arange(SEQ // 64, dtype=np.float32), 64)
    cols = np.tile(np.arange(64, dtype=np.float32), SEQ // 64)

    def ang(rot_dim):
        q = rot_dim // 4
        fr = (np.float32(10000.0) ** (-np.arange(q, dtype=np.float32) / np.float32(q))).astype(np.float32)
        return np.concatenate([rows[:, None] * fr, cols[:, None] * fr], axis=-1).astype(np.float32)

    a64 = ang(64)
    aM = ang(32)
    c64 = np.zeros((128, SEQ), np.float32)
    s64 = np.zeros((128, SEQ), np.float32)
    for p in range(128):
        d = p % 64
        c64[p] = np.cos(a64[:, d % 32])
        s64[p] = np.sin(a64[:, d % 32]) * (-1.0 if d < 32 else 1.0)
    cM = np.zeros((128, SEQ), np.float32)
    sM = np.zeros((128, SEQ), np.float32)
    cM[0:64] = 1.0
    for p in range(64, 96):
        d = p - 64
        cM[p] = np.cos(aM[:, d % 16])
        sM[p] = np.sin(aM[:, d % 16]) * (-1.0 if d < 16 else 1.0)
    return c64, s64, cM, sM


def _const_mats():
    m = np.zeros((7, 128, 128), np.float32)
    m[M_ID] = np.eye(128)
    m[M_ONES64, 0:64, 0:64] = 1
    m[M_ONES64, 64:128, 64:128] = 1
    m[M_ONES96, 0:96, 0:96] = 1
    m[M_ONES128] = 1
    for p in range(128):
        d = p % 64
        m[M_SWAP64, (p - d) + (d + 32) % 64, p] = 1
    for p in range(64, 96):
        d = p - 64
        m[M_SWAPM, 64 + (d + 16) % 32, p] = 1
    for d in range(32):
        m[M_EKR, d, 64 + d] = 1
    return m


def _masks(h):
    k = np.arange(128)[:, None]
    q = np.arange(128)[None, :]
    prev = (k >= q).astype(np.float32)
    nxt = (k <= q).astype(np.float32)
    ones = np.ones((128, 128), np.float32)
    lv = 1.0 if h == 1 else 0.0
    rv = 1.0 if h == 0 else 0.0
    return np.concatenate([nxt, ones, prev, prev * lv, nxt * rv], axis=1)


def _vecs(p, l, h):
    v = np.zeros((128, NV), np.float32)
    v[:, V_NORMW:V_NORMW + 8] = p['norm_w'][l].reshape(8, 128).T
    v[:, V_BSHIFT:V_BSHIFT + 8] = p['b_mod'][l][0:D].reshape(8, 128).T
    v[:, V_BSCALE:V_BSCALE + 8] = p['b_mod'][l][D:2 * D].reshape(8, 128).T
    v[:, V_GCQ:V_GCQ + 2] = p['mla_cq_norm'][l].reshape(2, 128).T
    v[:, V_GCKV] = p['mla_ckv_norm'][l]
    v[0:96, V_GMQ] = p['mla_q_norm'][l]
    v[0:96, V_GMK] = p['mla_k_norm'][l]
    v[:, V_GSQ] = np.tile(p['swa_q_norm'][l], 2)
    v[:, V_GSK] = np.tile(p['swa_k_norm'][l], 2)
    v[:, V_GAQ] = np.tile(p['axa_q_norm'][l], 2)
    v[:, V_GAK] = np.tile(p['axa_k_norm'][l], 2)
    v[:, V_SINK:V_SINK + 4] = p['swa_sink'][l][None, :]
    ch = slice(h * 128, (h + 1) * 128)
    v[:, V_CONVW:V_CONVW + 4] = p['lru_conv_w'][l][:, ch].T
    v[:, V_CONVB] = p['lru_conv_b'][l][ch]
    for d in range(2):
        v[:, V_BR + d * 2] = p['lru_b_r'][l][d, ch]
        v[:, V_BI + d * 2] = p['lru_b_i'][l][d, ch]
        v[:, V_LAM + d * 2] = p['lru_lambda'][l][d, ch]
    v[:, V_SEL] = 1.0 if h == 0 else 0.0
    v[:, V_SEL + 1] = 1.0 if h == 1 else 0.0
    return v


_PROGS = {}
_CONST = {}


def kernel(**inputs):
    p = {k: np.asarray(v, dtype=np.float32) for k, v in inputs.items()}
    if 'prog' not in _PROGS:
        _PROGS['prog'] = build()
    P = _PROGS['prog']
    if 'tabs' not in _CONST:
        _CONST['tabs'] = _rope_tables()
        _CONST['mats'] = _const_mats()
    c64, s64, cM, sM = _CONST['tabs']
    A = np.ascontiguousarray
    x, ctx = p['x'], p['ctx']
    lruw = np.stack([p['lru_w_r'], p['lru_w_i']], axis=1)
    rows = A(np.stack([np.broadcast_to(p['b_mod'][l][2 * D:3 * D][None, :], (128, D)) for l in range(2)], axis=0))
    shared = {"wmod": A(p['w_mod']), "win": A(p['w_in']), "wuq": A(p['mla_w_uq']), "wukv": A(p['mla_w_ukv']),
              "wbr": A(p['w_branch']), "wout": A(p['w_out']), "rows": rows, "mats": _CONST['mats'],
              "c64f": c64, "s64f": s64, "cMf": cM, "sMf": sM}
    in_maps = []
    for core in range(8):
        b, h = core // 2, core % 2
        own = slice(h * HALF, (h + 1) * HALF)
        ext = np.concatenate([np.arange(1920, 2048), np.arange(2048, 2176), np.arange(h * HALF, (h + 1) * HALF)])
        ccv = np.stack([p['c'][b].reshape(8, 128).T, p['c_ctx'].reshape(8, 128).T], axis=-1)
        m = dict(shared)
        m.update({
            "xf": A(x[b]), "xo": A(x[b, own]), "cx": A(ctx[b]), "cc": A(ccv.astype(np.float32)),
            "vecs": A(np.stack([_vecs(p, l, h) for l in range(2)], axis=0)), "masks": _masks(h),
            "lruw": A(lruw[:, :, :, 2 * h:2 * h + 2]),
            "wlru": A(p['w_in'][:, :, C_LRU + h * 128:C_LRU + (h + 1) * 128]),
            "c64e": A(c64[:, ext]), "s64e": A(s64[:, ext]), "cMo": A(cM[:, own]), "sMo": A(sM[:, own]),
        })
        in_maps.append(m)
    res = run_bass_kernel_spmd(P.nc, in_maps, core_ids=list(range(8)))
    out = np.empty_like(x)
    for core in range(8):
        b, h = core // 2, core % 2
        out[b, h * HALF:(h + 1) * HALF] = res.results[core]["xn"]
    return out.astype(np.float32)
```

```python
import os
import numpy as np
from contextlib import ExitStack
import concourse.bass as bass
import concourse.mybir as mybir
from concourse.bass_utils import run_bass_kernel_spmd

F32 = mybir.dt.float32
BF16 = mybir.dt.bfloat16
AF = mybir.ActivationFunctionType
ALU = mybir.AluOpType

D = 1024
SEQ = 4096
HALF = 2048
CTX = 256
TF = CTX + SEQ
EPS = 1e-6
C_CQ, C_CKV, C_KR, C_SQ, C_SK, C_SV, C_AQ, C_AK, C_AV, C_LRU, C_GATE, C_MERGE = (
    0, 256, 384, 416, 672, 800, 928, 1184, 1312, 1440, 1696, 2720)
IN_COLS = 6816
V_NORMW, V_BSHIFT, V_BSCALE, V_GCQ, V_GCKV, V_GMQ, V_GMK, V_GSQ, V_GSK, V_GAQ, V_GAK = (
    0, 8, 16, 24, 26, 27, 28, 29, 30, 31, 32)
V_SINK, V_CONVW, V_CONVB, V_BR, V_BI, V_LAM, V_SEL = 33, 37, 45, 47, 51, 55, 59
NV = 64
M_ID, M_ONES64, M_ONES96, M_ONES128, M_SWAP64, M_SWAPM, M_EKR = range(7)


class Sched:
    ENG = {'pe': 'tensor', 'act': 'scalar', 'dve': 'vector', 'pool': 'gpsimd', 'sp': 'sync'}
    ROLL = 30000

    def __init__(self, nc, es):
        self.nc = nc
        self.es = es
        self.eng = {k: getattr(nc, v) for k, v in self.ENG.items()}
        self.nsem = 0
        self.sem = {}
        self.cnt = {}
        self.allsems = []
        for k in self.eng:
            self._roll(k)
        self.last_w = {}
        self.readers = {}
        self.seen = {k: {} for k in self.eng}
        self.dma_sems = {}
        self.nops = 0

    def _alloc_sem(self):
        self.nsem += 1
        return self.es.enter_context(self.nc.semaphore("s%d" % self.nsem))

    def _roll(self, e):
        self.sem[e] = self._alloc_sem()
        self.cnt[e] = 0

    def op(self, e, fn, reads=(), writes=(), dma=None):
        deps = []
        for k in reads:
            t = self.last_w.get(k)
            if t is not None:
                deps.append((t, 0))
            if isinstance(k, tuple) and k[0] in ('ps', 'pt'):
                for t in self.readers.get(k, ()):
                    deps.append((t, 2))
        for k in writes:
            t = self.last_w.get(k)
            if t is not None:
                deps.append((t, 1))
            for t in self.readers.get(k, ()):
                deps.append((t, 2))
        eng = self.eng[e]
        need = {}
        for (sem, val, pe, is_dma), kind in deps:
            if pe == e and (not is_dma) and dma is None and (kind == 2 or (kind == 1 and e == 'pe')):
                continue
            sid = id(sem)
            if self.seen[e].get(sid, 0) >= val:
                continue
            if sid not in need or need[sid][1] < val:
                need[sid] = (sem, val)
        for sem, val in need.values():
            eng.wait_ge(sem, val)
            self.seen[e][id(sem)] = val
        ins = fn(eng)
        self.nops += 1
        if dma is not None:
            s = self.dma_sems.get(dma)
            if s is None:
                s = self.dma_sems[dma] = [self._alloc_sem(), 0]
            s[1] += 16
            ins.then_inc(s[0], 16)
            tok = (s[0], s[1], e, True)
        else:
            if self.cnt[e] >= self.ROLL:
                self._roll(e)
            self.cnt[e] += 1
            ins.then_inc(self.sem[e], 1)
            tok = (self.sem[e], self.cnt[e], e, False)
        for k in reads:
            self.readers.setdefault(k, []).append(tok)
        for k in writes:
            self.last_w[k] = tok
            self.readers[k] = []
        return tok

    def wait_all(self, e, toks):
        eng = self.eng[e]
        for (sem, val, pe, is_dma) in toks:
            if self.seen[e].get(id(sem), 0) >= val:
                continue
            eng.wait_ge(sem, val)
            self.seen[e][id(sem)] = val

    def barrier(self):
        toks = [(self.sem[p], self.cnt[p], p, False) for p in self.eng if self.cnt[p] > 0]
        toks += [(s[0], s[1], None, True) for s in self.dma_sems.values()]
        for e in self.eng:
            self.wait_all(e, toks)


class Prog:
    def __init__(self, update_ctx):
        self.update_ctx = update_ctx
        self.nc = bass.Bass("TRN2", target_bir_lowering=False, num_devices=8)
        self.din = {}
        self.dout = {}

    def inp(self, name, shape, dt=F32):
        t = self.nc.dram_tensor(name, list(shape), dt, kind="ExternalInput").ap()
        self.din[name] = t
        return t

    def outp(self, name, shape, dt=F32):
        t = self.nc.dram_tensor(name, list(shape), dt, kind="ExternalOutput").ap()
        self.dout[name] = t
        return t


def build():
    P = Prog(True)
    nc = P.nc
    xf_in = P.inp("xf", [SEQ, D])
    xo_in = P.inp("xo", [HALF, D])
    cx_in = P.inp("cx", [CTX, D])
    cc = P.inp("cc", [128, 8, 2])
    wmod_a = P.inp("wmod", [2, D, 3 * D])
    win_a = P.inp("win", [2, D, IN_COLS])
    wuq_a = P.inp("wuq", [2, 256, 384])
    wukv_a = P.inp("wukv", [2, 128, 512])
    wbr_a = P.inp("wbr", [2, 4, 256, D])
    wout_a = P.inp("wout", [2, D, D])
    lruw_a = P.inp("lruw", [2, 2, 2, 2, 64, 64])
    wlru_a = P.inp("wlru", [2, D, 128])
    vecs_a = P.inp("vecs", [2, 128, NV])
    rows_a = P.inp("rows", [2, 128, D])
    mats_d = P.inp("mats", [7, 128, 128])
    masks_d = P.inp("masks", [128, 640])
    c64f = P.inp("c64f", [128, SEQ])
    s64f = P.inp("s64f", [128, SEQ])
    c64e = P.inp("c64e", [128, 256 + HALF])
    s64e = P.inp("s64e", [128, 256 + HALF])
    cMf = P.inp("cMf", [128, SEQ])
    sMf = P.inp("sMf", [128, SEQ])
    cMo = P.inp("cMo", [128, HALF])
    sMo = P.inp("sMo", [128, HALF])
    y_out = P.outp("xn", [HALF, D])
    hxf = nc.dram_tensor("hxf", [D, TF], BF16, kind="Internal").ap()
    hxo = nc.dram_tensor("hxo", [D, HALF], BF16, kind="Internal").ap()
    x1o = nc.dram_tensor("x1o", [HALF, D], F32, kind="Internal").ap()
    c1 = nc.dram_tensor("c1", [CTX, D], F32, kind="Internal").ap()
    x1f = nc.dram_tensor("x1f", [SEQ, D], F32, kind="Internal").ap()
    HTF = TF // 2
    ydm = [nc.dram_tensor("ydm%d" % i, [128, HTF], BF16, kind="Internal").ap() for i in range(2)]
    ydg = [nc.dram_tensor("ydg%d" % i, [256, HTF], BF16, kind="Internal").ap() for i in range(2)]
    hxf_v = hxf.rearrange("(c p) t -> p c t", p=128)
    hxo_v = hxo.rearrange("(c p) t -> p c t", p=128)


    with ExitStack() as es:
        S = Sched(nc, es)

        uniq = [0]

        def sbuf(st, name, shape, dt):
            uniq[0] += 1
            return st.enter_context(nc.sbuf_tensor("sb%d_%s" % (uniq[0], name), list(shape), dt))

        def dma(q, out, in_, reads, writes, slot):
            return S.op(q, lambda e: e.dma_start(out=out, in_=in_), reads, writes, dma=slot)

        def mm(out, lhsT, rhs, start, stop, reads, writes):
            return S.op('pe', lambda e: e.matmul(out, lhsT=lhsT, rhs=rhs, start=start, stop=stop), reads, writes)

        def act(out, in_, func, reads, writes, **kw):
            return S.op('act', lambda e: e.activation(out=out, in_=in_, func=func, **kw), reads, writes)

        def tt(en, out, in0, in1, op, reads, writes):
            return S.op(en, lambda e: e.tensor_tensor(out=out, in0=in0, in1=in1, op=op), reads, writes)

        def ts(en, out, in0, s1, s2, op0, op1, reads, writes):
            if s2 is None:
                return S.op(en, lambda e: e.tensor_scalar(out=out, in0=in0, scalar1=s1, scalar2=None, op0=op0),
                            reads, writes)
            return S.op(en, lambda e: e.tensor_scalar(out=out, in0=in0, scalar1=s1, scalar2=s2, op0=op0, op1=op1),
                        reads, writes)

        def stt(out, in0, scalar, in1, op0, op1, reads, writes):
            return S.op('dve', lambda e: e.scalar_tensor_tensor(out=out, in0=in0, scalar=scalar, in1=in1,
                                                                op0=op0, op1=op1), reads, writes)

        def cp(en, out, in_, reads, writes):
            return S.op(en, lambda e: e.tensor_copy(out=out, in_=in_), reads, writes)

        def memset(en, ap, val, writes):
            return S.op(en, lambda e: e.memset(ap, val), (), writes)

        PT = [es.enter_context(nc.psum_tensor("pt%d" % i, [128, 2, 512], BF16)) for i in range(2)]
        PS = [es.enter_context(nc.psum_tensor("ps%d" % i, [128, 512], F32)) for i in range(6)]
        rot = [0]

        def nps(pool=(0, 1, 2)):
            i = pool[rot[0] % len(pool)]
            rot[0] += 1
            return PS[i], ('ps', i)

        matsf = sbuf(es, "matsf", [128, 7, 128], F32)
        mats = sbuf(es, "mats", [128, 7, 128], BF16)
        onesf = sbuf(es, "onesf", [128, 128], F32)
        masks = sbuf(es, "masks", [128, 640], BF16)
        masksf = sbuf(es, "masksf", [128, 640], F32)
        dma('sp', matsf[:], mats_d.rearrange("m p n -> p m n"), (), ['matsf'], 'matsf')
        dma('sp', masksf[:], masks_d, (), ['masksf'], 'masksf')
        cp('dve', mats[:], matsf[:], ['matsf'], ['mats'])
        cp('dve', masks[:], masksf[:], ['masksf'], ['masks'])
        memset('pool', onesf[:], 1.0, ['onesf'])

        def M(i, k=128, m=128):
            return mats[0:k, i, 0:m]

        ccy = es.enter_context(nc.semaphore("ccy"))

        def emit_layer(l, update_ctx, xf, xo, cx, xn_out, cn_out, final, after_p0=None):
            NQ = HALF + (CTX if update_ctx else 0)
            wmod_v = wmod_a[l].rearrange("(c p) n -> p c n", p=128)
            win_v = win_a[l].rearrange("(c p) n -> p c n", p=128)
            wuq_d, wukv_d, wbr_d, wout_d, lruw_d = wuq_a[l], wukv_a[l], wbr_a[l], wout_a[l], lruw_a[l]
            wlru_v = wlru_a[l].rearrange("(c p) n -> p c n", p=128)
            vecs_d, rows_d = vecs_a[l], rows_a[l]
            with ExitStack() as esl:
                emit_layer_body(l, update_ctx, xf, xo, cx, xn_out, cn_out, final, NQ, wmod_v, win_v, wuq_d, wukv_d,
                                wbr_d, wout_d, lruw_d, vecs_d, rows_d, esl, after_p0, wlru_v)
            S.barrier()

        def emit_layer_body(l, update_ctx, xf, xo, cx, xn_out, cn_out, final, NQ, wmod_v, win_v, wuq_d, wukv_d,
                            wbr_d, wout_d, lruw_d, vecs_d, rows_d, esl, after_p0, wlru_v):
            vecs = sbuf(esl, "vecs", [128, NV], F32)
            AB = sbuf(esl, "AB", [128, 2, 2, 8], F32)
            G = sbuf(esl, "G", [128, 2, D], F32)
            yT = sbuf(esl, "yT", [128, 8, NQ], BF16)
            esink = sbuf(esl, "esink", [128, 4], F32)
            cs = sbuf(esl, "cs", [128, 4], F32)
            nbr = sbuf(esl, "nbr", [128, 8], F32)
            dma('sp', vecs[:], vecs_d, (), ['vecs'], 'vecs')
            act(esink[:], vecs[:, V_SINK:V_SINK + 4], AF.Exp, ['vecs'], ['esink'])
            act(cs[:], vecs[:, V_LAM:V_LAM + 4], AF.Exp, ['vecs'], ['cs'], scale=-1.0)
            act(cs[:], cs[:], AF.Ln, ['cs'], ['cs'], bias=1.0, scale=1.0)
            ts('dve', cs[:], cs[:], -8.0, None, ALU.mult, None, ['cs'], ['cs'])
            ts('dve', nbr[:], vecs[:, V_BR:V_BR + 8], -1.0, None, ALU.mult, None, ['vecs'], ['nbr'])

            with ExitStack() as st:
                cct = sbuf(st, "cct", [128, 8, 2], F32)
                sct = sbuf(st, "sct", [128, 8, 2], F32)
                scb = sbuf(st, "scb", [128, 2, 8, 128], F32)
                wm = [sbuf(st, "wm%d" % i, [128, 8, 512], F32) for i in range(2)]
                modT = sbuf(st, "modT", [128, 2, 16], F32)
                rowsb = sbuf(st, "rowsb", [128, D], F32)
                dma('sp', cct[:], cc, (), ['cct'], 'cct')
                dma('sp', rowsb[:], rows_d, (), ['rowsb'], 'rowsb')
                act(sct[:], cct[:], AF.Exp, ['cct'], ['sct'], scale=-1.0)
                ts('dve', sct[:], sct[:], 1.0, None, ALU.add, None, ['sct'], ['sct'])
                S.op('dve', lambda e: e.reciprocal(out=sct[:], in_=sct[:]), ['sct'], ['sct'])
                tt('dve', sct[:], sct[:], cct[:], ALU.mult, ['sct', 'cct'], ['sct'])
                for j in range(2):
                    for kc in range(8):
                        cp('pool', scb[:, j, kc, :], sct[:, kc, j:j + 1].to_broadcast([128, 128]), ['sct'], ['scb'])
                pm, pmk = PS[5], ('ps', 5)
                for cb in range(6):
                    b = cb % 2
                    dma('sp', wm[b][:], wmod_v[:, :, cb * 512:(cb + 1) * 512], (), [('wm', b)], ('wm', b))
                    if cb < 4:
                        for fc in range(4):
                            f = cb * 4 + fc
                            for kc in range(8):
                                mm(pm[:, f * 2:f * 2 + 2], wm[b][:, kc, fc * 128:(fc + 1) * 128], sct[:, kc, :],
                                   kc == 0, kc == 7, [('wm', b), 'sct'], [pmk])
                    else:
                        nb = cb - 4
                        for j in range(2 if update_ctx else 1):
                            pg, pgk = nps()
                            for kc in range(8):
                                mm(pg[:], scb[:, j, kc, :], wm[b][:, kc, :], kc == 0, kc == 7, [('wm', b), 'scb'], [pgk])
                            tt('dve', G[:, j, nb * 512:(nb + 1) * 512], pg[:], rowsb[:, nb * 512:(nb + 1) * 512], ALU.add,
                               [pgk, 'rowsb'], ['G'])
                    if cb == 3:
                        pmv = pm[:, 0:32].rearrange("p (f j) -> p j f", j=2)
                        for j in range(2):
                            tt('dve', modT[:, j, :], pmv[:, j, :], vecs[:, V_BSHIFT:V_BSHIFT + 16], ALU.add,
                               [pmk, 'vecs'], ['modT'])
                            stt(AB[:, j, 0, :], modT[:, j, 8:16], 1.0, vecs[:, V_NORMW:V_NORMW + 8], ALU.add, ALU.mult,
                                ['modT', 'vecs'], ['AB'])
                            cp('dve', AB[:, j, 1, :], modT[:, j, 0:8], ['modT'], ['AB'])
            S.barrier()
            if after_p0 is not None:
                after_p0()

            with ExitStack() as st:
                xt = [sbuf(st, "xt%d" % i, [128, 4, D], F32) for i in range(2)]
                sqj = sbuf(st, "sqj", [128, D], BF16)
                ssq = [sbuf(st, "ssq%d" % i, [128, 4], F32) for i in range(2)]
                rs = [sbuf(st, "rs%d" % i, [128, 4], F32) for i in range(2)]
                xnb = [sbuf(st, "xnb%d" % i, [128, 4, D], BF16) for i in range(2)]
                hblk = [sbuf(st, "hblk%d" % i, [128, 8, 512], BF16) for i in range(2)]
                groups = [(1, cx, 0, 256, hxf_v, 0, ('hxf', 0))]
                groups += [(0, xf, i * 512, 512, hxf_v, 256 + i * 512, ('hxf', i + 1)) for i in range(8)]
                groups += [(0, xo, i * 512, 512, hxo_v, i * 512, ('hxo', i)) for i in range(4)]

                def stage_a(gi):
                    mj, src, r0, n, dst, c0, dkey = groups[gi]
                    b = gi % 2
                    ns = n // 128
                    for s_ in range(ns):
                        t0_ = r0 + s_ * 128
                        src_rows = src(t0_) if callable(src) else src[t0_:t0_ + 128, :]
                        dma('sp', xt[b][:, s_, :], src_rows, (), [('xt', b, s_)], ('xt', b))
                    xk = [('xt', b, s_) for s_ in range(ns)]
                    for s_ in range(ns):
                        act(sqj[:], xt[b][:, s_, :], AF.Square, xk, ['sqj', ('ssq', b)], accum_out=ssq[b][:, s_:s_ + 1])
                    act(rs[b][:, 0:ns], ssq[b][:, 0:ns], AF.Ln, [('ssq', b)], [('rs', b)], bias=EPS, scale=1.0 / D)
                    act(rs[b][:, 0:ns], rs[b][:, 0:ns], AF.Exp, [('rs', b)], [('rs', b)], scale=-0.5)
                    for s_ in range(ns):
                        ts('dve', xnb[b][:, s_, :], xt[b][:, s_, :], rs[b][:, s_:s_ + 1], None,
                           ALU.mult, None, xk + [('rs', b)], [('xnb', b, s_)])

                def stage_b(gi):
                    mj, src, r0, n, dst, c0, dkey = groups[gi]
                    b = gi % 2
                    ns = n // 128
                    for cp_ in range(4):
                        pv_, pk_ = PT[cp_ % 2], ('pt', cp_ % 2)
                        for s_ in range(ns):
                            for cc_ in range(2):
                                c = cp_ * 2 + cc_
                                S.op('pe', lambda e, c=c, cc_=cc_, s_=s_, pv_=pv_: e.transpose(
                                    out=pv_[:, cc_, s_ * 128:(s_ + 1) * 128], in_=xnb[b][:, s_, c * 128:(c + 1) * 128],
                                    identity=M(M_ID)), [('xnb', b, s_), 'mats'], [pk_])
                        for cc_ in range(2):
                            c = cp_ * 2 + cc_
                            o = hblk[b][:, c, 0:n]
                            if cp_ % 2 == 0:
                                ts('dve', o, pv_[:, cc_, 0:n], AB[:, mj, 0, c:c + 1], AB[:, mj, 1, c:c + 1], ALU.mult,
                                   ALU.add, [pk_, 'AB'], [('hblk', b, c)])
                            else:
                                act(o, pv_[:, cc_, 0:n], AF.Identity, [pk_, 'AB'], [('hblk', b, c)],
                                    scale=AB[:, mj, 0, c:c + 1], bias=AB[:, mj, 1, c:c + 1])
                    dma('pool', dst[:, :, c0:c0 + n], hblk[b][:, :, 0:n], [('hblk', b, c) for c in range(8)], [dkey],
                        ('hst', b))

                for gi in range(len(groups) + 1):
                    if gi < len(groups):
                        stage_a(gi)
                    if gi >= 1:
                        stage_b(gi - 1)
            S.barrier()

            if os.environ.get("KSTOP") == "p1":
                return
            FB = [(0, 256, ('hxf', 0), True)] + [(256 + i * 512, 512, ('hxf', i + 1), False) for i in range(8)]

            nr_ctr = [0]

            def norm_rope_g(st_tiles, src_ps, src_key, rows, n, ones_i, inv_d, gcol, out_ap, out_keys, rope=None, post=None):
                si = nr_ctr[0] % 4
                nr_ctr[0] += 1
                sq_t, rstd_t, kn_t, t1_t = st_tiles[si]
                ksq, krs, kkn, kt1 = ('nr_sq', si), ('nr_rstd', si), ('nr_kn', si), ('nr_t1', si)
                act(sq_t[0:rows, 0:n], src_ps[0:rows, 0:n], AF.Square, [src_key], [ksq])
                yield
                pq, pqk = nps((3, 4))
                mm(pq[0:rows, 0:n], M(ones_i, rows, rows), sq_t[0:rows, 0:n], True, True, [ksq, 'mats'], [pqk])
                act(rstd_t[0:rows, 0:n], pq[0:rows, 0:n], AF.Ln, [pqk], [krs], bias=EPS, scale=inv_d)
                act(rstd_t[0:rows, 0:n], rstd_t[0:rows, 0:n], AF.Exp, [krs], [krs], scale=-0.5)
                if rope is None:
                    stt(out_ap, src_ps[0:rows, 0:n], vecs[0:rows, gcol:gcol + 1], rstd_t[0:rows, 0:n], ALU.mult, ALU.mult,
                        [src_key, krs, 'vecs'], out_keys)
                    if post is not None:
                        post()
                    return
                swap_i, cos_ap, sin_ap, tab_keys = rope
                stt(kn_t[0:rows, 0:n], src_ps[0:rows, 0:n], vecs[0:rows, gcol:gcol + 1], rstd_t[0:rows, 0:n], ALU.mult,
                    ALU.mult, [src_key, krs, 'vecs'], [kkn])
                yield
                pw, pwk = nps((3, 4))
                mm(pw[0:rows, 0:n], M(swap_i, rows, rows), kn_t[0:rows, 0:n], True, True, [kkn, 'mats'], [pwk])
                tt('pool', t1_t[0:rows, 0:n], kn_t[0:rows, 0:n], cos_ap, ALU.mult, [kkn] + tab_keys, [kt1])
                tt('dve', rstd_t[0:rows, 0:n], pw[0:rows, 0:n], sin_ap, ALU.mult, [pwk] + tab_keys, [krs])
                yield
                tt('dve', out_ap, t1_t[0:rows, 0:n], rstd_t[0:rows, 0:n], ALU.add, [kt1, krs], out_keys)
                if post is not None:
                    post()

            def run_staged(gens):
                gens = list(gens)
                while gens:
                    nxt = []
                    for g_ in gens:
                        try:
                            next(g_)
                            nxt.append(g_)
                        except StopIteration:
                            pass
                    gens = nxt

            def norm_rope(*a, **k):
                run_staged([norm_rope_g(*a, **k)])

            def alloc_nr(st):
                return [(sbuf(st, "nr_sq", [128, 512], BF16), sbuf(st, "nr_rstd", [128, 512], F32),
                         sbuf(st, "nr_kn", [128, 512], BF16), sbuf(st, "nr_t1", [128, 512], F32)) for _ in range(4)]

            ZW = 4358
            with ExitStack() as st:
                wl = sbuf(st, "wl", [128, 8, 128], BF16)
                bdf = sbuf(st, "bdf", [128, 4, 128], F32)
                bd = sbuf(st, "bd", [128, 4, 128], BF16)
                hb_t = [sbuf(st, "lhb%d" % i, [128, 8, 512], BF16) for i in range(2)]
                zl = sbuf(st, "zl", [128, ZW], F32)
                ul = sbuf(st, "ul", [128, TF], F32)
                ub = sbuf(st, "ub", [128, TF], BF16)
                ltmp = [tuple(sbuf(st, "l%s%d" % (nm, i), [128, 512], F32) for nm in "AT") for i in range(2)]
                lh = [sbuf(st, "lH%d" % i, [128, 512], F32) for i in range(3)]
                lctr = [0]
                Rall = sbuf(st, "Rall", [128, TF], F32)
                Iall = sbuf(st, "Iall", [128, TF], F32)
                Yall = sbuf(st, "Yall", [128, TF], F32)
                Yb = sbuf(st, "Yb", [128, TF], BF16)
                dma('pool', wl[:], wlru_v, (), ['wl'], 'wl')
                memset('pool', bdf[:], 0.0, ['bdf'])
                for g in range(2):
                    for d in range(2):
                        for hh in range(2):
                            dma('sp', bdf[hh * 64:(hh + 1) * 64, g * 2 + d, hh * 64:(hh + 1) * 64],
                                lruw_d[g, d, hh], ['bdf'], [('bdfq', g, d, hh)], 'bdf')
                cp('dve', bd[:], bdf[:], ['bdf'] + [('bdfq', g, d, hh) for g in range(2) for d in range(2) for hh in range(2)],
                   ['bd'])
                c = 0
                memset('pool', zl[:], 0.0, ['zl'])
                for bi, (c0, n, hk, isc) in enumerate(FB):
                    b = bi % 2
                    dma('sp', hb_t[b][:, :, 0:n], hxf_v[:, :, c0:c0 + n], [hk], [('lhb', b)], ('lhb', b))
                    pz, pzk = nps()
                    for kc in range(8):
                        mm(pz[:, 0:n], wl[:, kc, :], hb_t[b][:, kc, 0:n], kc == 0, kc == 7, ['wl', ('lhb', b)], [pzk])
                    zc0 = 2 if isc else 261 + (c0 - 256)
                    act(zl[:, zc0:zc0 + n], pz[:, 0:n], AF.Copy, [pzk], ['zl'])
                for (u0, z0, n) in ((0, 2, 256), (256, 261, SEQ)):
                    for j in range(4):
                        wj = vecs[:, V_CONVW + c * 4 + j:V_CONVW + c * 4 + j + 1]
                        zin = zl[:, z0 + j - 2:z0 + j - 2 + n]
                        if j == 0:
                            ts('dve', ul[:, u0:u0 + n], zin, wj, vecs[:, V_CONVB + c:V_CONVB + c + 1], ALU.mult, ALU.add,
                               ['zl', 'vecs'], ['ul'])
                        else:
                            stt(ul[:, u0:u0 + n], zin, wj, ul[:, u0:u0 + n], ALU.mult, ALU.add, ['zl', 'vecs', 'ul'],
                                ['ul'])
                for d in range(2):
                    order = list(range(9)) if d == 0 else [0] + list(range(8, 0, -1))
                    for bi in order:
                        c0, n, hk, isc = FB[bi]
                        if d == 0:
                            cp('pool', ub[:, c0:c0 + n], ul[:, c0:c0 + n], ['ul'], [('ub', bi)])
                        pr, prk = nps((0, 1, 2))
                        mm(pr[:, 0:n], bd[:, 0 * 2 + d, :], ub[:, c0:c0 + n], True, True, ['bd', ('ub', bi)], [prk])
                        pi_, pik = nps((3, 4, 5))
                        mm(pi_[:, 0:n], bd[:, 1 * 2 + d, :], ub[:, c0:c0 + n], True, True, ['bd', ('ub', bi)], [pik])
                        bcr = V_BR + d * 2 + c
                        bci = V_BI + d * 2 + c
                        act(Rall[:, c0:c0 + n], pr[:, 0:n], AF.Sigmoid, [prk, 'vecs'], [('Rall', bi)], scale=1.0,
                            bias=vecs[:, bcr:bcr + 1])
                        act(Iall[:, c0:c0 + n], pi_[:, 0:n], AF.Sigmoid, [pik, 'vecs'], [('Iall', bi)], scale=1.0,
                            bias=vecs[:, bci:bci + 1])
                    prev_h = None
                    for oi, bi in enumerate(order):
                        c0, n, hk, isc = FB[bi]
                        j = lctr[0] % 2
                        j3 = lctr[0] % 3
                        lctr[0] += 1
                        At, Tt = ltmp[j]
                        Rt = Rall[:, c0:c0 + n]
                        It = Iall[:, c0:c0 + n]
                        kR, kI, kA, kT = ('Rall', bi), ('Iall', bi), ('lA', j), ('lT', j)
                        if d == 0:
                            Ht, kH = Yall[:, c0:c0 + n], ('Yall', bi)
                        else:
                            Ht, kH = lh[j3][:, 0:n], ('lH', j3)
                        act(At[:, 0:n], Rt, AF.Exp, [kR, 'cs'], [kA], scale=cs[:, d * 2 + c:d * 2 + c + 1])
                        act(Tt[:, 0:n], At[:, 0:n], AF.Square, [kA], [kT])
                        act(Tt[:, 0:n], Tt[:, 0:n], AF.Ln, [kT], [kT], scale=-1.0, bias=1.0)
                        act(Tt[:, 0:n], Tt[:, 0:n], AF.Exp, [kT], [kT], scale=0.5)
                        tt('pool', It, It, ul[:, c0:c0 + n], ALU.mult, [kI, 'ul'], [kI])
                        tt('dve', It, It, Tt[:, 0:n], ALU.mult, [kI, kT], [kI])
                        if d == 0:
                            o_, da_, db_ = Ht, At[:, 0:n], It
                            init = 0.0 if prev_h is None else prev_h[0][:, prev_h[1] - 1:prev_h[1]]
                        else:
                            o_, da_, db_ = Ht[:, ::-1], At[:, 0:n][:, ::-1], It[:, ::-1]
                            init = 0.0 if prev_h is None else prev_h[0][:, 0:1]
                        rk = [kA, kI] + ([] if prev_h is None else [prev_h[2]])
                        S.op('dve', lambda e, o_=o_, da_=da_, db_=db_, init=init: e.tensor_tensor_scan(
                            out=o_, data0=da_, data1=db_, initial=init, op0=ALU.mult, op1=ALU.add), rk, [kH])
                        prev_h = (Ht, n, kH)
                        if d == 1:
                            S.op('dve', lambda e, c0=c0, n=n, Ht=Ht: e.tensor_tensor(
                                out=Yb[:, c0:c0 + n], in0=Yall[:, c0:c0 + n], in1=Ht, op=ALU.add),
                                [kH, ('Yall', bi)], [('Yb', bi)])
                ybk = [('Yb', bi) for bi in range(9)]
                t0_ = dma('sp', ydm[0], Yb[:, 0:HTF], ybk, [('ydm', l, 0)], 'ydm')
                t1_ = dma('sp', ydm[1], Yb[:, HTF:TF], ybk, [('ydm', l, 1)], 'ydm')
                S.wait_all('pool', [t0_, t1_])
                for i in range(2):
                    nc.gpsimd.collective_compute("AllGather", ALU.bypass, replica_groups=[[0, 1], [2, 3], [4, 5], [6, 7]],
                                                 ins=[ydm[i]], outs=[ydg[i]]).then_inc(ccy, 1)
            S.barrier()

            if os.environ.get("KSTOP") == "pA":
                return
            QB = [(i * 512, 512, hxo_v, i * 512, ('hxo', i), False) for i in range(4)]
            if update_ctx:
                QB.append((HALF, 256, hxf_v, 0, ('hxf', 0), True))

            fin_ctr = [0]
            fin_pend = []

            def finish_a(o_ps, o_key, odd, sink_col, ych, q0, n, scrs):
                si = fin_ctr[0] % 2
                fin_ctr[0] += 1
                osb, rden = scrs[si]
                ko, kr_ = ('osb', si), ('rden', si)
                if not odd:
                    drow, r0, r1 = 64, 0, 64
                    cp('dve', osb[0:65, 0:n], o_ps[0:65, 0:n], [o_key], [ko])
                else:
                    drow, r0, r1 = 0, 64, 128
                    cp('dve', osb[:, 0:n], o_ps[:, 0:n], [o_key], [ko])
                if sink_col is not None:
                    ts('dve', rden[drow:drow + 1, 0:n], osb[drow:drow + 1, 0:n], esink[drow:drow + 1, sink_col:sink_col + 1],
                       None, ALU.add, None, [ko, 'esink'], [kr_])
                    S.op('dve', lambda e: e.reciprocal(out=rden[drow:drow + 1, 0:n], in_=rden[drow:drow + 1, 0:n]),
                         [kr_], [kr_])
                else:
                    S.op('dve', lambda e: e.reciprocal(out=rden[drow:drow + 1, 0:n], in_=osb[drow:drow + 1, 0:n]),
                         [ko], [kr_])
                fin_pend.append((osb, rden, ko, kr_, drow, r0, r1, ych, q0, n))

            def finish_b():
                osb, rden, ko, kr_, drow, r0, r1, ych, q0, n = fin_pend.pop(0)
                pb, pbk = PS[5], ('ps', 5)
                mm(pb[0:r1, 0:n], onesf[drow:drow + 1, 0:r1], rden[drow:drow + 1, 0:n], True, True, [kr_, 'onesf'], [pbk])
                tt('dve', yT[r0:r1, ych, q0:q0 + n], osb[r0:r1, 0:n], pb[r0:r1, 0:n], ALU.mult, [ko, pbk], [('yT', ych, r0)])

            def finish_head(o_ps, o_key, odd, sink_col, ych, q0, n, scrs):
                finish_a(o_ps, o_key, odd, sink_col, ych, q0, n, scrs)
                while len(fin_pend) > 1:
                    finish_b()

            def finish_flush():
                while fin_pend:
                    finish_b()

            def attend(jobs, n, scale, scr_p):
                o_ps, o_key = nps((3, 4))
                pend = []
                first = [True]

                left = [len(jobs)]

                def flush_one():
                    (pt_t, ptk, vl, qlo, qhi, mrows, rd) = pend.pop(0)
                    left[0] -= 1
                    mm(o_ps[0:mrows, qlo:qhi], vl, pt_t[:, qlo:qhi], first[0], left[0] == 0, [ptk] + rd, [o_key])
                    first[0] = False

                for ji, (kl, rq, vl, mask, qlo, qhi, mrows, rd) in enumerate(jobs):
                    sp_t, spk = nps((0, 1, 2))
                    mm(sp_t[:, qlo:qhi], kl, rq, True, True, rd, [spk])
                    pi = ji % len(scr_p)
                    pt_t, ptk = scr_p[pi], ('pT', pi)
                    act(pt_t[:, qlo:qhi], sp_t[:, qlo:qhi], AF.Exp, [spk], [ptk], scale=scale)
                    if mask is not None:
                        tt('pool', pt_t[:, qlo:qhi], pt_t[:, qlo:qhi], mask, ALU.mult, [ptk, 'masks'], [ptk])
                    pend.append((pt_t, ptk, vl, qlo, qhi, mrows, rd))
                    if len(pend) > 2:
                        flush_one()
                while pend:
                    flush_one()
                return o_ps, o_key

            with ExitStack() as st:
                nr = alloc_nr(st)
                KmT = sbuf(st, "KmT", [128, 4, TF], BF16)
                Vm = sbuf(st, "Vm", [128, 34, 386], BF16)
                wkv1 = sbuf(st, "wkv1", [128, 8, 160], BF16)
                wkn = sbuf(st, "wkn", [128, 4, 96], BF16)
                wv = sbuf(st, "wv", [128, 4, 64], BF16)
                wcq = sbuf(st, "wcq", [128, 8, 256], BF16)
                wuq = sbuf(st, "wuq", [128, 2, 384], BF16)
                hb_t = [sbuf(st, "mhb%d" % i, [128, 8, 512], BF16) for i in range(2)]
                ckvn2 = [sbuf(st, "ckvn%d" % i, [128, 512], BF16) for i in range(2)]
                krT2 = [sbuf(st, "krT%d" % i, [32, 512], BF16) for i in range(2)]
                tabs = [sbuf(st, "mtab%d" % i, [128, 2, 512], F32) for i in range(2)]
                cqn = sbuf(st, "cqn", [128, 2, 512], BF16)
                QmT = sbuf(st, "QmT", [128, 4, 512], BF16)
                pTs = [sbuf(st, "mpT%d" % i, [128, 512], BF16) for i in range(4)]
                fscr = [(sbuf(st, "mosb", [128, 512], F32), sbuf(st, "mrden", [128, 512], F32)) for _ in range(2)]
                dma('pool', wkv1[:], win_v[:, :, C_CKV:C_CKV + 160], (), ['wkv1'], 'wkv1')
                memset('pool', wkn[:], 0.0, ['wkn'])
                wukv_h = wukv_d.rearrange("p (h n) -> p h n", h=4)
                dma('pool', wkn[:, :, 0:64], wukv_h[:, :, 0:64], ['wkn'], ['wkn2'], 'wkn')
                dma('pool', wv[:], wukv_h[:, :, 64:128], (), ['wv'], 'wv')
                dma('pool', wcq[:], win_v[:, :, C_CQ:C_CQ + 256], (), ['wcq'], 'wcq')
                dma('pool', wuq[:], wuq_d.rearrange("(c p) n -> p c n", p=128), (), ['wuq'], 'wuq')
                memset('pool', Vm[:], 0.0, ['Vm0'])
                for oc in (64, 65, 257, 258):
                    memset('pool', Vm[:, :, oc:oc + 1], 1.0, ['Vm0'])
                VMV = {0: (0, 65), 1: (65, 193), 2: (193, 258), 3: (258, 386)}
                def b1_front(bi):
                    c0, n, hk, isc = FB[bi]
                    b = bi % 2
                    dma('sp', hb_t[b][:, :, 0:n], hxf_v[:, :, c0:c0 + n], [hk], [('mhb', b)], ('mhb', b))
                    if not isc:
                        dma('sp', tabs[b][0:96, 0, :], cMf[0:96, c0 - 256:c0 - 256 + 512], (), [('mtab', b)], ('mtab', b))
                        dma('sp', tabs[b][0:96, 1, :], sMf[0:96, c0 - 256:c0 - 256 + 512], (), [('mtab', b, 1)], ('mtab', b))
                    pa, pak = nps()
                    for kc in range(8):
                        mm(pa[:, 0:n], wkv1[:, kc, 0:128], hb_t[b][:, kc, 0:n], kc == 0, kc == 7, ['wkv1', ('mhb', b)], [pak])
                    pk, pkk = nps()
                    for kc in range(8):
                        mm(pk[0:32, 0:n], wkv1[:, kc, 128:160], hb_t[b][:, kc, 0:n], kc == 0, kc == 7,
                           ['wkv1', ('mhb', b)], [pkk])
                    norm_rope(nr, pa, pak, 128, n, M_ONES128, 1.0 / 128, V_GCKV, ckvn2[b][:, 0:n], [('ckvn', b)])
                    cp('dve', krT2[b][:, 0:n], pk[0:32, 0:n], [pkk], [('krT', b)])

                def b1_back(bi):
                    c0, n, hk, isc = FB[bi]
                    b = bi % 2
                    ckvn, krT = ckvn2[b], krT2[b]
                    gens = []
                    for h in range(4):
                        pd, pdk = PS[(0, 1, 2, 5)[h]], ('ps', (0, 1, 2, 5)[h])
                        mm(pd[0:96, 0:n], wkn[:, h, :], ckvn[:, 0:n], True, False, ['wkn', 'wkn2', ('ckvn', b)], [pdk])
                        mm(pd[0:96, 0:n], M(M_EKR, 32, 96), krT[:, 0:n], False, True, [('krT', b), 'mats'], [pdk])
                        rope = None if isc else (M_SWAPM, tabs[b][0:96, 0, 0:n], tabs[b][0:96, 1, 0:n],
                                                 [('mtab', b), ('mtab', b, 1)])
                        gens.append(norm_rope_g(nr, pd, pdk, 96, n, M_ONES96, 1.0 / 96, V_GMK, KmT[0:96, h, c0:c0 + n],
                                                [('KmT', bi)], rope))
                    run_staged(gens)
                    for s in range(n // 128):
                        kt = c0 // 128 + s
                        pvv, pvk = nps()
                        mm(pvv[:, 0:256], ckvn[:, s * 128:(s + 1) * 128], wv[:].rearrange("p h n -> p (h n)"), True, True,
                           [('ckvn', b), 'wv'], [pvk])
                        vsrc = pvv[:, 0:256].rearrange("p (a b n) -> p a b n", a=2, b=2)
                        vdst = Vm[:, kt, :].rearrange("p (a c) -> p a c", a=2)
                        cp('dve', vdst[:, :, 0:64], vsrc[:, :, 0, :], [pvk, 'Vm0'], [('Vm', bi)])
                        cp('dve', vdst[:, :, 129:193], vsrc[:, :, 1, :], [pvk, 'Vm0'], [('Vm', bi, 1)])

                for bi in range(len(FB) + 1):
                    if bi < len(FB):
                        b1_front(bi)
                    if bi >= 1:
                        b1_back(bi - 1)
                allK = [('KmT', bi) for bi in range(9)] + [('Vm', bi) for bi in range(9)] + [('Vm', bi, 1) for bi in range(9)] + ['Vm0']
                for qi, (q0, n, hv, h0, hk, isc) in enumerate(QB):
                    b = qi % 2
                    dma('sp', hb_t[b][:, :, 0:n], hv[:, :, h0:h0 + n], [hk], [('mhb', b)], ('mhb', b))
                    if not isc:
                        dma('sp', tabs[b][0:96, 0, :], cMo[0:96, q0:q0 + 512], (), [('mtab', b)], ('mtab', b))
                        dma('sp', tabs[b][0:96, 1, :], sMo[0:96, q0:q0 + 512], (), [('mtab', b, 1)], ('mtab', b))
                    pcs = []
                    for c in range(2):
                        pc_, pck = nps((0, 1))
                        for kc in range(8):
                            mm(pc_[:, 0:n], wcq[:, kc, c * 128:(c + 1) * 128], hb_t[b][:, kc, 0:n], kc == 0, kc == 7,
                               ['wcq', ('mhb', b)], [pck])
                        pcs.append((pc_, pck))
                    sq_t, rstd_t, kn_t, t1_t = nr[0]
                    pq, pqk = PS[2], ('ps', 2)
                    for c in range(2):
                        act(sq_t[:, 0:n], pcs[c][0][:, 0:n], AF.Square, [pcs[c][1]], [('nr_sq', 0)])
                        mm(pq[:, 0:n], M(M_ONES128), sq_t[:, 0:n], c == 0, c == 1, [('nr_sq', 0), 'mats'], [pqk])
                    act(rstd_t[:, 0:n], pq[:, 0:n], AF.Ln, [pqk], [('nr_rstd', 0)], bias=EPS, scale=1.0 / 256)
                    act(rstd_t[:, 0:n], rstd_t[:, 0:n], AF.Exp, [('nr_rstd', 0)], [('nr_rstd', 0)], scale=-0.5)
                    for c in range(2):
                        stt(cqn[:, c, 0:n], pcs[c][0][:, 0:n], vecs[:, V_GCQ + c:V_GCQ + c + 1], rstd_t[:, 0:n], ALU.mult,
                            ALU.mult, [pcs[c][1], ('nr_rstd', 0), 'vecs'], ['cqn'])
                    gens = []
                    for h in range(4):
                        pd, pdk = PS[(0, 1, 2, 5)[h]], ('ps', (0, 1, 2, 5)[h])
                        for c in range(2):
                            mm(pd[0:96, 0:n], wuq[:, c, h * 96:(h + 1) * 96], cqn[:, c, 0:n], c == 0, c == 1, ['wuq', 'cqn'],
                               [pdk])
                        rope = None if isc else (M_SWAPM, tabs[b][0:96, 0, 0:n], tabs[b][0:96, 1, 0:n],
                                                 [('mtab', b), ('mtab', b, 1)])
                        gens.append(norm_rope_g(nr, pd, pdk, 96, n, M_ONES96, 1.0 / 96, V_GMQ, QmT[0:96, h, 0:n],
                                                [('QmT', h)], rope))
                    run_staged(gens)
                    kts = range(2) if isc else range(34)
                    for h in range(4):
                        odd = h % 2 == 1
                        jobs = []
                        for kt in kts:
                            vl = Vm[:, kt, VMV[h][0]:VMV[h][1]]
                            jobs.append((KmT[0:96, h, kt * 128:(kt + 1) * 128], QmT[0:96, h, 0:n], vl, None, 0, n,
                                         128 if odd else 65, allK + [('QmT', h)]))
                        o_ps, o_key = attend(jobs, n, 96.0 ** -0.5, pTs)
                        finish_head(o_ps, o_key, odd, None, h // 2, q0, n, fscr)
                finish_flush()
            S.barrier()

            if os.environ.get("KSTOP") == "pB1":
                return
            with ExitStack() as st:
                nr = alloc_nr(st)
                KaT = sbuf(st, "KaT", [128, TF], BF16)
                Va = sbuf(st, "Va", [128, 34, 2, 129], BF16)
                NSK = CTX + 256 + HALF
                KsT = sbuf(st, "KsT", [128, NSK], BF16)
                Vs = sbuf(st, "Vs", [128, 20, 2, 129], BF16)
                wk_ = sbuf(st, "wk_", [128, 8, 4, 128], BF16)
                wq_ = sbuf(st, "wq_", [128, 8, 2, 256], BF16)
                hb_t = [sbuf(st, "ahb%d" % i, [128, 8, 512], BF16) for i in range(2)]
                tabs = [sbuf(st, "atab%d" % i, [128, 2, 512], F32) for i in range(2)]
                QaT = sbuf(st, "QaT", [128, 2, 512], BF16)
                QsT = sbuf(st, "QsT", [128, 2, 512], BF16)
                Qz = sbuf(st, "Qz", [128, 2, 4, 512], BF16)
                memset('pool', Qz[:], 0.0, ['Qz0'])
                pTs = [sbuf(st, "apT%d" % i, [128, 512], BF16) for i in range(4)]
                fscr = [(sbuf(st, "aosb", [128, 512], F32), sbuf(st, "arden", [128, 512], F32)) for _ in range(2)]
                for i, cb in enumerate((C_AK, C_AV, C_SK, C_SV)):
                    dma('pool', wk_[:, :, i, :], win_v[:, :, cb:cb + 128], (), [('wk_', i)], 'wk_')
                for i, cb in enumerate((C_AQ, C_SQ)):
                    for pos, hq in enumerate((0, 2, 1, 3)):
                        dma('pool', wq_[:, :, i, pos * 64:(pos + 1) * 64], win_v[:, :, cb + hq * 64:cb + (hq + 1) * 64], (),
                            [('wq_', i, pos)], 'wq_')
                wkk = [('wk_', i) for i in range(4)]
                wqk = [('wq_', i, pos) for i in range(2) for pos in range(4)]
                for V_ in (Va, Vs):
                    memset('pool', V_[:], 0.0, ['V0'])
                    memset('pool', V_[:, :, :, 0:1], 1.0, ['V0'])
                    memset('pool', V_[:, :, :, 128:129], 1.0, ['V0'])

                def kv_block(b, n, wi_k, wi_v, KT_ap, kkeys, V_t, kt0, vkeys, rope):
                    pa, pak = nps()
                    for kc in range(8):
                        mm(pa[:, 0:n], wk_[:, kc, wi_k, :], hb_t[b][:, kc, 0:n], kc == 0, kc == 7, wkk + [('ahb', b)], [pak])
                    pvs = []
                    for s in range(n // 128):
                        pvv, pvk = nps((1, 2, 5) if s % 2 == 0 else (0, 2, 5))
                        if pvk == pak:
                            pvv, pvk = nps((1, 2, 5) if s % 2 == 0 else (0, 2, 5))
                        for kc in range(8):
                            mm(pvv[:, 0:128], hb_t[b][:, kc, s * 128:(s + 1) * 128], wk_[:, kc, wi_v, :], kc == 0, kc == 7,
                               wkk + [('ahb', b)], [pvk])
                        cp('dve', V_t[:, kt0 + s, :, 64:128], pvv[:, 0:128].rearrange("p (h n) -> p h n", h=2),
                           [pvk, 'V0'], vkeys)
                    norm_rope(nr, pa, pak, 128, n, M_ONES64, 1.0 / 64, V_GAK if wi_k == 0 else V_GSK, KT_ap, kkeys, rope)

                blk = 0
                for bi, (c0, n, hk, isc) in enumerate(FB):
                    b = blk % 2
                    blk += 1
                    dma('sp', hb_t[b][:, :, 0:n], hxf_v[:, :, c0:c0 + n], [hk], [('ahb', b)], ('ahb', b))
                    rope = None
                    if not isc:
                        dma('sp', tabs[b][:, 0, :], c64f[:, c0 - 256:c0 - 256 + 512], (), [('atab', b)], ('atab', b))
                        dma('sp', tabs[b][:, 1, :], s64f[:, c0 - 256:c0 - 256 + 512], (), [('atab', b, 1)], ('atab', b))
                        rope = (M_SWAP64, tabs[b][:, 0, 0:n], tabs[b][:, 1, 0:n], [('atab', b), ('atab', b, 1)])
                    kv_block(b, n, 0, 1, KaT[:, c0:c0 + n], [('KaT', bi)], Va, c0 // 128, [('Va', bi)], rope)
                    if isc:
                        kv_block(b, n, 2, 3, KsT[:, 0:256], [('KsT', 0)], Vs, 0, [('Vs', 0)], None)
                b = blk % 2
                blk += 1
                dma('sp', hb_t[b][:, :, 0:128], hxf_v[:, :, 256 + 1920:256 + 2048], [('hxf', 4)], [('ahb', b)], ('ahb', b))
                dma('sp', hb_t[b][:, :, 128:256], hxf_v[:, :, 256 + 2048:256 + 2176], [('hxf', 5)], [('ahb', b)], ('ahb', b))
                dma('sp', tabs[b][:, 0, 0:256], c64e[:, 0:256], (), [('atab', b)], ('atab', b))
                dma('sp', tabs[b][:, 1, 0:256], s64e[:, 0:256], (), [('atab', b, 1)], ('atab', b))
                kv_block(b, 256, 2, 3, KsT[:, 256:512], [('KsT', 1)], Vs, 2, [('Vs', 1)],
                         (M_SWAP64, tabs[b][:, 0, 0:256], tabs[b][:, 1, 0:256], [('atab', b), ('atab', b, 1)]))
                for i in range(4):
                    b = blk % 2
                    blk += 1
                    dma('sp', hb_t[b][:], hxo_v[:, :, i * 512:(i + 1) * 512], [('hxo', i)], [('ahb', b)], ('ahb', b))
                    dma('sp', tabs[b][:, 0, :], c64e[:, 256 + i * 512:256 + (i + 1) * 512], (), [('atab', b)], ('atab', b))
                    dma('sp', tabs[b][:, 1, :], s64e[:, 256 + i * 512:256 + (i + 1) * 512], (), [('atab', b, 1)], ('atab', b))
                    kv_block(b, 512, 2, 3, KsT[:, 512 + i * 512:512 + (i + 1) * 512], [('KsT', 2 + i)], Vs, 4 + i * 4,
                             [('Vs', 2 + i)], (M_SWAP64, tabs[b][:, 0, :], tabs[b][:, 1, :], [('atab', b), ('atab', b, 1)]))
                allKa = [('KaT', bi) for bi in range(9)] + [('Va', bi) for bi in range(9)] + ['V0']
                allKs = [('KsT', i) for i in range(6)] + [('Vs', i) for i in range(6)] + ['V0']
                for qi, (q0, n, hv, h0, hk, isc) in enumerate(QB):
                    b = blk % 2
                    blk += 1
                    dma('sp', hb_t[b][:, :, 0:n], hv[:, :, h0:h0 + n], [hk], [('ahb', b)], ('ahb', b))
                    rope = None
                    if not isc:
                        dma('sp', tabs[b][:, 0, :], c64e[:, 256 + q0:256 + q0 + 512], (), [('atab', b)], ('atab', b))
                        dma('sp', tabs[b][:, 1, :], s64e[:, 256 + q0:256 + q0 + 512], (), [('atab', b, 1)], ('atab', b))
                        rope = (M_SWAP64, tabs[b][:, 0, 0:n], tabs[b][:, 1, 0:n], [('atab', b), ('atab', b, 1)])
                    gens = []
                    for wi, (QT, gcol, qn) in enumerate(((QaT, V_GAQ, 'QaT'), (QsT, V_GSQ, 'QsT'))):
                        for tl in range(2):
                            bk = (0, 1, 2, 5)[wi * 2 + tl]
                            pa, pak = PS[bk], ('ps', bk)
                            for kc in range(8):
                                mm(pa[:, 0:n], wq_[:, kc, wi, tl * 128:(tl + 1) * 128], hb_t[b][:, kc, 0:n], kc == 0, kc == 7,
                                   wqk + [('ahb', b)], [pak])

                            def post(QT=QT, qn=qn, wi=wi, tl=tl):
                                cp('pool', Qz[0:64, wi, tl, 0:n], QT[0:64, tl, 0:n], [(qn, tl), 'Qz0'], [('Qz', wi, tl)])
                                cp('pool', Qz[64:128, wi, 2 + tl, 0:n], QT[64:128, tl, 0:n], [(qn, tl), 'Qz0'],
                                   [('Qz', wi, 2 + tl)])
                            gens.append(norm_rope_g(nr, pa, pak, 128, n, M_ONES64, 1.0 / 64, gcol, QT[:, tl, 0:n], [(qn, tl)],
                                                    rope, post))
                    run_staged(gens)
                    for hq in range(4):
                        hkv, g = hq // 2, hq % 2
                        odd = hq % 2 == 1
                        r0 = hkv * 64
                        kts = range(2) if isc else range(34)
                        jobs = []
                        for kt in kts:
                            vl = Va[:, kt, hkv, 0:128] if odd else Va[:, kt, hkv, 64:129]
                            jobs.append((KaT[:, kt * 128:(kt + 1) * 128], Qz[:, 0, hq, 0:n], vl, None, 0, n,
                                         128 if odd else 65, allKa + [('Qz', 0, hq), 'Qz0']))
                        o_ps, o_key = attend(jobs, n, 0.125, pTs)
                        finish_head(o_ps, o_key, odd, None, 4 + hq // 2, q0, n, fscr)
                        jobs = []
                        for kt in range(2):
                            vl = Vs[:, kt, hkv, 0:128] if odd else Vs[:, kt, hkv, 64:129]
                            jobs.append((KsT[:, kt * 128:(kt + 1) * 128], Qz[:, 1, hq, 0:n], vl, None, 0, n,
                                         128 if odd else 65, allKs + [('Qz', 1, hq), 'Qz0']))
                        if not isc:
                            jj0 = q0 // 128
                            for m in range(jj0, jj0 + 6):
                                lo, hi = max(m - 2, jj0), min(m, jj0 + 3)
                                if m == 0:
                                    kt, mask = 2, masks[:, 384:512]
                                elif m == 17:
                                    kt, mask = 3, masks[:, 512:640]
                                else:
                                    kt = 4 + (m - 1)
                                    mask = masks[:, (lo - (m - 2)) * 128:(hi - (m - 2) + 1) * 128]
                                qlo, qhi = (lo - jj0) * 128, (hi - jj0 + 1) * 128
                                vl = Vs[:, kt, hkv, 0:128] if odd else Vs[:, kt, hkv, 64:129]
                                jobs.append((KsT[:, kt * 128:(kt + 1) * 128], Qz[:, 1, hq, qlo:qhi], vl, mask,
                                             qlo, qhi, 128 if odd else 65, allKs + [('Qz', 1, hq), 'Qz0']))
                        o_ps, o_key = attend(jobs, n, 0.125, pTs)
                        finish_head(o_ps, o_key, odd, hq, 2 + hq // 2, q0, n, fscr)
                finish_flush()
            S.barrier()

            if os.environ.get("KSTOP") == "pB2":
                return
            yk = [('yT', c) for c in (6, 7)] + [('yT', c, 'c') for c in (6, 7)] + [('yT', c, r) for c in range(6) for r in (0, 64)]
            for e_ in S.eng.values():
                e_.wait_ge(ccy, 2 * (l + 1))
            with ExitStack() as st:
                YA = sbuf(st, "YA", [128, 2, HALF], BF16)
                YB = sbuf(st, "YB", [128, 2, HALF], BF16)
                for j in range(2):
                    rows_ = slice(j * 128, (j + 1) * 128)
                    dma('sp', YA[:, j, 0:HTF - 256], ydg[0][rows_, 256:HTF], (), [('YA', j)], ('YA', j))
                    dma('sp', YA[:, j, HTF - 256:HALF], ydg[1][rows_, 0:HALF - (HTF - 256)], (), [('YA', j, 1)], ('YA', j))
                    dma('sp', YB[:, j, :], ydg[1][rows_, HTF - HALF:HTF], (), [('YB', j)], ('YB', j))
                    if update_ctx:
                        dma('sp', yT[:, 6 + j, HALF:NQ], ydg[0][rows_, 0:CTX], (), [('yT', 6 + j, 'c')], ('yTc', j))
                    ts('dve', YA[:, j, :], YA[:, j, :], vecs[:, V_SEL:V_SEL + 1], None, ALU.mult, None,
                       [('YA', j), ('YA', j, 1), 'vecs'], [('YA', j), ('YA', j, 1)])
                    stt(yT[:, 6 + j, 0:HALF], YB[:, j, :], vecs[:, V_SEL + 1:V_SEL + 2], YA[:, j, :], ALU.mult, ALU.add,
                        [('YA', j), ('YA', j, 1), ('YB', j), 'vecs'], [('yT', 6 + j)])
            S.barrier()
            with ExitStack() as st0:
              mixT = sbuf(st0, "mixT", [128, 8, NQ], BF16)
              with ExitStack() as st:
                hxq = sbuf(st, "hxq", [128, 8, NQ], BF16)
                wg = [sbuf(st, "wg%d" % i, [128, 8, 128], BF16) for i in range(2)]
                wmg = [sbuf(st, "wmg%d" % i, [128, 8, 4, 128], BF16) for i in range(2)]
                wbr = [sbuf(st, "wbr%d" % i, [128, 2, 4, 128], BF16) for i in range(2)]
                sg = [sbuf(st, "sg%d" % i, [128, 512], BF16) for i in range(2)]
                sig = [sbuf(st, "sig%d" % i, [128, 512], F32) for i in range(2)]
                acc = [sbuf(st, "acc%d" % i, [128, 512], F32) for i in range(2)]
                for i in range(4):
                    dma('sp', hxq[:, :, i * 512:(i + 1) * 512], hxo_v[:, :, i * 512:(i + 1) * 512], [('hxo', i)], ['hxq'], 'hxq')
                if update_ctx:
                    dma('sp', hxq[:, :, HALF:NQ], hxf_v[:, :, 0:CTX], [('hxf', 0)], ['hxq'], 'hxq')
                TB = [(i * 512, 512) for i in range(4)] + ([(HALF, 256)] if update_ctx else [])
                for gc in range(8):
                    b = gc % 2
                    dma('pool', wg[b][:], win_v[:, :, C_GATE + gc * 128:C_GATE + (gc + 1) * 128], (), [('wg', b)], ('wg', b))
                    for ti_, (t0, n) in enumerate(TB):
                        pa, pak = nps()
                        for kc in range(8):
                            mm(pa[:, 0:n], wg[b][:, kc, :], hxq[:, kc, t0:t0 + n], kc == 0, kc == 7, [('wg', b), 'hxq'], [pak])
                        sb_ = ti_ % 2
                        act(sg[sb_][:, 0:n], pa[:, 0:n], AF.Silu, [pak], [('sg', sb_)])
                        tt('dve', yT[:, gc, t0:t0 + n], yT[:, gc, t0:t0 + n], sg[sb_][:, 0:n],
                           ALU.mult, yk + [('sg', sb_)], [('yg', gc, ti_)])
                ygk = [('yg', gc, ti_) for gc in range(8) for ti_ in range(len(TB))]
                it = 0
                for f in range(8):
                    b = f % 2
                    for k in range(4):
                        dma('pool', wmg[b][:, :, k, :], win_v[:, :, C_MERGE + k * 1024 + f * 128:C_MERGE + k * 1024 + (f + 1) * 128],
                            (), [('wmg', b, k)], ('wmg', b))
                        dma('pool', wbr[b][:, :, k, :], wbr_d[k].rearrange("(c p) n -> p c n", p=128)[:, :, f * 128:(f + 1) * 128],
                            (), [('wbr', b, k)], ('wbr', b))
                    wk4 = [('wmg', b, k) for k in range(4)] + [('wbr', b, k) for k in range(4)]
                    for (t0, n) in TB:
                        ab = it % 2
                        it += 1
                        for k in range(4):
                            pz, pzk = nps((0, 1, 2))
                            for kc in range(8):
                                mm(pz[:, 0:n], wmg[b][:, kc, k, :], hxq[:, kc, t0:t0 + n], kc == 0, kc == 7, wk4 + ['hxq'], [pzk])
                            pj, pjk = nps((3, 4, 5))
                            for kc in range(2):
                                mm(pj[:, 0:n], wbr[b][:, kc, k, :], yT[:, 2 * k + kc, t0:t0 + n], kc == 0, kc == 1, wk4 + ygk,
                                   [pjk])
                            sb_ = k % 2
                            act(sig[sb_][:, 0:n], pz[:, 0:n], AF.Sigmoid, [pzk], [('sig', sb_)])
                            if k == 0:
                                tt('dve', acc[ab][:, 0:n], sig[sb_][:, 0:n], pj[:, 0:n], ALU.mult, [('sig', sb_), pjk],
                                   [('acc', ab)])
                            else:
                                tt('dve', sig[sb_][:, 0:n], sig[sb_][:, 0:n], pj[:, 0:n], ALU.mult, [('sig', sb_), pjk],
                                   [('sig', sb_)])
                                if k < 3:
                                    tt('dve', acc[ab][:, 0:n], acc[ab][:, 0:n], sig[sb_][:, 0:n], ALU.add,
                                       [('sig', sb_), ('acc', ab)], [('acc', ab)])
                                else:
                                    tt('dve', mixT[:, f, t0:t0 + n], acc[ab][:, 0:n], sig[sb_][:, 0:n], ALU.add,
                                       [('sig', sb_), ('acc', ab)], [('mixT', f)])
              S.barrier()
              with ExitStack() as st:
                wo = sbuf(st, "wo", [128, 8, D], BF16)
                xt = [sbuf(st, "oxt%d" % i, [128, D], F32) for i in range(2)]
                t1 = [sbuf(st, "ot%d" % i, [128, D], F32) for i in range(2)]
                dma('pool', wo[:], wout_d.rearrange("(c p) n -> p c n", p=128), (), ['wo'], 'wo')
                mk = [('mixT', f) for f in range(8)]
                tiles = [(0, xo, i * 128, xn_out, i * 128, i * 128) for i in range(16)]
                if update_ctx:
                    tiles += [(1, cx, i * 128, cn_out, i * 128, HALF + i * 128) for i in range(2)]
                outs = []
                for ti_, (mj, src, r0, dst, d0, t0) in enumerate(tiles):
                    b = ti_ % 2
                    dma('sp', xt[b][:], src[r0:r0 + 128, :], (), [('oxt', b)], ('oxt', b))
                    for nb in range(2):
                        po, pok = nps((0, 1, 2, 3))
                        for kc in range(8):
                            mm(po[:], mixT[:, kc, t0:t0 + 128], wo[:, kc, nb * 512:(nb + 1) * 512], kc == 0, kc == 7,
                               mk + ['wo'], [pok])
                        tt('dve', t1[b][:, nb * 512:(nb + 1) * 512], po[:], G[:, mj, nb * 512:(nb + 1) * 512], ALU.mult,
                           [pok, 'G'], [('ot', b, nb)])
                        tt('pool', t1[b][:, nb * 512:(nb + 1) * 512], t1[b][:, nb * 512:(nb + 1) * 512],
                           xt[b][:, nb * 512:(nb + 1) * 512], ALU.add, [('ot', b, nb), ('oxt', b)], [('ot', b, nb)])
                    outs.append(dma('sp', dst[d0:d0 + 128, :], t1[b][:], [('ot', b, 0), ('ot', b, 1)], [('out', ti_)], ('ost', b)))
                if final:
                    S.wait_all('sp', outs)
        emit_layer(0, True, xf_in, xo_in, cx_in, x1o, c1, False)
        if os.environ.get("KSTOP"):
            P.nops = S.nops
            return P
        ccs = es.enter_context(nc.semaphore("ccs"))
        CH = 256
        nch = HALF // CH
        x1g = x1f.rearrange("(k q) n -> k q n", k=nch)
        for k in range(nch):
            nc.gpsimd.collective_compute("AllGather", ALU.bypass, replica_groups=[[0, 1], [2, 3], [4, 5], [6, 7]],
                                         ins=[x1o[k * CH:(k + 1) * CH]], outs=[x1g[k]]).then_inc(ccs, 1)

        def wait_cc():
            for e in S.eng.values():
                e.wait_ge(ccs, nch)

        def x1_rows(t0):
            hh, rem = t0 // HALF, t0 % HALF
            r = (rem // CH) * (2 * CH) + hh * CH + rem % CH
            return x1f[r:r + 128, :]

        emit_layer(1, False, x1_rows, x1o, c1, y_out, None, True, after_p0=wait_cc)
        P.nops = S.nops
    return P


def _rope_tables():
    rows = np.repeat(np.arange(SEQ // 64, dtype=np.float32), 64)
    cols = np.tile(np.arange(64, dtype=np.float32), SEQ // 64)

    def ang(rot_dim):
        q = rot_dim // 4
        fr = (np.float32(10000.0) ** (-np.arange(q, dtype=np.float32) / np.float32(q))).astype(np.float32)
        return np.concatenate([rows[:, None] * fr, cols[:, None] * fr], axis=-1).astype(np.float32)

    a64 = ang(64)
    aM = ang(32)
    c64 = np.zeros((128, SEQ), np.float32)
    s64 = np.zeros((128, SEQ), np.float32)
    for p in range(128):
        d = p % 64
        c64[p] = np.cos(a64[:, d % 32])
        s64[p] = np.sin(a64[:, d % 32]) * (-1.0 if d < 32 else 1.0)
    cM = np.zeros((128, SEQ), np.float32)
    sM = np.zeros((128, SEQ), np.float32)
    cM[0:64] = 1.0
    for p in range(64, 96):
        d = p - 64
        cM[p] = np.cos(aM[:, d % 16])
        sM[p] = np.sin(aM[:, d % 16]) * (-1.0 if d < 16 else 1.0)
    return c64, s64, cM, sM


def _const_mats():
    m = np.zeros((7, 128, 128), np.float32)
    m[M_ID] = np.eye(128)
    m[M_ONES64, 0:64, 0:64] = 1
    m[M_ONES64, 64:128, 64:128] = 1
    m[M_ONES96, 0:96, 0:96] = 1
    m[M_ONES128] = 1
    for p in range(128):
        d = p % 64
        m[M_SWAP64, (p - d) + (d + 32) % 64, p] = 1
    for p in range(64, 96):
        d = p - 64
        m[M_SWAPM, 64 + (d + 16) % 32, p] = 1
    for d in range(32):
        m[M_EKR, d, 64 + d] = 1
    return m


def _masks(h):
    k = np.arange(128)[:, None]
    q = np.arange(128)[None, :]
    prev = (k >= q).astype(np.float32)
    nxt = (k <= q).astype(np.float32)
    ones = np.ones((128, 128), np.float32)
    lv = 1.0 if h == 1 else 0.0
    rv = 1.0 if h == 0 else 0.0
    return np.concatenate([nxt, ones, prev, prev * lv, nxt * rv], axis=1)


def _vecs(p, l, h):
    v = np.zeros((128, NV), np.float32)
    v[:, V_NORMW:V_NORMW + 8] = p['norm_w'][l].reshape(8, 128).T
    v[:, V_BSHIFT:V_BSHIFT + 8] = p['b_mod'][l][0:D].reshape(8, 128).T
    v[:, V_BSCALE:V_BSCALE + 8] = p['b_mod'][l][D:2 * D].reshape(8, 128).T
    v[:, V_GCQ:V_GCQ + 2] = p['mla_cq_norm'][l].reshape(2, 128).T
    v[:, V_GCKV] = p['mla_ckv_norm'][l]
    v[0:96, V_GMQ] = p['mla_q_norm'][l]
    v[0:96, V_GMK] = p['mla_k_norm'][l]
    v[:, V_GSQ] = np.tile(p['swa_q_norm'][l], 2)
    v[:, V_GSK] = np.tile(p['swa_k_norm'][l], 2)
    v[:, V_GAQ] = np.tile(p['axa_q_norm'][l], 2)
    v[:, V_GAK] = np.tile(p['axa_k_norm'][l], 2)
    v[:, V_SINK:V_SINK + 4] = p['swa_sink'][l][None, :]
    ch = slice(h * 128, (h + 1) * 128)
    v[:, V_CONVW:V_CONVW + 4] = p['lru_conv_w'][l][:, ch].T
    v[:, V_CONVB] = p['lru_conv_b'][l][ch]
    for d in range(2):
        v[:, V_BR + d * 2] = p['lru_b_r'][l][d, ch]
        v[:, V_BI + d * 2] = p['lru_b_i'][l][d, ch]
        v[:, V_LAM + d * 2] = p['lru_lambda'][l][d, ch]
    v[:, V_SEL] = 1.0 if h == 0 else 0.0
    v[:, V_SEL + 1] = 1.0 if h == 1 else 0.0
    return v


_PROGS = {}
_CONST = {}


def kernel(**inputs):
    p = {k: np.asarray(v, dtype=np.float32) for k, v in inputs.items()}
    if 'prog' not in _PROGS:
        _PROGS['prog'] = build()
    P = _PROGS['prog']
    if 'tabs' not in _CONST:
        _CONST['tabs'] = _rope_tables()
        _CONST['mats'] = _const_mats()
    c64, s64, cM, sM = _CONST['tabs']
    A = np.ascontiguousarray
    x, ctx = p['x'], p['ctx']
    lruw = np.stack([p['lru_w_r'], p['lru_w_i']], axis=1)
    rows = A(np.stack([np.broadcast_to(p['b_mod'][l][2 * D:3 * D][None, :], (128, D)) for l in range(2)], axis=0))
    shared = {"wmod": A(p['w_mod']), "win": A(p['w_in']), "wuq": A(p['mla_w_uq']), "wukv": A(p['mla_w_ukv']),
              "wbr": A(p['w_branch']), "wout": A(p['w_out']), "rows": rows, "mats": _CONST['mats'],
              "c64f": c64, "s64f": s64, "cMf": cM, "sMf": sM}
    in_maps = []
    for core in range(8):
        b, h = core // 2, core % 2
        own = slice(h * HALF, (h + 1) * HALF)
        ext = np.concatenate([np.arange(1920, 2048), np.arange(2048, 2176), np.arange(h * HALF, (h + 1) * HALF)])
        ccv = np.stack([p['c'][b].reshape(8, 128).T, p['c_ctx'].reshape(8, 128).T], axis=-1)
        m = dict(shared)
        m.update({
            "xf": A(x[b]), "xo": A(x[b, own]), "cx": A(ctx[b]), "cc": A(ccv.astype(np.float32)),
            "vecs": A(np.stack([_vecs(p, l, h) for l in range(2)], axis=0)), "masks": _masks(h),
            "lruw": A(lruw[:, :, :, 2 * h:2 * h + 2]),
            "wlru": A(p['w_in'][:, :, C_LRU + h * 128:C_LRU + (h + 1) * 128]),
            "c64e": A(c64[:, ext]), "s64e": A(s64[:, ext]), "cMo": A(cM[:, own]), "sMo": A(sM[:, own]),
        })
        in_maps.append(m)
    res = run_bass_kernel_spmd(P.nc, in_maps, core_ids=list(range(8)))
    out = np.empty_like(x)
    for core in range(8):
        b, h = core // 2, core % 2
        out[b, h * HALF:(h + 1) * HALF] = res.results[core]["xn"]
    return out.astype(np.float32)
```

```python
import os
import numpy as np
from contextlib import ExitStack
import concourse.bass as bass
import concourse.mybir as mybir
from concourse.bass_utils import run_bass_kernel_spmd

F32 = mybir.dt.float32
BF16 = mybir.dt.bfloat16
AF = mybir.ActivationFunctionType
ALU = mybir.AluOpType

D = 1024
SEQ = 4096
HALF = 2048
CTX = 256
TF = CTX + SEQ
EPS = 1e-6
C_CQ, C_CKV, C_KR, C_SQ, C_SK, C_SV, C_AQ, C_AK, C_AV, C_LRU, C_GATE, C_MERGE = (
    0, 256, 384, 416, 672, 800, 928, 1184, 1312, 1440, 1696, 2720)
IN_COLS = 6816
V_NORMW, V_BSHIFT, V_BSCALE, V_GCQ, V_GCKV, V_GMQ, V_GMK, V_GSQ, V_GSK, V_GAQ, V_GAK = (
    0, 8, 16, 24, 26, 27, 28, 29, 30, 31, 32)
V_SINK, V_CONVW, V_CONVB, V_BR, V_BI, V_LAM, V_SEL = 33, 37, 45, 47, 51, 55, 59
NV = 64
M_ID, M_ONES64, M_ONES96, M_ONES128, M_SWAP64, M_SWAPM, M_EKR = range(7)


class Sched:
    ENG = {'pe': 'tensor', 'act': 'scalar', 'dve': 'vector', 'pool': 'gpsimd', 'sp': 'sync'}
    ROLL = 30000

    def __init__(self, nc, es):
        self.nc = nc
        self.es = es
        self.eng = {k: getattr(nc, v) for k, v in self.ENG.items()}
        self.nsem = 0
        self.sem = {}
        self.cnt = {}
        self.allsems = []
        for k in self.eng:
            self._roll(k)
        self.last_w = {}
        self.readers = {}
        self.seen = {k: {} for k in self.eng}
        self.dma_sems = {}
        self.nops = 0

    def _alloc_sem(self):
        self.nsem += 1
        return self.es.enter_context(self.nc.semaphore("s%d" % self.nsem))

    def _roll(self, e):
        self.sem[e] = self._alloc_sem()
        self.cnt[e] = 0

    def op(self, e, fn, reads=(), writes=(), dma=None):
        deps = []
        for k in reads:
            t = self.last_w.get(k)
            if t is not None:
                deps.append((t, 0))
            if isinstance(k, tuple) and k[0] in ('ps', 'pt'):
                for t in self.readers.get(k, ()):
                    deps.append((t, 2))
        for k in writes:
            t = self.last_w.get(k)
            if t is not None:
                deps.append((t, 1))
            for t in self.readers.get(k, ()):
                deps.append((t, 2))
        eng = self.eng[e]
        need = {}
        for (sem, val, pe, is_dma), kind in deps:
            if pe == e and (not is_dma) and dma is None and (kind == 2 or (kind == 1 and e == 'pe')):
                continue
            sid = id(sem)
            if self.seen[e].get(sid, 0) >= val:
                continue
            if sid not in need or need[sid][1] < val:
                need[sid] = (sem, val)
        for sem, val in need.values():
            eng.wait_ge(sem, val)
            self.seen[e][id(sem)] = val
        ins = fn(eng)
        self.nops += 1
        if dma is not None:
            s = self.dma_sems.get(dma)
            if s is None:
                s = self.dma_sems[dma] = [self._alloc_sem(), 0]
            s[1] += 16
            ins.then_inc(s[0], 16)
            tok = (s[0], s[1], e, True)
        else:
            if self.cnt[e] >= self.ROLL:
                self._roll(e)
            self.cnt[e] += 1
            ins.then_inc(self.sem[e], 1)
            tok = (self.sem[e], self.cnt[e], e, False)
        for k in reads:
            self.readers.setdefault(k, []).append(tok)
        for k in writes:
            self.last_w[k] = tok
            self.readers[k] = []
        return tok

    def wait_all(self, e, toks):
        eng = self.eng[e]
        for (sem, val, pe, is_dma) in toks:
            if self.seen[e].get(id(sem), 0) >= val:
                continue
            eng.wait_ge(sem, val)
            self.seen[e][id(sem)] = val

    def barrier(self):
        toks = [(self.sem[p], self.cnt[p], p, False) for p in self.eng if self.cnt[p] > 0]
        toks += [(s[0], s[1], None, True) for s in self.dma_sems.values()]
        for e in self.eng:
            self.wait_all(e, toks)


class Prog:
    def __init__(self, update_ctx):
        self.update_ctx = update_ctx
        self.nc = bass.Bass("TRN2", target_bir_lowering=False, num_devices=8)
        self.din = {}
        self.dout = {}

    def inp(self, name, shape, dt=F32):
        t = self.nc.dram_tensor(name, list(shape), dt, kind="ExternalInput").ap()
        self.din[name] = t
        return t

    def outp(self, name, shape, dt=F32):
        t = self.nc.dram_tensor(name, list(shape), dt, kind="ExternalOutput").ap()
        self.dout[name] = t
        return t


def build():
    P = Prog(True)
    nc = P.nc
    xf_in = P.inp("xf", [SEQ, D])
    xo_in = P.inp("xo", [HALF, D])
    cx_in = P.inp("cx", [CTX, D])
    cc = P.inp("cc", [128, 8, 2])
    wmod_a = P.inp("wmod", [2, D, 3 * D])
    win_a = P.inp("win", [2, D, IN_COLS])
    wuq_a = P.inp("wuq", [2, 256, 384])
    wukv_a = P.inp("wukv", [2, 128, 512])
    wbr_a = P.inp("wbr", [2, 4, 256, D])
    wout_a = P.inp("wout", [2, D, D])
    lruw_a = P.inp("lruw", [2, 2, 2, 2, 64, 64])
    wlru_a = P.inp("wlru", [2, D, 128])
    vecs_a = P.inp("vecs", [2, 128, NV])
    rows_a = P.inp("rows", [2, 128, D])
    mats_d = P.inp("mats", [7, 128, 128])
    masks_d = P.inp("masks", [128, 640])
    c64f = P.inp("c64f", [128, SEQ])
    s64f = P.inp("s64f", [128, SEQ])
    c64e = P.inp("c64e", [128, 256 + HALF])
    s64e = P.inp("s64e", [128, 256 + HALF])
    cMf = P.inp("cMf", [128, SEQ])
    sMf = P.inp("sMf", [128, SEQ])
    cMo = P.inp("cMo", [128, HALF])
    sMo = P.inp("sMo", [128, HALF])
    y_out = P.outp("xn", [HALF, D])
    hxf = nc.dram_tensor("hxf", [D, TF], BF16, kind="Internal").ap()
    hxo = nc.dram_tensor("hxo", [D, HALF], BF16, kind="Internal").ap()
    x1o = nc.dram_tensor("x1o", [HALF, D], F32, kind="Internal").ap()
    c1 = nc.dram_tensor("c1", [CTX, D], F32, kind="Internal").ap()
    x1f = nc.dram_tensor("x1f", [SEQ, D], F32, kind="Internal").ap()
    HTF = TF // 2
    ydm = [nc.dram_tensor("ydm%d" % i, [128, HTF], BF16, kind="Internal").ap() for i in range(2)]
    ydg = [nc.dram_tensor("ydg%d" % i, [256, HTF], BF16, kind="Internal").ap() for i in range(2)]
    hxf_v = hxf.rearrange("(c p) t -> p c t", p=128)
    hxo_v = hxo.rearrange("(c p) t -> p c t", p=128)


    with ExitStack() as es:
        S = Sched(nc, es)

        uniq = [0]

        def sbuf(st, name, shape, dt):
            uniq[0] += 1
            return st.enter_context(nc.sbuf_tensor("sb%d_%s" % (uniq[0], name), list(shape), dt))

        def dma(q, out, in_, reads, writes, slot):
            return S.op(q, lambda e: e.dma_start(out=out, in_=in_), reads, writes, dma=slot)

        def mm(out, lhsT, rhs, start, stop, reads, writes):
            return S.op('pe', lambda e: e.matmul(out, lhsT=lhsT, rhs=rhs, start=start, stop=stop), reads, writes)

        def act(out, in_, func, reads, writes, **kw):
            return S.op('act', lambda e: e.activation(out=out, in_=in_, func=func, **kw), reads, writes)

        def tt(en, out, in0, in1, op, reads, writes):
            return S.op(en, lambda e: e.tensor_tensor(out=out, in0=in0, in1=in1, op=op), reads, writes)

        def ts(en, out, in0, s1, s2, op0, op1, reads, writes):
            if s2 is None:
                return S.op(en, lambda e: e.tensor_scalar(out=out, in0=in0, scalar1=s1, scalar2=None, op0=op0),
                            reads, writes)
            return S.op(en, lambda e: e.tensor_scalar(out=out, in0=in0, scalar1=s1, scalar2=s2, op0=op0, op1=op1),
                        reads, writes)

        def stt(out, in0, scalar, in1, op0, op1, reads, writes):
            return S.op('dve', lambda e: e.scalar_tensor_tensor(out=out, in0=in0, scalar=scalar, in1=in1,
                                                                op0=op0, op1=op1), reads, writes)

        def cp(en, out, in_, reads, writes):
            return S.op(en, lambda e: e.tensor_copy(out=out, in_=in_), reads, writes)

        def memset(en, ap, val, writes):
            return S.op(en, lambda e: e.memset(ap, val), (), writes)

        PT = [es.enter_context(nc.psum_tensor("pt%d" % i, [128, 2, 512], BF16)) for i in range(2)]
        PS = [es.enter_context(nc.psum_tensor("ps%d" % i, [128, 512], F32)) for i in range(6)]
        rot = [0]

        def nps(pool=(0, 1, 2)):
            i = pool[rot[0] % len(pool)]
            rot[0] += 1
            return PS[i], ('ps', i)

        matsf = sbuf(es, "matsf", [128, 7, 128], F32)
        mats = sbuf(es, "mats", [128, 7, 128], BF16)
        onesf = sbuf(es, "onesf", [128, 128], F32)
        masks = sbuf(es, "masks", [128, 640], BF16)
        masksf = sbuf(es, "masksf", [128, 640], F32)
        dma('sp', matsf[:], mats_d.rearrange("m p n -> p m n"), (), ['matsf'], 'matsf')
        dma('sp', masksf[:], masks_d, (), ['masksf'], 'masksf')
        cp('dve', mats[:], matsf[:], ['matsf'], ['mats'])
        cp('dve', masks[:], masksf[:], ['masksf'], ['masks'])
        memset('pool', onesf[:], 1.0, ['onesf'])

        def M(i, k=128, m=128):
            return mats[0:k, i, 0:m]

        ccy = es.enter_context(nc.semaphore("ccy"))

        def emit_layer(l, update_ctx, xf, xo, cx, xn_out, cn_out, final, after_p0=None):
            NQ = HALF + (CTX if update_ctx else 0)
            wmod_v = wmod_a[l].rearrange("(c p) n -> p c n", p=128)
            win_v = win_a[l].rearrange("(c p) n -> p c n", p=128)
            wuq_d, wukv_d, wbr_d, wout_d, lruw_d = wuq_a[l], wukv_a[l], wbr_a[l], wout_a[l], lruw_a[l]
            wlru_v = wlru_a[l].rearrange("(c p) n -> p c n", p=128)
            vecs_d, rows_d = vecs_a[l], rows_a[l]
            with ExitStack() as esl:
                emit_layer_body(l, update_ctx, xf, xo, cx, xn_out, cn_out, final, NQ, wmod_v, win_v, wuq_d, wukv_d,
                                wbr_d, wout_d, lruw_d, vecs_d, rows_d, esl, after_p0, wlru_v)
            S.barrier()

        def emit_layer_body(l, update_ctx, xf, xo, cx, xn_out, cn_out, final, NQ, wmod_v, win_v, wuq_d, wukv_d,
                            wbr_d, wout_d, lruw_d, vecs_d, rows_d, esl, after_p0, wlru_v):
            vecs = sbuf(esl, "vecs", [128, NV], F32)
            AB = sbuf(esl, "AB", [128, 2, 2, 8], F32)
            G = sbuf(esl, "G", [128, 2, D], F32)
            yT = sbuf(esl, "yT", [128, 8, NQ], BF16)
            esink = sbuf(esl, "esink", [128, 4], F32)
            cs = sbuf(esl, "cs", [128, 4], F32)
            nbr = sbuf(esl, "nbr", [128, 8], F32)
            dma('sp', vecs[:], vecs_d, (), ['vecs'], 'vecs')
            act(esink[:], vecs[:, V_SINK:V_SINK + 4], AF.Exp, ['vecs'], ['esink'])
            act(cs[:], vecs[:, V_LAM:V_LAM + 4], AF.Exp, ['vecs'], ['cs'], scale=-1.0)
            act(cs[:], cs[:], AF.Ln, ['cs'], ['cs'], bias=1.0, scale=1.0)
            ts('dve', cs[:], cs[:], -8.0, None, ALU.mult, None, ['cs'], ['cs'])
            ts('dve', nbr[:], vecs[:, V_BR:V_BR + 8], -1.0, None, ALU.mult, None, ['vecs'], ['nbr'])

            with ExitStack() as st:
                cct = sbuf(st, "cct", [128, 8, 2], F32)
                sct = sbuf(st, "sct", [128, 8, 2], F32)
                scb = sbuf(st, "scb", [128, 2, 8, 128], F32)
                wm = [sbuf(st, "wm%d" % i, [128, 8, 512], F32) for i in range(2)]
                modT = sbuf(st, "modT", [128, 2, 16], F32)
                rowsb = sbuf(st, "rowsb", [128, D], F32)
                dma('sp', cct[:], cc, (), ['cct'], 'cct')
                dma('sp', rowsb[:], rows_d, (), ['rowsb'], 'rowsb')
                act(sct[:], cct[:], AF.Exp, ['cct'], ['sct'], scale=-1.0)
                ts('dve', sct[:], sct[:], 1.0, None, ALU.add, None, ['sct'], ['sct'])
                S.op('dve', lambda e: e.reciprocal(out=sct[:], in_=sct[:]), ['sct'], ['sct'])
                tt('dve', sct[:], sct[:], cct[:], ALU.mult, ['sct', 'cct'], ['sct'])
                for j in range(2):
                    for kc in range(8):
                        cp('pool', scb[:, j, kc, :], sct[:, kc, j:j + 1].to_broadcast([128, 128]), ['sct'], ['scb'])
                pm, pmk = PS[5], ('ps', 5)
                for cb in range(6):
                    b = cb % 2
                    dma('sp', wm[b][:], wmod_v[:, :, cb * 512:(cb + 1) * 512], (), [('wm', b)], ('wm', b))
                    if cb < 4:
                        for fc in range(4):
                            f = cb * 4 + fc
                            for kc in range(8):
                                mm(pm[:, f * 2:f * 2 + 2], wm[b][:, kc, fc * 128:(fc + 1) * 128], sct[:, kc, :],
                                   kc == 0, kc == 7, [('wm', b), 'sct'], [pmk])
                    else:
                        nb = cb - 4
                        for j in range(2 if update_ctx else 1):
                            pg, pgk = nps()
                            for kc in range(8):
                                mm(pg[:], scb[:, j, kc, :], wm[b][:, kc, :], kc == 0, kc == 7, [('wm', b), 'scb'], [pgk])
                            tt('dve', G[:, j, nb * 512:(nb + 1) * 512], pg[:], rowsb[:, nb * 512:(nb + 1) * 512], ALU.add,
                               [pgk, 'rowsb'], ['G'])
                    if cb == 3:
                        pmv = pm[:, 0:32].rearrange("p (f j) -> p j f", j=2)
                        for j in range(2):
                            tt('dve', modT[:, j, :], pmv[:, j, :], vecs[:, V_BSHIFT:V_BSHIFT + 16], ALU.add,
                               [pmk, 'vecs'], ['modT'])
                            stt(AB[:, j, 0, :], modT[:, j, 8:16], 1.0, vecs[:, V_NORMW:V_NORMW + 8], ALU.add, ALU.mult,
                                ['modT', 'vecs'], ['AB'])
                            cp('dve', AB[:, j, 1, :], modT[:, j, 0:8], ['modT'], ['AB'])
            S.barrier()

            with ExitStack() as st:
                xt = [sbuf(st, "xt%d" % i, [128, 4, D], F32) for i in range(2)]
                sqj = sbuf(st, "sqj", [128, D], BF16)
                ssq = [sbuf(st, "ssq%d" % i, [128, 4], F32) for i in range(2)]
                rs = [sbuf(st, "rs%d" % i, [128, 4], F32) for i in range(2)]
                xnb = [sbuf(st, "xnb%d" % i, [128, 4, D], BF16) for i in range(2)]
                hblk = [sbuf(st, "hblk%d" % i, [128, 8, 512], BF16) for i in range(2)]
                groups = [(1, cx, 0, 256, hxf_v, 0, ('hxf', 0))]
                groups += [(0, xo, i * 512, 512, hxo_v, i * 512, ('hxo', i)) for i in range(4)]
                first_full = len(groups)
                groups += [(0, xf, i * 512, 512, hxf_v, 256 + i * 512, ('hxf', i + 1)) for i in range(8)]

                def stage_a(gi):
                    mj, src, r0, n, dst, c0, dkey = groups[gi]
                    b = gi % 2
                    ns = n // 128
                    for s_ in range(ns):
                        t0_ = r0 + s_ * 128
                        src_rows = src(t0_) if callable(src) else src[t0_:t0_ + 128, :]
                        dma('sp', xt[b][:, s_, :], src_rows, (), [('xt', b, s_)], ('xt', b))
                    xk = [('xt', b, s_) for s_ in range(ns)]
                    for s_ in range(ns):
                        act(sqj[:], xt[b][:, s_, :], AF.Square, xk, ['sqj', ('ssq', b)], accum_out=ssq[b][:, s_:s_ + 1])
                    act(rs[b][:, 0:ns], ssq[b][:, 0:ns], AF.Ln, [('ssq', b)], [('rs', b)], bias=EPS, scale=1.0 / D)
                    act(rs[b][:, 0:ns], rs[b][:, 0:ns], AF.Exp, [('rs', b)], [('rs', b)], scale=-0.5)
                    for s_ in range(ns):
                        ts('dve', xnb[b][:, s_, :], xt[b][:, s_, :], rs[b][:, s_:s_ + 1], None,
                           ALU.mult, None, xk + [('rs', b)], [('xnb', b, s_)])

                def stage_b(gi):
                    mj, src, r0, n, dst, c0, dkey = groups[gi]
                    b = gi % 2
                    ns = n // 128
                    for cp_ in range(4):
                        pv_, pk_ = PT[cp_ % 2], ('pt', cp_ % 2)
                        for s_ in range(ns):
                            for cc_ in range(2):
                                c = cp_ * 2 + cc_
                                S.op('pe', lambda e, c=c, cc_=cc_, s_=s_, pv_=pv_: e.transpose(
                                    out=pv_[:, cc_, s_ * 128:(s_ + 1) * 128], in_=xnb[b][:, s_, c * 128:(c + 1) * 128],
                                    identity=M(M_ID)), [('xnb', b, s_), 'mats'], [pk_])
                        for cc_ in range(2):
                            c = cp_ * 2 + cc_
                            o = hblk[b][:, c, 0:n]
                            if cp_ % 2 == 0:
                                ts('dve', o, pv_[:, cc_, 0:n], AB[:, mj, 0, c:c + 1], AB[:, mj, 1, c:c + 1], ALU.mult,
                                   ALU.add, [pk_, 'AB'], [('hblk', b, c)])
                            else:
                                act(o, pv_[:, cc_, 0:n], AF.Identity, [pk_, 'AB'], [('hblk', b, c)],
                                    scale=AB[:, mj, 0, c:c + 1], bias=AB[:, mj, 1, c:c + 1])
                    dma('pool', dst[:, :, c0:c0 + n], hblk[b][:, :, 0:n], [('hblk', b, c) for c in range(8)], [dkey],
                        ('hst', b))

                for gi in range(len(groups) + 1):
                    if gi == first_full and after_p0 is not None:
                        after_p0()
                    if gi < len(groups):
                        stage_a(gi)
                    if gi >= 1:
                        stage_b(gi - 1)
            S.barrier()

            if os.environ.get("KSTOP") == "p1":
                return
            FB = [(0, 256, ('hxf', 0), True)] + [(256 + i * 512, 512, ('hxf', i + 1), False) for i in range(8)]

            nr_ctr = [0]

            def norm_rope_g(st_tiles, src_ps, src_key, rows, n, ones_i, inv_d, gcol, out_ap, out_keys, rope=None, post=None):
                si = nr_ctr[0] % 4
                nr_ctr[0] += 1
                sq_t, rstd_t, kn_t, t1_t = st_tiles[si]
                ksq, krs, kkn, kt1 = ('nr_sq', si), ('nr_rstd', si), ('nr_kn', si), ('nr_t1', si)
                act(sq_t[0:rows, 0:n], src_ps[0:rows, 0:n], AF.Square, [src_key], [ksq])
                yield
                pq, pqk = nps((3, 4))
                mm(pq[0:rows, 0:n], M(ones_i, rows, rows), sq_t[0:rows, 0:n], True, True, [ksq, 'mats'], [pqk])
                act(rstd_t[0:rows, 0:n], pq[0:rows, 0:n], AF.Ln, [pqk], [krs], bias=EPS, scale=inv_d)
                act(rstd_t[0:rows, 0:n], rstd_t[0:rows, 0:n], AF.Exp, [krs], [krs], scale=-0.5)
                if rope is None:
                    stt(out_ap, src_ps[0:rows, 0:n], vecs[0:rows, gcol:gcol + 1], rstd_t[0:rows, 0:n], ALU.mult, ALU.mult,
                        [src_key, krs, 'vecs'], out_keys)
                    if post is not None:
                        post()
                    return
                swap_i, cos_ap, sin_ap, tab_keys = rope
                stt(kn_t[0:rows, 0:n], src_ps[0:rows, 0:n], vecs[0:rows, gcol:gcol + 1], rstd_t[0:rows, 0:n], ALU.mult,
                    ALU.mult, [src_key, krs, 'vecs'], [kkn])
                yield
                pw, pwk = nps((3, 4))
                mm(pw[0:rows, 0:n], M(swap_i, rows, rows), kn_t[0:rows, 0:n], True, True, [kkn, 'mats'], [pwk])
                tt('pool', t1_t[0:rows, 0:n], kn_t[0:rows, 0:n], cos_ap, ALU.mult, [kkn] + tab_keys, [kt1])
                tt('dve', rstd_t[0:rows, 0:n], pw[0:rows, 0:n], sin_ap, ALU.mult, [pwk] + tab_keys, [krs])
                yield
                tt('dve', out_ap, t1_t[0:rows, 0:n], rstd_t[0:rows, 0:n], ALU.add, [kt1, krs], out_keys)
                if post is not None:
                    post()

            def run_staged(gens):
                gens = list(gens)
                while gens:
                    nxt = []
                    for g_ in gens:
                        try:
                            next(g_)
                            nxt.append(g_)
                        except StopIteration:
                            pass
                    gens = nxt

            def norm_rope(*a, **k):
                run_staged([norm_rope_g(*a, **k)])

            def alloc_nr(st):
                return [(sbuf(st, "nr_sq", [128, 512], BF16), sbuf(st, "nr_rstd", [128, 512], F32),
                         sbuf(st, "nr_kn", [128, 512], BF16), sbuf(st, "nr_t1", [128, 512], F32)) for _ in range(4)]

            ZW = 4358
            with ExitStack() as st:
                wl = sbuf(st, "wl", [128, 8, 128], BF16)
                bdf = sbuf(st, "bdf", [128, 4, 128], F32)
                bd = sbuf(st, "bd", [128, 4, 128], BF16)
                hb_t = [sbuf(st, "lhb%d" % i, [128, 8, 512], BF16) for i in range(2)]
                zl = sbuf(st, "zl", [128, ZW], F32)
                ul = sbuf(st, "ul", [128, TF], F32)
                ub = sbuf(st, "ub", [128, TF], BF16)
                ltmp = [tuple(sbuf(st, "l%s%d" % (nm, i), [128, 512], F32) for nm in "AT") for i in range(2)]
                lh = [sbuf(st, "lH%d" % i, [128, 512], F32) for i in range(3)]
                lctr = [0]
                Rall = sbuf(st, "Rall", [128, TF], F32)
                Iall = sbuf(st, "Iall", [128, TF], F32)
                Yall = sbuf(st, "Yall", [128, TF], F32)
                Yb = sbuf(st, "Yb", [128, TF], BF16)
                dma('pool', wl[:], wlru_v, (), ['wl'], 'wl')
                memset('pool', bdf[:], 0.0, ['bdf'])
                for g in range(2):
                    for d in range(2):
                        for hh in range(2):
                            dma('sp', bdf[hh * 64:(hh + 1) * 64, g * 2 + d, hh * 64:(hh + 1) * 64],
                                lruw_d[g, d, hh], ['bdf'], [('bdfq', g, d, hh)], 'bdf')
                cp('dve', bd[:], bdf[:], ['bdf'] + [('bdfq', g, d, hh) for g in range(2) for d in range(2) for hh in range(2)],
                   ['bd'])
                c = 0
                memset('pool', zl[:], 0.0, ['zl'])
                for bi, (c0, n, hk, isc) in enumerate(FB):
                    b = bi % 2
                    dma('sp', hb_t[b][:, :, 0:n], hxf_v[:, :, c0:c0 + n], [hk], [('lhb', b)], ('lhb', b))
                    pz, pzk = nps()
                    for kc in range(8):
                        mm(pz[:, 0:n], wl[:, kc, :], hb_t[b][:, kc, 0:n], kc == 0, kc == 7, ['wl', ('lhb', b)], [pzk])
                    zc0 = 2 if isc else 261 + (c0 - 256)
                    act(zl[:, zc0:zc0 + n], pz[:, 0:n], AF.Copy, [pzk], ['zl'])
                for (u0, z0, n) in ((0, 2, 256), (256, 261, SEQ)):
                    for j in range(4):
                        wj = vecs[:, V_CONVW + c * 4 + j:V_CONVW + c * 4 + j + 1]
                        zin = zl[:, z0 + j - 2:z0 + j - 2 + n]
                        if j == 0:
                            ts('dve', ul[:, u0:u0 + n], zin, wj, vecs[:, V_CONVB + c:V_CONVB + c + 1], ALU.mult, ALU.add,
                               ['zl', 'vecs'], ['ul'])
                        else:
                            stt(ul[:, u0:u0 + n], zin, wj, ul[:, u0:u0 + n], ALU.mult, ALU.add, ['zl', 'vecs', 'ul'],
                                ['ul'])
                for d in range(2):
                    order = list(range(9)) if d == 0 else [0] + list(range(8, 0, -1))
                    for bi in order:
                        c0, n, hk, isc = FB[bi]
                        if d == 0:
                            cp('pool', ub[:, c0:c0 + n], ul[:, c0:c0 + n], ['ul'], [('ub', bi)])
                        pr, prk = nps((0, 1, 2))
                        mm(pr[:, 0:n], bd[:, 0 * 2 + d, :], ub[:, c0:c0 + n], True, True, ['bd', ('ub', bi)], [prk])
                        pi_, pik = nps((3, 4, 5))
                        mm(pi_[:, 0:n], bd[:, 1 * 2 + d, :], ub[:, c0:c0 + n], True, True, ['bd', ('ub', bi)], [pik])
                        bcr = V_BR + d * 2 + c
                        bci = V_BI + d * 2 + c
                        act(Rall[:, c0:c0 + n], pr[:, 0:n], AF.Sigmoid, [prk, 'vecs'], [('Rall', bi)], scale=1.0,
                            bias=vecs[:, bcr:bcr + 1])
                        act(Iall[:, c0:c0 + n], pi_[:, 0:n], AF.Sigmoid, [pik, 'vecs'], [('Iall', bi)], scale=1.0,
                            bias=vecs[:, bci:bci + 1])
                    prev_h = None
                    for oi, bi in enumerate(order):
                        c0, n, hk, isc = FB[bi]
                        j = lctr[0] % 2
                        j3 = lctr[0] % 3
                        lctr[0] += 1
                        At, Tt = ltmp[j]
                        Rt = Rall[:, c0:c0 + n]
                        It = Iall[:, c0:c0 + n]
                        kR, kI, kA, kT = ('Rall', bi), ('Iall', bi), ('lA', j), ('lT', j)
                        if d == 0:
                            Ht, kH = Yall[:, c0:c0 + n], ('Yall', bi)
                        else:
                            Ht, kH = lh[j3][:, 0:n], ('lH', j3)
                        act(At[:, 0:n], Rt, AF.Exp, [kR, 'cs'], [kA], scale=cs[:, d * 2 + c:d * 2 + c + 1])
                        act(Tt[:, 0:n], At[:, 0:n], AF.Square, [kA], [kT])
                        act(Tt[:, 0:n], Tt[:, 0:n], AF.Ln, [kT], [kT], scale=-1.0, bias=1.0)
                        act(Tt[:, 0:n], Tt[:, 0:n], AF.Exp, [kT], [kT], scale=0.5)
                        tt('pool', It, It, ul[:, c0:c0 + n], ALU.mult, [kI, 'ul'], [kI])
                        tt('dve', It, It, Tt[:, 0:n], ALU.mult, [kI, kT], [kI])
                        if d == 0:
                            o_, da_, db_ = Ht, At[:, 0:n], It
                            init = 0.0 if prev_h is None else prev_h[0][:, prev_h[1] - 1:prev_h[1]]
                        else:
                            o_, da_, db_ = Ht[:, ::-1], At[:, 0:n][:, ::-1], It[:, ::-1]
                            init = 0.0 if prev_h is None else prev_h[0][:, 0:1]
                        rk = [kA, kI] + ([] if prev_h is None else [prev_h[2]])
                        S.op('dve', lambda e, o_=o_, da_=da_, db_=db_, init=init: e.tensor_tensor_scan(
                            out=o_, data0=da_, data1=db_, initial=init, op0=ALU.mult, op1=ALU.add), rk, [kH])
                        prev_h = (Ht, n, kH)
                        if d == 1:
                            S.op('dve', lambda e, c0=c0, n=n, Ht=Ht: e.tensor_tensor(
                                out=Yb[:, c0:c0 + n], in0=Yall[:, c0:c0 + n], in1=Ht, op=ALU.add),
                                [kH, ('Yall', bi)], [('Yb', bi)])
                ybk = [('Yb', bi) for bi in range(9)]
                t0_ = dma('sp', ydm[0], Yb[:, 0:HTF], ybk, [('ydm', l, 0)], 'ydm')
                t1_ = dma('sp', ydm[1], Yb[:, HTF:TF], ybk, [('ydm', l, 1)], 'ydm')
                S.wait_all('pool', [t0_, t1_])
                for i in range(2):
                    nc.gpsimd.collective_compute("AllGather", ALU.bypass, replica_groups=[[0, 1], [2, 3], [4, 5], [6, 7]],
                                                 ins=[ydm[i]], outs=[ydg[i]]).then_inc(ccy, 1)
            S.barrier()

            if os.environ.get("KSTOP") == "pA":
                return
            QB = [(i * 512, 512, hxo_v, i * 512, ('hxo', i), False) for i in range(4)]
            if update_ctx:
                QB.append((HALF, 256, hxf_v, 0, ('hxf', 0), True))

            fin_ctr = [0]
            fin_pend = []

            def finish_a(o_ps, o_key, odd, sink_col, ych, q0, n, scrs):
                si = fin_ctr[0] % 2
                fin_ctr[0] += 1
                osb, rden = scrs[si]
                ko, kr_ = ('osb', si), ('rden', si)
                if not odd:
                    drow, r0, r1 = 64, 0, 64
                    cp('dve', osb[0:65, 0:n], o_ps[0:65, 0:n], [o_key], [ko])
                else:
                    drow, r0, r1 = 0, 64, 128
                    cp('dve', osb[:, 0:n], o_ps[:, 0:n], [o_key], [ko])
                if sink_col is not None:
                    ts('dve', rden[drow:drow + 1, 0:n], osb[drow:drow + 1, 0:n], esink[drow:drow + 1, sink_col:sink_col + 1],
                       None, ALU.add, None, [ko, 'esink'], [kr_])
                    S.op('dve', lambda e: e.reciprocal(out=rden[drow:drow + 1, 0:n], in_=rden[drow:drow + 1, 0:n]),
                         [kr_], [kr_])
                else:
                    S.op('dve', lambda e: e.reciprocal(out=rden[drow:drow + 1, 0:n], in_=osb[drow:drow + 1, 0:n]),
                         [ko], [kr_])
                fin_pend.append((osb, rden, ko, kr_, drow, r0, r1, ych, q0, n))

            def finish_b():
                osb, rden, ko, kr_, drow, r0, r1, ych, q0, n = fin_pend.pop(0)
                pb, pbk = PS[5], ('ps', 5)
                mm(pb[0:r1, 0:n], onesf[drow:drow + 1, 0:r1], rden[drow:drow + 1, 0:n], True, True, [kr_, 'onesf'], [pbk])
                tt('dve', yT[r0:r1, ych, q0:q0 + n], osb[r0:r1, 0:n], pb[r0:r1, 0:n], ALU.mult, [ko, pbk], [('yT', ych, r0)])

            def finish_head(o_ps, o_key, odd, sink_col, ych, q0, n, scrs):
                finish_a(o_ps, o_key, odd, sink_col, ych, q0, n, scrs)
                while len(fin_pend) > 1:
                    finish_b()

            def finish_flush():
                while fin_pend:
                    finish_b()

            def attend(jobs, n, scale, scr_p):
                o_ps, o_key = nps((3, 4))
                pend = []
                first = [True]

                left = [len(jobs)]

                def flush_one():
                    (pt_t, ptk, vl, qlo, qhi, mrows, rd) = pend.pop(0)
                    left[0] -= 1
                    mm(o_ps[0:mrows, qlo:qhi], vl, pt_t[:, qlo:qhi], first[0], left[0] == 0, [ptk] + rd, [o_key])
                    first[0] = False

                for ji, (kl, rq, vl, mask, qlo, qhi, mrows, rd) in enumerate(jobs):
                    sp_t, spk = nps((0, 1, 2))
                    mm(sp_t[:, qlo:qhi], kl, rq, True, True, rd, [spk])
                    pi = ji % len(scr_p)
                    pt_t, ptk = scr_p[pi], ('pT', pi)
                    act(pt_t[:, qlo:qhi], sp_t[:, qlo:qhi], AF.Exp, [spk], [ptk], scale=scale)
                    if mask is not None:
                        tt('pool', pt_t[:, qlo:qhi], pt_t[:, qlo:qhi], mask, ALU.mult, [ptk, 'masks'], [ptk])
                    pend.append((pt_t, ptk, vl, qlo, qhi, mrows, rd))
                    if len(pend) > 2:
                        flush_one()
                while pend:
                    flush_one()
                return o_ps, o_key

            with ExitStack() as st:
                nr = alloc_nr(st)
                KmT = sbuf(st, "KmT", [128, 4, TF], BF16)
                Vm = sbuf(st, "Vm", [128, 34, 386], BF16)
                wkv1 = sbuf(st, "wkv1", [128, 8, 160], BF16)
                wkn = sbuf(st, "wkn", [128, 4, 96], BF16)
                wv = sbuf(st, "wv", [128, 4, 64], BF16)
                wcq = sbuf(st, "wcq", [128, 8, 256], BF16)
                wuq = sbuf(st, "wuq", [128, 2, 384], BF16)
                hb_t = [sbuf(st, "mhb%d" % i, [128, 8, 512], BF16) for i in range(2)]
                ckvn2 = [sbuf(st, "ckvn%d" % i, [128, 512], BF16) for i in range(2)]
                krT2 = [sbuf(st, "krT%d" % i, [32, 512], BF16) for i in range(2)]
                tabs = [sbuf(st, "mtab%d" % i, [128, 2, 512], F32) for i in range(2)]
                cqn = sbuf(st, "cqn", [128, 2, 512], BF16)
                QmT = sbuf(st, "QmT", [128, 4, 512], BF16)
                pTs = [sbuf(st, "mpT%d" % i, [128, 512], BF16) for i in range(4)]
                fscr = [(sbuf(st, "mosb", [128, 512], F32), sbuf(st, "mrden", [128, 512], F32)) for _ in range(2)]
                dma('pool', wkv1[:], win_v[:, :, C_CKV:C_CKV + 160], (), ['wkv1'], 'wkv1')
                memset('pool', wkn[:], 0.0, ['wkn'])
                wukv_h = wukv_d.rearrange("p (h n) -> p h n", h=4)
                dma('pool', wkn[:, :, 0:64], wukv_h[:, :, 0:64], ['wkn'], ['wkn2'], 'wkn')
                dma('pool', wv[:], wukv_h[:, :, 64:128], (), ['wv'], 'wv')
                dma('pool', wcq[:], win_v[:, :, C_CQ:C_CQ + 256], (), ['wcq'], 'wcq')
                dma('pool', wuq[:], wuq_d.rearrange("(c p) n -> p c n", p=128), (), ['wuq'], 'wuq')
                memset('pool', Vm[:], 0.0, ['Vm0'])
                for oc in (64, 65, 257, 258):
                    memset('pool', Vm[:, :, oc:oc + 1], 1.0, ['Vm0'])
                VMV = {0: (0, 65), 1: (65, 193), 2: (193, 258), 3: (258, 386)}
                def b1_front(bi):
                    c0, n, hk, isc = FB[bi]
                    b = bi % 2
                    dma('sp', hb_t[b][:, :, 0:n], hxf_v[:, :, c0:c0 + n], [hk], [('mhb', b)], ('mhb', b))
                    if not isc:
                        dma('sp', tabs[b][0:96, 0, :], cMf[0:96, c0 - 256:c0 - 256 + 512], (), [('mtab', b)], ('mtab', b))
                        dma('sp', tabs[b][0:96, 1, :], sMf[0:96, c0 - 256:c0 - 256 + 512], (), [('mtab', b, 1)], ('mtab', b))
                    pa, pak = nps()
                    for kc in range(8):
                        mm(pa[:, 0:n], wkv1[:, kc, 0:128], hb_t[b][:, kc, 0:n], kc == 0, kc == 7, ['wkv1', ('mhb', b)], [pak])
                    pk, pkk = nps()
                    for kc in range(8):
                        mm(pk[0:32, 0:n], wkv1[:, kc, 128:160], hb_t[b][:, kc, 0:n], kc == 0, kc == 7,
                           ['wkv1', ('mhb', b)], [pkk])
                    norm_rope(nr, pa, pak, 128, n, M_ONES128, 1.0 / 128, V_GCKV, ckvn2[b][:, 0:n], [('ckvn', b)])
                    cp('dve', krT2[b][:, 0:n], pk[0:32, 0:n], [pkk], [('krT', b)])

                def b1_back(bi):
                    c0, n, hk, isc = FB[bi]
                    b = bi % 2
                    ckvn, krT = ckvn2[b], krT2[b]
                    gens = []
                    for h in range(4):
                        pd, pdk = PS[(0, 1, 2, 5)[h]], ('ps', (0, 1, 2, 5)[h])
                        mm(pd[0:96, 0:n], wkn[:, h, :], ckvn[:, 0:n], True, False, ['wkn', 'wkn2', ('ckvn', b)], [pdk])
                        mm(pd[0:96, 0:n], M(M_EKR, 32, 96), krT[:, 0:n], False, True, [('krT', b), 'mats'], [pdk])
                        rope = None if isc else (M_SWAPM, tabs[b][0:96, 0, 0:n], tabs[b][0:96, 1, 0:n],
                                                 [('mtab', b), ('mtab', b, 1)])
                        gens.append(norm_rope_g(nr, pd, pdk, 96, n, M_ONES96, 1.0 / 96, V_GMK, KmT[0:96, h, c0:c0 + n],
                                                [('KmT', bi)], rope))
                    run_staged(gens)
                    for s in range(n // 128):
                        kt = c0 // 128 + s
                        pvv, pvk = nps()
                        mm(pvv[:, 0:256], ckvn[:, s * 128:(s + 1) * 128], wv[:].rearrange("p h n -> p (h n)"), True, True,
                           [('ckvn', b), 'wv'], [pvk])
                        vsrc = pvv[:, 0:256].rearrange("p (a b n) -> p a b n", a=2, b=2)
                        vdst = Vm[:, kt, :].rearrange("p (a c) -> p a c", a=2)
                        cp('dve', vdst[:, :, 0:64], vsrc[:, :, 0, :], [pvk, 'Vm0'], [('Vm', bi)])
                        cp('dve', vdst[:, :, 129:193], vsrc[:, :, 1, :], [pvk, 'Vm0'], [('Vm', bi, 1)])

                for bi in range(len(FB) + 1):
                    if bi < len(FB):
                        b1_front(bi)
                    if bi >= 1:
                        b1_back(bi - 1)
                allK = [('KmT', bi) for bi in range(9)] + [('Vm', bi) for bi in range(9)] + [('Vm', bi, 1) for bi in range(9)] + ['Vm0']
                for qi, (q0, n, hv, h0, hk, isc) in enumerate(QB):
                    b = qi % 2
                    dma('sp', hb_t[b][:, :, 0:n], hv[:, :, h0:h0 + n], [hk], [('mhb', b)], ('mhb', b))
                    if not isc:
                        dma('sp', tabs[b][0:96, 0, :], cMo[0:96, q0:q0 + 512], (), [('mtab', b)], ('mtab', b))
                        dma('sp', tabs[b][0:96, 1, :], sMo[0:96, q0:q0 + 512], (), [('mtab', b, 1)], ('mtab', b))
                    pcs = []
                    for c in range(2):
                        pc_, pck = nps((0, 1))
                        for kc in range(8):
                            mm(pc_[:, 0:n], wcq[:, kc, c * 128:(c + 1) * 128], hb_t[b][:, kc, 0:n], kc == 0, kc == 7,
                               ['wcq', ('mhb', b)], [pck])
                        pcs.append((pc_, pck))
                    sq_t, rstd_t, kn_t, t1_t = nr[0]
                    pq, pqk = PS[2], ('ps', 2)
                    for c in range(2):
                        act(sq_t[:, 0:n], pcs[c][0][:, 0:n], AF.Square, [pcs[c][1]], [('nr_sq', 0)])
                        mm(pq[:, 0:n], M(M_ONES128), sq_t[:, 0:n], c == 0, c == 1, [('nr_sq', 0), 'mats'], [pqk])
                    act(rstd_t[:, 0:n], pq[:, 0:n], AF.Ln, [pqk], [('nr_rstd', 0)], bias=EPS, scale=1.0 / 256)
                    act(rstd_t[:, 0:n], rstd_t[:, 0:n], AF.Exp, [('nr_rstd', 0)], [('nr_rstd', 0)], scale=-0.5)
                    for c in range(2):
                        stt(cqn[:, c, 0:n], pcs[c][0][:, 0:n], vecs[:, V_GCQ + c:V_GCQ + c + 1], rstd_t[:, 0:n], ALU.mult,
                            ALU.mult, [pcs[c][1], ('nr_rstd', 0), 'vecs'], ['cqn'])
                    gens = []
                    for h in range(4):
                        pd, pdk = PS[(0, 1, 2, 5)[h]], ('ps', (0, 1, 2, 5)[h])
                        for c in range(2):
                            mm(pd[0:96, 0:n], wuq[:, c, h * 96:(h + 1) * 96], cqn[:, c, 0:n], c == 0, c == 1, ['wuq', 'cqn'],
                               [pdk])
                        rope = None if isc else (M_SWAPM, tabs[b][0:96, 0, 0:n], tabs[b][0:96, 1, 0:n],
                                                 [('mtab', b), ('mtab', b, 1)])
                        gens.append(norm_rope_g(nr, pd, pdk, 96, n, M_ONES96, 1.0 / 96, V_GMQ, QmT[0:96, h, 0:n],
                                                [('QmT', h)], rope))
                    run_staged(gens)
                    kts = range(2) if isc else range(34)
                    for h in range(4):
                        odd = h % 2 == 1
                        jobs = []
                        for kt in kts:
                            vl = Vm[:, kt, VMV[h][0]:VMV[h][1]]
                            jobs.append((KmT[0:96, h, kt * 128:(kt + 1) * 128], QmT[0:96, h, 0:n], vl, None, 0, n,
                                         128 if odd else 65, allK + [('QmT', h)]))
                        o_ps, o_key = attend(jobs, n, 96.0 ** -0.5, pTs)
                        finish_head(o_ps, o_key, odd, None, h // 2, q0, n, fscr)
                finish_flush()
            S.barrier()

            if os.environ.get("KSTOP") == "pB1":
                return
            with ExitStack() as st:
                nr = alloc_nr(st)
                KaT = sbuf(st, "KaT", [128, TF], BF16)
                Va = sbuf(st, "Va", [128, 34, 2, 129], BF16)
                NSK = CTX + 256 + HALF
                KsT = sbuf(st, "KsT", [128, NSK], BF16)
                Vs = sbuf(st, "Vs", [128, 20, 2, 129], BF16)
                wk_ = sbuf(st, "wk_", [128, 8, 4, 128], BF16)
                wq_ = sbuf(st, "wq_", [128, 8, 2, 256], BF16)
                hb_t = [sbuf(st, "ahb%d" % i, [128, 8, 512], BF16) for i in range(2)]
                tabs = [sbuf(st, "atab%d" % i, [128, 2, 512], F32) for i in range(2)]
                QaT = sbuf(st, "QaT", [128, 2, 512], BF16)
                QsT = sbuf(st, "QsT", [128, 2, 512], BF16)
                Qz = sbuf(st, "Qz", [128, 2, 4, 512], BF16)
                memset('pool', Qz[:], 0.0, ['Qz0'])
                pTs = [sbuf(st, "apT%d" % i, [128, 512], BF16) for i in range(4)]
                fscr = [(sbuf(st, "aosb", [128, 512], F32), sbuf(st, "arden", [128, 512], F32)) for _ in range(2)]
                for i, cb in enumerate((C_AK, C_AV, C_SK, C_SV)):
                    dma('pool', wk_[:, :, i, :], win_v[:, :, cb:cb + 128], (), [('wk_', i)], 'wk_')
                for i, cb in enumerate((C_AQ, C_SQ)):
                    for pos, hq in enumerate((0, 2, 1, 3)):
                        dma('pool', wq_[:, :, i, pos * 64:(pos + 1) * 64], win_v[:, :, cb + hq * 64:cb + (hq + 1) * 64], (),
                            [('wq_', i, pos)], 'wq_')
                wkk = [('wk_', i) for i in range(4)]
                wqk = [('wq_', i, pos) for i in range(2) for pos in range(4)]
                for V_ in (Va, Vs):
                    memset('pool', V_[:], 0.0, ['V0'])
                    memset('pool', V_[:, :, :, 0:1], 1.0, ['V0'])
                    memset('pool', V_[:, :, :, 128:129], 1.0, ['V0'])

                def kv_block(b, n, wi_k, wi_v, KT_ap, kkeys, V_t, kt0, vkeys, rope):
                    pa, pak = nps()
                    for kc in range(8):
                        mm(pa[:, 0:n], wk_[:, kc, wi_k, :], hb_t[b][:, kc, 0:n], kc == 0, kc == 7, wkk + [('ahb', b)], [pak])
                    pvs = []
                    for s in range(n // 128):
                        pvv, pvk = nps((1, 2, 5) if s % 2 == 0 else (0, 2, 5))
                        if pvk == pak:
                            pvv, pvk = nps((1, 2, 5) if s % 2 == 0 else (0, 2, 5))
                        for kc in range(8):
                            mm(pvv[:, 0:128], hb_t[b][:, kc, s * 128:(s + 1) * 128], wk_[:, kc, wi_v, :], kc == 0, kc == 7,
                               wkk + [('ahb', b)], [pvk])
                        cp('dve', V_t[:, kt0 + s, :, 64:128], pvv[:, 0:128].rearrange("p (h n) -> p h n", h=2),
                           [pvk, 'V0'], vkeys)
                    norm_rope(nr, pa, pak, 128, n, M_ONES64, 1.0 / 64, V_GAK if wi_k == 0 else V_GSK, KT_ap, kkeys, rope)

                blk = 0
                for bi, (c0, n, hk, isc) in enumerate(FB):
                    b = blk % 2
                    blk += 1
                    dma('sp', hb_t[b][:, :, 0:n], hxf_v[:, :, c0:c0 + n], [hk], [('ahb', b)], ('ahb', b))
                    rope = None
                    if not isc:
                        dma('sp', tabs[b][:, 0, :], c64f[:, c0 - 256:c0 - 256 + 512], (), [('atab', b)], ('atab', b))
                        dma('sp', tabs[b][:, 1, :], s64f[:, c0 - 256:c0 - 256 + 512], (), [('atab', b, 1)], ('atab', b))
                        rope = (M_SWAP64, tabs[b][:, 0, 0:n], tabs[b][:, 1, 0:n], [('atab', b), ('atab', b, 1)])
                    kv_block(b, n, 0, 1, KaT[:, c0:c0 + n], [('KaT', bi)], Va, c0 // 128, [('Va', bi)], rope)
                    if isc:
                        kv_block(b, n, 2, 3, KsT[:, 0:256], [('KsT', 0)], Vs, 0, [('Vs', 0)], None)
                b = blk % 2
                blk += 1
                dma('sp', hb_t[b][:, :, 0:128], hxf_v[:, :, 256 + 1920:256 + 2048], [('hxf', 4)], [('ahb', b)], ('ahb', b))
                dma('sp', hb_t[b][:, :, 128:256], hxf_v[:, :, 256 + 2048:256 + 2176], [('hxf', 5)], [('ahb', b)], ('ahb', b))
                dma('sp', tabs[b][:, 0, 0:256], c64e[:, 0:256], (), [('atab', b)], ('atab', b))
                dma('sp', tabs[b][:, 1, 0:256], s64e[:, 0:256], (), [('atab', b, 1)], ('atab', b))
                kv_block(b, 256, 2, 3, KsT[:, 256:512], [('KsT', 1)], Vs, 2, [('Vs', 1)],
                         (M_SWAP64, tabs[b][:, 0, 0:256], tabs[b][:, 1, 0:256], [('atab', b), ('atab', b, 1)]))
                for i in range(4):
                    b = blk % 2
                    blk += 1
                    dma('sp', hb_t[b][:], hxo_v[:, :, i * 512:(i + 1) * 512], [('hxo', i)], [('ahb', b)], ('ahb', b))
                    dma('sp', tabs[b][:, 0, :], c64e[:, 256 + i * 512:256 + (i + 1) * 512], (), [('atab', b)], ('atab', b))
                    dma('sp', tabs[b][:, 1, :], s64e[:, 256 + i * 512:256 + (i + 1) * 512], (), [('atab', b, 1)], ('atab', b))
                    kv_block(b, 512, 2, 3, KsT[:, 512 + i * 512:512 + (i + 1) * 512], [('KsT', 2 + i)], Vs, 4 + i * 4,
                             [('Vs', 2 + i)], (M_SWAP64, tabs[b][:, 0, :], tabs[b][:, 1, :], [('atab', b), ('atab', b, 1)]))
                allKa = [('KaT', bi) for bi in range(9)] + [('Va', bi) for bi in range(9)] + ['V0']
                allKs = [('KsT', i) for i in range(6)] + [('Vs', i) for i in range(6)] + ['V0']
                for qi, (q0, n, hv, h0, hk, isc) in enumerate(QB):
                    b = blk % 2
                    blk += 1
                    dma('sp', hb_t[b][:, :, 0:n], hv[:, :, h0:h0 + n], [hk], [('ahb', b)], ('ahb', b))
                    rope = None
                    if not isc:
                        dma('sp', tabs[b][:, 0, :], c64e[:, 256 + q0:256 + q0 + 512], (), [('atab', b)], ('atab', b))
                        dma('sp', tabs[b][:, 1, :], s64e[:, 256 + q0:256 + q0 + 512], (), [('atab', b, 1)], ('atab', b))
                        rope = (M_SWAP64, tabs[b][:, 0, 0:n], tabs[b][:, 1, 0:n], [('atab', b), ('atab', b, 1)])
                    gens = []
                    for wi, (QT, gcol, qn) in enumerate(((QaT, V_GAQ, 'QaT'), (QsT, V_GSQ, 'QsT'))):
                        for tl in range(2):
                            bk = (0, 1, 2, 5)[wi * 2 + tl]
                            pa, pak = PS[bk], ('ps', bk)
                            for kc in range(8):
                                mm(pa[:, 0:n], wq_[:, kc, wi, tl * 128:(tl + 1) * 128], hb_t[b][:, kc, 0:n], kc == 0, kc == 7,
                                   wqk + [('ahb', b)], [pak])

                            def post(QT=QT, qn=qn, wi=wi, tl=tl):
                                cp('pool', Qz[0:64, wi, tl, 0:n], QT[0:64, tl, 0:n], [(qn, tl), 'Qz0'], [('Qz', wi, tl)])
                                cp('pool', Qz[64:128, wi, 2 + tl, 0:n], QT[64:128, tl, 0:n], [(qn, tl), 'Qz0'],
                                   [('Qz', wi, 2 + tl)])
                            gens.append(norm_rope_g(nr, pa, pak, 128, n, M_ONES64, 1.0 / 64, gcol, QT[:, tl, 0:n], [(qn, tl)],
                                                    rope, post))
                    run_staged(gens)
                    for hq in range(4):
                        hkv, g = hq // 2, hq % 2
                        odd = hq % 2 == 1
                        r0 = hkv * 64
                        kts = range(2) if isc else range(34)
                        jobs = []
                        for kt in kts:
                            vl = Va[:, kt, hkv, 0:128] if odd else Va[:, kt, hkv, 64:129]
                            jobs.append((KaT[:, kt * 128:(kt + 1) * 128], Qz[:, 0, hq, 0:n], vl, None, 0, n,
                                         128 if odd else 65, allKa + [('Qz', 0, hq), 'Qz0']))
                        o_ps, o_key = attend(jobs, n, 0.125, pTs)
                        finish_head(o_ps, o_key, odd, None, 4 + hq // 2, q0, n, fscr)
                        jobs = []
                        for kt in range(2):
                            vl = Vs[:, kt, hkv, 0:128] if odd else Vs[:, kt, hkv, 64:129]
                            jobs.append((KsT[:, kt * 128:(kt + 1) * 128], Qz[:, 1, hq, 0:n], vl, None, 0, n,
                                         128 if odd else 65, allKs + [('Qz', 1, hq), 'Qz0']))
                        if not isc:
                            jj0 = q0 // 128
                            for m in range(jj0, jj0 + 6):
                                lo, hi = max(m - 2, jj0), min(m, jj0 + 3)
                                if m == 0:
                                    kt, mask = 2, masks[:, 384:512]
                                elif m == 17:
                                    kt, mask = 3, masks[:, 512:640]
                                else:
                                    kt = 4 + (m - 1)
                                    mask = masks[:, (lo - (m - 2)) * 128:(hi - (m - 2) + 1) * 128]
                                qlo, qhi = (lo - jj0) * 128, (hi - jj0 + 1) * 128
                                vl = Vs[:, kt, hkv, 0:128] if odd else Vs[:, kt, hkv, 64:129]
                                jobs.append((KsT[:, kt * 128:(kt + 1) * 128], Qz[:, 1, hq, qlo:qhi], vl, mask,
                                             qlo, qhi, 128 if odd else 65, allKs + [('Qz', 1, hq), 'Qz0']))
                        o_ps, o_key = attend(jobs, n, 0.125, pTs)
                        finish_head(o_ps, o_key, odd, hq, 2 + hq // 2, q0, n, fscr)
                finish_flush()
            S.barrier()

            if os.environ.get("KSTOP") == "pB2":
                return
            yk = [('yT', c) for c in (6, 7)] + [('yT', c, 'c') for c in (6, 7)] + [('yT', c, r) for c in range(6) for r in (0, 64)]
            for e_ in S.eng.values():
                e_.wait_ge(ccy, 2 * (l + 1))
            with ExitStack() as st:
                YA = sbuf(st, "YA", [128, 2, HALF], BF16)
                YB = sbuf(st, "YB", [128, 2, HALF], BF16)
                for j in range(2):
                    rows_ = slice(j * 128, (j + 1) * 128)
                    dma('sp', YA[:, j, 0:HTF - 256], ydg[0][rows_, 256:HTF], (), [('YA', j)], ('YA', j))
                    dma('sp', YA[:, j, HTF - 256:HALF], ydg[1][rows_, 0:HALF - (HTF - 256)], (), [('YA', j, 1)], ('YA', j))
                    dma('sp', YB[:, j, :], ydg[1][rows_, HTF - HALF:HTF], (), [('YB', j)], ('YB', j))
                    if update_ctx:
                        dma('sp', yT[:, 6 + j, HALF:NQ], ydg[0][rows_, 0:CTX], (), [('yT', 6 + j, 'c')], ('yTc', j))
                    ts('dve', YA[:, j, :], YA[:, j, :], vecs[:, V_SEL:V_SEL + 1], None, ALU.mult, None,
                       [('YA', j), ('YA', j, 1), 'vecs'], [('YA', j), ('YA', j, 1)])
                    stt(yT[:, 6 + j, 0:HALF], YB[:, j, :], vecs[:, V_SEL + 1:V_SEL + 2], YA[:, j, :], ALU.mult, ALU.add,
                        [('YA', j), ('YA', j, 1), ('YB', j), 'vecs'], [('yT', 6 + j)])
            S.barrier()
            with ExitStack() as st0:
              mixT = sbuf(st0, "mixT", [128, 8, NQ], BF16)
              with ExitStack() as st:
                hxq = sbuf(st, "hxq", [128, 8, NQ], BF16)
                wg = [sbuf(st, "wg%d" % i, [128, 8, 128], BF16) for i in range(2)]
                wmg = [sbuf(st, "wmg%d" % i, [128, 8, 4, 128], BF16) for i in range(2)]
                wbr = [sbuf(st, "wbr%d" % i, [128, 2, 4, 128], BF16) for i in range(2)]
                sg = [sbuf(st, "sg%d" % i, [128, 512], BF16) for i in range(2)]
                sig = [sbuf(st, "sig%d" % i, [128, 512], F32) for i in range(2)]
                acc = [sbuf(st, "acc%d" % i, [128, 512], F32) for i in range(2)]
                for i in range(4):
                    dma('sp', hxq[:, :, i * 512:(i + 1) * 512], hxo_v[:, :, i * 512:(i + 1) * 512], [('hxo', i)], ['hxq'], 'hxq')
                if update_ctx:
                    dma('sp', hxq[:, :, HALF:NQ], hxf_v[:, :, 0:CTX], [('hxf', 0)], ['hxq'], 'hxq')
                TB = [(i * 512, 512) for i in range(4)] + ([(HALF, 256)] if update_ctx else [])
                for gc in range(8):
                    b = gc % 2
                    dma('pool', wg[b][:], win_v[:, :, C_GATE + gc * 128:C_GATE + (gc + 1) * 128], (), [('wg', b)], ('wg', b))
                    for ti_, (t0, n) in enumerate(TB):
                        pa, pak = nps()
                        for kc in range(8):
                            mm(pa[:, 0:n], wg[b][:, kc, :], hxq[:, kc, t0:t0 + n], kc == 0, kc == 7, [('wg', b), 'hxq'], [pak])
                        sb_ = ti_ % 2
                        act(sg[sb_][:, 0:n], pa[:, 0:n], AF.Silu, [pak], [('sg', sb_)])
                        tt('dve', yT[:, gc, t0:t0 + n], yT[:, gc, t0:t0 + n], sg[sb_][:, 0:n],
                           ALU.mult, yk + [('sg', sb_)], [('yg', gc, ti_)])
                ygk = [('yg', gc, ti_) for gc in range(8) for ti_ in range(len(TB))]
                it = 0
                for f in range(8):
                    b = f % 2
                    for k in range(4):
                        dma('pool', wmg[b][:, :, k, :], win_v[:, :, C_MERGE + k * 1024 + f * 128:C_MERGE + k * 1024 + (f + 1) * 128],
                            (), [('wmg', b, k)], ('wmg', b))
                        dma('pool', wbr[b][:, :, k, :], wbr_d[k].rearrange("(c p) n -> p c n", p=128)[:, :, f * 128:(f + 1) * 128],
                            (), [('wbr', b, k)], ('wbr', b))
                    wk4 = [('wmg', b, k) for k in range(4)] + [('wbr', b, k) for k in range(4)]
                    for (t0, n) in TB:
                        ab = it % 2
                        it += 1
                        for k in range(4):
                            pz, pzk = nps((0, 1, 2))
                            for kc in range(8):
                                mm(pz[:, 0:n], wmg[b][:, kc, k, :], hxq[:, kc, t0:t0 + n], kc == 0, kc == 7, wk4 + ['hxq'], [pzk])
                            pj, pjk = nps((3, 4, 5))
                            for kc in range(2):
                                mm(pj[:, 0:n], wbr[b][:, kc, k, :], yT[:, 2 * k + kc, t0:t0 + n], kc == 0, kc == 1, wk4 + ygk,
                                   [pjk])
                            sb_ = k % 2
                            act(sig[sb_][:, 0:n], pz[:, 0:n], AF.Sigmoid, [pzk], [('sig', sb_)])
                            if k == 0:
                                tt('dve', acc[ab][:, 0:n], sig[sb_][:, 0:n], pj[:, 0:n], ALU.mult, [('sig', sb_), pjk],
                                   [('acc', ab)])
                            else:
                                tt('dve', sig[sb_][:, 0:n], sig[sb_][:, 0:n], pj[:, 0:n], ALU.mult, [('sig', sb_), pjk],
                                   [('sig', sb_)])
                                if k < 3:
                                    tt('dve', acc[ab][:, 0:n], acc[ab][:, 0:n], sig[sb_][:, 0:n], ALU.add,
                                       [('sig', sb_), ('acc', ab)], [('acc', ab)])
                                else:
                                    tt('dve', mixT[:, f, t0:t0 + n], acc[ab][:, 0:n], sig[sb_][:, 0:n], ALU.add,
                                       [('sig', sb_), ('acc', ab)], [('mixT', f)])
              S.barrier()
              with ExitStack() as st:
                wo = sbuf(st, "wo", [128, 8, D], BF16)
                xt = [sbuf(st, "oxt%d" % i, [128, D], F32) for i in range(2)]
                t1 = [sbuf(st, "ot%d" % i, [128, D], F32) for i in range(2)]
                dma('pool', wo[:], wout_d.rearrange("(c p) n -> p c n", p=128), (), ['wo'], 'wo')
                mk = [('mixT', f) for f in range(8)]
                tiles = [(0, xo, i * 128, xn_out, i * 128, i * 128) for i in range(16)]
                if update_ctx:
                    tiles += [(1, cx, i * 128, cn_out, i * 128, HALF + i * 128) for i in range(2)]
                outs = []
                for ti_, (mj, src, r0, dst, d0, t0) in enumerate(tiles):
                    b = ti_ % 2
                    dma('sp', xt[b][:], src[r0:r0 + 128, :], (), [('oxt', b)], ('oxt', b))
                    for nb in range(2):
                        po, pok = nps((0, 1, 2, 3))
                        for kc in range(8):
                            mm(po[:], mixT[:, kc, t0:t0 + 128], wo[:, kc, nb * 512:(nb + 1) * 512], kc == 0, kc == 7,
                               mk + ['wo'], [pok])
                        tt('dve', t1[b][:, nb * 512:(nb + 1) * 512], po[:], G[:, mj, nb * 512:(nb + 1) * 512], ALU.mult,
                           [pok, 'G'], [('ot', b, nb)])
                        tt('pool', t1[b][:, nb * 512:(nb + 1) * 512], t1[b][:, nb * 512:(nb + 1) * 512],
                           xt[b][:, nb * 512:(nb + 1) * 512], ALU.add, [('ot', b, nb), ('oxt', b)], [('ot', b, nb)])
                    outs.append(dma('pool', dst[d0:d0 + 128, :], t1[b][:], [('ot', b, 0), ('ot', b, 1)], [('out', ti_)], ('ost', b)))
                if final:
                    S.wait_all('sp', outs)
        emit_layer(0, True, xf_in, xo_in, cx_in, x1o, c1, False)
        if os.environ.get("KSTOP"):
            P.nops = S.nops
            return P
        ccs = es.enter_context(nc.semaphore("ccs"))
        CH = 256
        nch = HALF // CH
        x1g = x1f.rearrange("(k q) n -> k q n", k=nch)
        for k in range(nch):
            nc.gpsimd.collective_compute("AllGather", ALU.bypass, replica_groups=[[0, 1], [2, 3], [4, 5], [6, 7]],
                                         ins=[x1o[k * CH:(k + 1) * CH]], outs=[x1g[k]]).then_inc(ccs, 1)

        def wait_cc():
            nc.sync.wait_ge(ccs, nch)

        def x1_rows(t0):
            hh, rem = t0 // HALF, t0 % HALF
            r = (rem // CH) * (2 * CH) + hh * CH + rem % CH
            return x1f[r:r + 128, :]

        emit_layer(1, False, x1_rows, x1o, c1, y_out, None, True, after_p0=wait_cc)
        P.nops = S.nops
    return P


def _rope_tables():
    rows = np.repeat(np.arange(SEQ // 64, dtype=np.float32), 64)
    cols = np.tile(np.arange(64, dtype=np.float32), SEQ // 64)

    def ang(rot_dim):
        q = rot_dim // 4
        fr = (np.float32(10000.0) ** (-np.arange(q, dtype=np.float32) / np.float32(q))).astype(np.float32)
        return np.concatenate([rows[:, None] * fr, cols[:, None] * fr], axis=-1).astype(np.float32)

    a64 = ang(64)
    aM = ang(32)
    c64 = np.zeros((128, SEQ), np.float32)
    s64 = np.zeros((128, SEQ), np.float32)
    for p in range(128):
        d = p % 64
        c64[p] = np.cos(a64[:, d % 32])
        s64[p] = np.sin(a64[:, d % 32]) * (-1.0 if d < 32 else 1.0)
    cM = np.zeros((128, SEQ), np.float32)
    sM = np.zeros((128, SEQ), np.float32)
    cM[0:64] = 1.0
    for p in range(64, 96):
        d = p - 64
        cM[p] = np.cos(aM[:, d % 16])
        sM[p] = np.sin(aM[:, d % 16]) * (-1.0 if d < 16 else 1.0)
    return c64, s64, cM, sM


def _const_mats():
    m = np.zeros((7, 128, 128), np.float32)
    m[M_ID] = np.eye(128)
    m[M_ONES64, 0:64, 0:64] = 1
    m[M_ONES64, 64:128, 64:128] = 1
    m[M_ONES96, 0:96, 0:96] = 1
    m[M_ONES128] = 1
    for p in range(128):
        d = p % 64
        m[M_SWAP64, (p - d) + (d + 32) % 64, p] = 1
    for p in range(64, 96):
        d = p - 64
        m[M_SWAPM, 64 + (d + 16) % 32, p] = 1
    for d in range(32):
        m[M_EKR, d, 64 + d] = 1
    return m


def _masks(h):
    k = np.arange(128)[:, None]
    q = np.arange(128)[None, :]
    prev = (k >= q).astype(np.float32)
    nxt = (k <= q).astype(np.float32)
    ones = np.ones((128, 128), np.float32)
    lv = 1.0 if h == 1 else 0.0
    rv = 1.0 if h == 0 else 0.0
    return np.concatenate([nxt, ones, prev, prev * lv, nxt * rv], axis=1)


def _vecs(p, l, h):
    v = np.zeros((128, NV), np.float32)
    v[:, V_NORMW:V_NORMW + 8] = p['norm_w'][l].reshape(8, 128).T
    v[:, V_BSHIFT:V_BSHIFT + 8] = p['b_mod'][l][0:D].reshape(8, 128).T
    v[:, V_BSCALE:V_BSCALE + 8] = p['b_mod'][l][D:2 * D].reshape(8, 128).T
    v[:, V_GCQ:V_GCQ + 2] = p['mla_cq_norm'][l].reshape(2, 128).T
    v[:, V_GCKV] = p['mla_ckv_norm'][l]
    v[0:96, V_GMQ] = p['mla_q_norm'][l]
    v[0:96, V_GMK] = p['mla_k_norm'][l]
    v[:, V_GSQ] = np.tile(p['swa_q_norm'][l], 2)
    v[:, V_GSK] = np.tile(p['swa_k_norm'][l], 2)
    v[:, V_GAQ] = np.tile(p['axa_q_norm'][l], 2)
    v[:, V_GAK] = np.tile(p['axa_k_norm'][l], 2)
    v[:, V_SINK:V_SINK + 4] = p['swa_sink'][l][None, :]
    ch = slice(h * 128, (h + 1) * 128)
    v[:, V_CONVW:V_CONVW + 4] = p['lru_conv_w'][l][:, ch].T
    v[:, V_CONVB] = p['lru_conv_b'][l][ch]
    for d in range(2):
        v[:, V_BR + d * 2] = p['lru_b_r'][l][d, ch]
        v[:, V_BI + d * 2] = p['lru_b_i'][l][d, ch]
        v[:, V_LAM + d * 2] = p['lru_lambda'][l][d, ch]
    v[:, V_SEL] = 1.0 if h == 0 else 0.0
    v[:, V_SEL + 1] = 1.0 if h == 1 else 0.0
    return v


_PROGS = {}
_CONST = {}


def kernel(**inputs):
    p = {k: np.asarray(v, dtype=np.float32) for k, v in inputs.items()}
    if 'prog' not in _PROGS:
        _PROGS['prog'] = build()
    P = _PROGS['prog']
    if 'tabs' not in _CONST:
        _CONST['tabs'] = _rope_tables()
        _CONST['mats'] = _const_mats()
    c64, s64, cM, sM = _CONST['tabs']
    A = np.ascontiguousarray
    x, ctx = p['x'], p['ctx']
    lruw = np.stack([p['lru_w_r'], p['lru_w_i']], axis=1)
    rows = A(np.stack([np.broadcast_to(p['b_mod'][l][2 * D:3 * D][None, :], (128, D)) for l in range(2)], axis=0))
    shared = {"wmod": A(p['w_mod']), "win": A(p['w_in']), "wuq": A(p['mla_w_uq']), "wukv": A(p['mla_w_ukv']),
              "wbr": A(p['w_branch']), "wout": A(p['w_out']), "rows": rows, "mats": _CONST['mats'],
              "c64f": c64, "s64f": s64, "cMf": cM, "sMf": sM}
    in_maps = []
    for core in range(8):
        b, h = core // 2, core % 2
        own = slice(h * HALF, (h + 1) * HALF)
        ext = np.concatenate([np.arange(1920, 2048), np.arange(2048, 2176), np.arange(h * HALF, (h + 1) * HALF)])
        ccv = np.stack([p['c'][b].reshape(8, 128).T, p['c_ctx'].reshape(8, 128).T], axis=-1)
        m = dict(shared)
        m.update({
            "xf": A(x[b]), "xo": A(x[b, own]), "cx": A(ctx[b]), "cc": A(ccv.astype(np.float32)),
            "vecs": A(np.stack([_vecs(p, l, h) for l in range(2)], axis=0)), "masks": _masks(h),
            "lruw": A(lruw[:, :, :, 2 * h:2 * h + 2]),
            "wlru": A(p['w_in'][:, :, C_LRU + h * 128:C_LRU + (h + 1) * 128]),
            "c64e": A(c64[:, ext]), "s64e": A(s64[:, ext]), "cMo": A(cM[:, own]), "sMo": A(sM[:, own]),
        })
        in_maps.append(m)
    res = run_bass_kernel_spmd(P.nc, in_maps, core_ids=list(range(8)))
    out = np.empty_like(x)
    for core in range(8):
        b, h = core // 2, core % 2
        out[b, h * HALF:(h + 1) * HALF] = res.results[core]["xn"]
    return out.astype(np.float32)
```

```python
import os
import numpy as np
from contextlib import ExitStack
import concourse.bass as bass
import concourse.mybir as mybir
from concourse.bass_utils import run_bass_kernel_spmd

F32 = mybir.dt.float32
BF16 = mybir.dt.bfloat16
AF = mybir.ActivationFunctionType
ALU = mybir.AluOpType

D = 1024
SEQ = 4096
HALF = 2048
CTX = 256
TF = CTX + SEQ
EPS = 1e-6
C_CQ, C_CKV, C_KR, C_SQ, C_SK, C_SV, C_AQ, C_AK, C_AV, C_LRU, C_GATE, C_MERGE = (
    0, 256, 384, 416, 672, 800, 928, 1184, 1312, 1440, 1696, 2720)
IN_COLS = 6816
V_NORMW, V_BSHIFT, V_BSCALE, V_GCQ, V_GCKV, V_GMQ, V_GMK, V_GSQ, V_GSK, V_GAQ, V_GAK = (
    0, 8, 16, 24, 26, 27, 28, 29, 30, 31, 32)
V_SINK, V_CONVW, V_CONVB, V_BR, V_BI, V_LAM, V_SEL = 33, 37, 45, 47, 51, 55, 59
NV = 64
M_ID, M_ONES64, M_ONES96, M_ONES128, M_SWAP64, M_SWAPM, M_EKR = range(7)


class Sched:
    ENG = {'pe': 'tensor', 'act': 'scalar', 'dve': 'vector', 'pool': 'gpsimd', 'sp': 'sync'}
    ROLL = 30000

    def __init__(self, nc, es):
        self.nc = nc
        self.es = es
        self.eng = {k: getattr(nc, v) for k, v in self.ENG.items()}
        self.nsem = 0
        self.sem = {}
        self.cnt = {}
        self.allsems = []
        for k in self.eng:
            self._roll(k)
        self.last_w = {}
        self.readers = {}
        self.seen = {k: {} for k in self.eng}
        self.dma_sems = {}
        self.nops = 0

    def _alloc_sem(self):
        self.nsem += 1
        return self.es.enter_context(self.nc.semaphore("s%d" % self.nsem))

    def _roll(self, e):
        self.sem[e] = self._alloc_sem()
        self.cnt[e] = 0

    def op(self, e, fn, reads=(), writes=(), dma=None):
        deps = []
        for k in reads:
            t = self.last_w.get(k)
            if t is not None:
                deps.append((t, 0))
            if isinstance(k, tuple) and k[0] in ('ps', 'pt'):
                for t in self.readers.get(k, ()):
                    deps.append((t, 2))
        for k in writes:
            t = self.last_w.get(k)
            if t is not None:
                deps.append((t, 1))
            for t in self.readers.get(k, ()):
                deps.append((t, 2))
        eng = self.eng[e]
        need = {}
        for (sem, val, pe, is_dma), kind in deps:
            if pe == e and (not is_dma) and dma is None and (kind == 2 or (kind == 1 and e == 'pe')):
                continue
            sid = id(sem)
            if self.seen[e].get(sid, 0) >= val:
                continue
            if sid not in need or need[sid][1] < val:
                need[sid] = (sem, val)
        for sem, val in need.values():
            eng.wait_ge(sem, val)
            self.seen[e][id(sem)] = val
        ins = fn(eng)
        self.nops += 1
        if dma is not None:
            s = self.dma_sems.get(dma)
            if s is None:
                s = self.dma_sems[dma] = [self._alloc_sem(), 0]
            s[1] += 16
            ins.then_inc(s[0], 16)
            tok = (s[0], s[1], e, True)
        else:
            if self.cnt[e] >= self.ROLL:
                self._roll(e)
            self.cnt[e] += 1
            ins.then_inc(self.sem[e], 1)
            tok = (self.sem[e], self.cnt[e], e, False)
        for k in reads:
            self.readers.setdefault(k, []).append(tok)
        for k in writes:
            self.last_w[k] = tok
            self.readers[k] = []
        return tok

    def wait_all(self, e, toks):
        eng = self.eng[e]
        for (sem, val, pe, is_dma) in toks:
            if self.seen[e].get(id(sem), 0) >= val:
                continue
            eng.wait_ge(sem, val)
            self.seen[e][id(sem)] = val

    def barrier(self):
        toks = [(self.sem[p], self.cnt[p], p, False) for p in self.eng if self.cnt[p] > 0]
        toks += [(s[0], s[1], None, True) for s in self.dma_sems.values()]
        for e in self.eng:
            self.wait_all(e, toks)


class Prog:
    def __init__(self, update_ctx):
        self.update_ctx = update_ctx
        self.nc = bass.Bass("TRN2", target_bir_lowering=False, num_devices=8)
        self.din = {}
        self.dout = {}

    def inp(self, name, shape, dt=F32):
        t = self.nc.dram_tensor(name, list(shape), dt, kind="ExternalInput").ap()
        self.din[name] = t
        return t

    def outp(self, name, shape, dt=F32):
        t = self.nc.dram_tensor(name, list(shape), dt, kind="ExternalOutput").ap()
        self.dout[name] = t
        return t


def build():
    P = Prog(True)
    nc = P.nc
    xf_in = P.inp("xf", [SEQ, D])
    xo_in = P.inp("xo", [HALF, D])
    cx_in = P.inp("cx", [CTX, D])
    cc = P.inp("cc", [128, 8, 2])
    wmod_a = P.inp("wmod", [2, D, 3 * D])
    win_a = P.inp("win", [2, D, IN_COLS])
    wuq_a = P.inp("wuq", [2, 256, 384])
    wukv_a = P.inp("wukv", [2, 128, 512])
    wbr_a = P.inp("wbr", [2, 4, 256, D])
    wout_a = P.inp("wout", [2, D, D])
    lruw_a = P.inp("lruw", [2, 2, 2, 2, 64, 64])
    wlru_a = P.inp("wlru", [2, D, 128])
    vecs_a = P.inp("vecs", [2, 128, NV])
    rows_a = P.inp("rows", [2, 128, D])
    mats_d = P.inp("mats", [7, 128, 128])
    masks_d = P.inp("masks", [128, 640])
    c64f = P.inp("c64f", [128, SEQ])
    s64f = P.inp("s64f", [128, SEQ])
    c64e = P.inp("c64e", [128, 256 + HALF])
    s64e = P.inp("s64e", [128, 256 + HALF])
    cMf = P.inp("cMf", [128, SEQ])
    sMf = P.inp("sMf", [128, SEQ])
    cMo = P.inp("cMo", [128, HALF])
    sMo = P.inp("sMo", [128, HALF])
    y_out = P.outp("xn", [HALF, D])
    hxf = nc.dram_tensor("hxf", [D, TF], BF16, kind="Internal").ap()
    hxo = nc.dram_tensor("hxo", [D, HALF], BF16, kind="Internal").ap()
    x1o = nc.dram_tensor("x1o", [HALF, D], F32, kind="Internal").ap()
    c1 = nc.dram_tensor("c1", [CTX, D], F32, kind="Internal").ap()
    x1f = nc.dram_tensor("x1f", [SEQ, D], F32, kind="Internal").ap()
    HTF = TF // 2
    ydm = [nc.dram_tensor("ydm%d" % i, [128, HTF], BF16, kind="Internal").ap() for i in range(2)]
    ydg = [nc.dram_tensor("ydg%d" % i, [256, HTF], BF16, kind="Internal").ap() for i in range(2)]
    hxf_v = hxf.rearrange("(c p) t -> p c t", p=128)
    hxo_v = hxo.rearrange("(c p) t -> p c t", p=128)


    with ExitStack() as es:
        S = Sched(nc, es)

        uniq = [0]

        def sbuf(st, name, shape, dt):
            uniq[0] += 1
            return st.enter_context(nc.sbuf_tensor("sb%d_%s" % (uniq[0], name), list(shape), dt))

        def dma(q, out, in_, reads, writes, slot):
            return S.op(q, lambda e: e.dma_start(out=out, in_=in_), reads, writes, dma=slot)

        def mm(out, lhsT, rhs, start, stop, reads, writes):
            return S.op('pe', lambda e: e.matmul(out, lhsT=lhsT, rhs=rhs, start=start, stop=stop), reads, writes)

        def act(out, in_, func, reads, writes, **kw):
            return S.op('act', lambda e: e.activation(out=out, in_=in_, func=func, **kw), reads, writes)

        def tt(en, out, in0, in1, op, reads, writes):
            return S.op(en, lambda e: e.tensor_tensor(out=out, in0=in0, in1=in1, op=op), reads, writes)

        def ts(en, out, in0, s1, s2, op0, op1, reads, writes):
            if s2 is None:
                return S.op(en, lambda e: e.tensor_scalar(out=out, in0=in0, scalar1=s1, scalar2=None, op0=op0),
                            reads, writes)
            return S.op(en, lambda e: e.tensor_scalar(out=out, in0=in0, scalar1=s1, scalar2=s2, op0=op0, op1=op1),
                        reads, writes)

        def stt(out, in0, scalar, in1, op0, op1, reads, writes):
            return S.op('dve', lambda e: e.scalar_tensor_tensor(out=out, in0=in0, scalar=scalar, in1=in1,
                                                                op0=op0, op1=op1), reads, writes)

        def cp(en, out, in_, reads, writes):
            return S.op(en, lambda e: e.tensor_copy(out=out, in_=in_), reads, writes)

        def memset(en, ap, val, writes):
            return S.op(en, lambda e: e.memset(ap, val), (), writes)

        PT = [es.enter_context(nc.psum_tensor("pt%d" % i, [128, 2, 512], BF16)) for i in range(2)]
        PS = [es.enter_context(nc.psum_tensor("ps%d" % i, [128, 512], F32)) for i in range(6)]
        rot = [0]

        def nps(pool=(0, 1, 2)):
            i = pool[rot[0] % len(pool)]
            rot[0] += 1
            return PS[i], ('ps', i)

        matsf = sbuf(es, "matsf", [128, 7, 128], F32)
        mats = sbuf(es, "mats", [128, 7, 128], BF16)
        onesf = sbuf(es, "onesf", [128, 128], F32)
        masks = sbuf(es, "masks", [128, 640], BF16)
        masksf = sbuf(es, "masksf", [128, 640], F32)
        dma('sp', matsf[:], mats_d.rearrange("m p n -> p m n"), (), ['matsf'], 'matsf')
        dma('sp', masksf[:], masks_d, (), ['masksf'], 'masksf')
        cp('dve', mats[:], matsf[:], ['matsf'], ['mats'])
        cp('dve', masks[:], masksf[:], ['masksf'], ['masks'])
        memset('pool', onesf[:], 1.0, ['onesf'])

        def M(i, k=128, m=128):
            return mats[0:k, i, 0:m]

        ccy = es.enter_context(nc.semaphore("ccy"))

        def emit_layer(l, update_ctx, xf, xo, cx, xn_out, cn_out, final, after_p0=None):
            NQ = HALF + (CTX if update_ctx else 0)
            wmod_v = wmod_a[l].rearrange("(c p) n -> p c n", p=128)
            win_v = win_a[l].rearrange("(c p) n -> p c n", p=128)
            wuq_d, wukv_d, wbr_d, wout_d, lruw_d = wuq_a[l], wukv_a[l], wbr_a[l], wout_a[l], lruw_a[l]
            wlru_v = wlru_a[l].rearrange("(c p) n -> p c n", p=128)
            vecs_d, rows_d = vecs_a[l], rows_a[l]
            with ExitStack() as esl:
                emit_layer_body(l, update_ctx, xf, xo, cx, xn_out, cn_out, final, NQ, wmod_v, win_v, wuq_d, wukv_d,
                                wbr_d, wout_d, lruw_d, vecs_d, rows_d, esl, after_p0, wlru_v)
            S.barrier()

        def emit_layer_body(l, update_ctx, xf, xo, cx, xn_out, cn_out, final, NQ, wmod_v, win_v, wuq_d, wukv_d,
                            wbr_d, wout_d, lruw_d, vecs_d, rows_d, esl, after_p0, wlru_v):
            vecs = sbuf(esl, "vecs", [128, NV], F32)
            AB = sbuf(esl, "AB", [128, 2, 2, 8], F32)
            G = sbuf(esl, "G", [128, 2, D], F32)
            yT = sbuf(esl, "yT", [128, 8, NQ], BF16)
            esink = sbuf(esl, "esink", [128, 4], F32)
            cs = sbuf(esl, "cs", [128, 4], F32)
            nbr = sbuf(esl, "nbr", [128, 8], F32)
            dma('sp', vecs[:], vecs_d, (), ['vecs'], 'vecs')
            act(esink[:], vecs[:, V_SINK:V_SINK + 4], AF.Exp, ['vecs'], ['esink'])
            act(cs[:], vecs[:, V_LAM:V_LAM + 4], AF.Exp, ['vecs'], ['cs'], scale=-1.0)
            act(cs[:], cs[:], AF.Ln, ['cs'], ['cs'], bias=1.0, scale=1.0)
            ts('dve', cs[:], cs[:], -8.0, None, ALU.mult, None, ['cs'], ['cs'])
            ts('dve', nbr[:], vecs[:, V_BR:V_BR + 8], -1.0, None, ALU.mult, None, ['vecs'], ['nbr'])

            with ExitStack() as st:
                cct = sbuf(st, "cct", [128, 8, 2], F32)
                sct = sbuf(st, "sct", [128, 8, 2], F32)
                scb = sbuf(st, "scb", [128, 2, 8, 128], F32)
                wm = [sbuf(st, "wm%d" % i, [128, 8, 512], F32) for i in range(2)]
                modT = sbuf(st, "modT", [128, 2, 16], F32)
                rowsb = sbuf(st, "rowsb", [128, D], F32)
                dma('sp', cct[:], cc, (), ['cct'], 'cct')
                dma('sp', rowsb[:], rows_d, (), ['rowsb'], 'rowsb')
                act(sct[:], cct[:], AF.Exp, ['cct'], ['sct'], scale=-1.0)
                ts('dve', sct[:], sct[:], 1.0, None, ALU.add, None, ['sct'], ['sct'])
                S.op('dve', lambda e: e.reciprocal(out=sct[:], in_=sct[:]), ['sct'], ['sct'])
                tt('dve', sct[:], sct[:], cct[:], ALU.mult, ['sct', 'cct'], ['sct'])
                for j in range(2):
                    for kc in range(8):
                        cp('pool', scb[:, j, kc, :], sct[:, kc, j:j + 1].to_broadcast([128, 128]), ['sct'], ['scb'])
                pm, pmk = PS[5], ('ps', 5)
                for cb in range(6):
                    b = cb % 2
                    dma('sp', wm[b][:], wmod_v[:, :, cb * 512:(cb + 1) * 512], (), [('wm', b)], ('wm', b))
                    if cb < 4:
                        for fc in range(4):
                            f = cb * 4 + fc
                            for kc in range(8):
                                mm(pm[:, f * 2:f * 2 + 2], wm[b][:, kc, fc * 128:(fc + 1) * 128], sct[:, kc, :],
                                   kc == 0, kc == 7, [('wm', b), 'sct'], [pmk])
                    else:
                        nb = cb - 4
                        for j in range(2 if update_ctx else 1):
                            pg, pgk = nps()
                            for kc in range(8):
                                mm(pg[:], scb[:, j, kc, :], wm[b][:, kc, :], kc == 0, kc == 7, [('wm', b), 'scb'], [pgk])
                            tt('dve', G[:, j, nb * 512:(nb + 1) * 512], pg[:], rowsb[:, nb * 512:(nb + 1) * 512], ALU.add,
                               [pgk, 'rowsb'], ['G'])
                    if cb == 3:
                        pmv = pm[:, 0:32].rearrange("p (f j) -> p j f", j=2)
                        for j in range(2):
                            tt('dve', modT[:, j, :], pmv[:, j, :], vecs[:, V_BSHIFT:V_BSHIFT + 16], ALU.add,
                               [pmk, 'vecs'], ['modT'])
                            stt(AB[:, j, 0, :], modT[:, j, 8:16], 1.0, vecs[:, V_NORMW:V_NORMW + 8], ALU.add, ALU.mult,
                                ['modT', 'vecs'], ['AB'])
                            cp('dve', AB[:, j, 1, :], modT[:, j, 0:8], ['modT'], ['AB'])
            S.barrier()

            with ExitStack() as st:
                xt = [sbuf(st, "xt%d" % i, [128, 4, D], F32) for i in range(2)]
                sqj = sbuf(st, "sqj", [128, D], BF16)
                ssq = [sbuf(st, "ssq%d" % i, [128, 4], F32) for i in range(2)]
                rs = [sbuf(st, "rs%d" % i, [128, 4], F32) for i in range(2)]
                xnb = [sbuf(st, "xnb%d" % i, [128, 4, D], BF16) for i in range(2)]
                hblk = [sbuf(st, "hblk%d" % i, [128, 8, 512], BF16) for i in range(2)]
                groups = [(1, cx, 0, 256, hxf_v, 0, ('hxf', 0))]
                groups += [(0, xo, i * 512, 512, hxo_v, i * 512, ('hxo', i)) for i in range(4)]
                first_full = len(groups)
                groups += [(0, xf, i * 512, 512, hxf_v, 256 + i * 512, ('hxf', i + 1)) for i in range(8)]

                def stage_a(gi):
                    mj, src, r0, n, dst, c0, dkey = groups[gi]
                    b = gi % 2
                    ns = n // 128
                    for s_ in range(ns):
                        t0_ = r0 + s_ * 128
                        src_rows = src(t0_) if callable(src) else src[t0_:t0_ + 128, :]
                        dma('sp', xt[b][:, s_, :], src_rows, (), [('xt', b, s_)], ('xt', b))
                    xk = [('xt', b, s_) for s_ in range(ns)]
                    for s_ in range(ns):
                        act(sqj[:], xt[b][:, s_, :], AF.Square, xk, ['sqj', ('ssq', b)], accum_out=ssq[b][:, s_:s_ + 1])
                    act(rs[b][:, 0:ns], ssq[b][:, 0:ns], AF.Ln, [('ssq', b)], [('rs', b)], bias=EPS, scale=1.0 / D)
                    act(rs[b][:, 0:ns], rs[b][:, 0:ns], AF.Exp, [('rs', b)], [('rs', b)], scale=-0.5)
                    for s_ in range(ns):
                        ts('dve', xnb[b][:, s_, :], xt[b][:, s_, :], rs[b][:, s_:s_ + 1], None,
                           ALU.mult, None, xk + [('rs', b)], [('xnb', b, s_)])

                def stage_b(gi):
                    mj, src, r0, n, dst, c0, dkey = groups[gi]
                    b = gi % 2
                    ns = n // 128
                    for cp_ in range(4):
                        pv_, pk_ = PT[cp_ % 2], ('pt', cp_ % 2)
                        for s_ in range(ns):
                            for cc_ in range(2):
                                c = cp_ * 2 + cc_
                                S.op('pe', lambda e, c=c, cc_=cc_, s_=s_, pv_=pv_: e.transpose(
                                    out=pv_[:, cc_, s_ * 128:(s_ + 1) * 128], in_=xnb[b][:, s_, c * 128:(c + 1) * 128],
                                    identity=M(M_ID)), [('xnb', b, s_), 'mats'], [pk_])
                        for cc_ in range(2):
                            c = cp_ * 2 + cc_
                            o = hblk[b][:, c, 0:n]
                            if cp_ % 2 == 0:
                                ts('dve', o, pv_[:, cc_, 0:n], AB[:, mj, 0, c:c + 1], AB[:, mj, 1, c:c + 1], ALU.mult,
                                   ALU.add, [pk_, 'AB'], [('hblk', b, c)])
                            else:
                                act(o, pv_[:, cc_, 0:n], AF.Identity, [pk_, 'AB'], [('hblk', b, c)],
                                    scale=AB[:, mj, 0, c:c + 1], bias=AB[:, mj, 1, c:c + 1])
                    dma('pool', dst[:, :, c0:c0 + n], hblk[b][:, :, 0:n], [('hblk', b, c) for c in range(8)], [dkey],
                        ('hst', b))

                for gi in range(len(groups) + 1):
                    if gi == first_full and after_p0 is not None:
                        after_p0()
                    if gi < len(groups):
                        stage_a(gi)
                    if gi >= 1:
                        stage_b(gi - 1)
            S.barrier()

            if os.environ.get("KSTOP") == "p1":
                return
            FB = [(0, 256, ('hxf', 0), True)] + [(256 + i * 512, 512, ('hxf', i + 1), False) for i in range(8)]

            nr_ctr = [0]

            def norm_rope_g(st_tiles, src_ps, src_key, rows, n, ones_i, inv_d, gcol, out_ap, out_keys, rope=None, post=None):
                si = nr_ctr[0] % 4
                nr_ctr[0] += 1
                sq_t, rstd_t, kn_t, t1_t = st_tiles[si]
                ksq, krs, kkn, kt1 = ('nr_sq', si), ('nr_rstd', si), ('nr_kn', si), ('nr_t1', si)
                act(sq_t[0:rows, 0:n], src_ps[0:rows, 0:n], AF.Square, [src_key], [ksq])
                yield
                pq, pqk = nps((3, 4))
                mm(pq[0:rows, 0:n], M(ones_i, rows, rows), sq_t[0:rows, 0:n], True, True, [ksq, 'mats'], [pqk])
                act(rstd_t[0:rows, 0:n], pq[0:rows, 0:n], AF.Ln, [pqk], [krs], bias=EPS, scale=inv_d)
                act(rstd_t[0:rows, 0:n], rstd_t[0:rows, 0:n], AF.Exp, [krs], [krs], scale=-0.5)
                if rope is None:
                    stt(out_ap, src_ps[0:rows, 0:n], vecs[0:rows, gcol:gcol + 1], rstd_t[0:rows, 0:n], ALU.mult, ALU.mult,
                        [src_key, krs, 'vecs'], out_keys)
                    if post is not None:
                        post()
                    return
                swap_i, cos_ap, sin_ap, tab_keys = rope
                stt(kn_t[0:rows, 0:n], src_ps[0:rows, 0:n], vecs[0:rows, gcol:gcol + 1], rstd_t[0:rows, 0:n], ALU.mult,
                    ALU.mult, [src_key, krs, 'vecs'], [kkn])
                yield
                pw, pwk = nps((3, 4))
                mm(pw[0:rows, 0:n], M(swap_i, rows, rows), kn_t[0:rows, 0:n], True, True, [kkn, 'mats'], [pwk])
                tt('pool', t1_t[0:rows, 0:n], kn_t[0:rows, 0:n], cos_ap, ALU.mult, [kkn] + tab_keys, [kt1])
                tt('dve', rstd_t[0:rows, 0:n], pw[0:rows, 0:n], sin_ap, ALU.mult, [pwk] + tab_keys, [krs])
                yield
                tt('dve', out_ap, t1_t[0:rows, 0:n], rstd_t[0:rows, 0:n], ALU.add, [kt1, krs], out_keys)
                if post is not None:
                    post()

            def run_staged(gens):
                gens = list(gens)
                while gens:
                    nxt = []
                    for g_ in gens:
                        try:
                            next(g_)
                            nxt.append(g_)
                        except StopIteration:
                            pass
                    gens = nxt

            def norm_rope(*a, **k):
                run_staged([norm_rope_g(*a, **k)])

            def alloc_nr(st):
                return [(sbuf(st, "nr_sq", [128, 512], BF16), sbuf(st, "nr_rstd", [128, 512], F32),
                         sbuf(st, "nr_kn", [128, 512], BF16), sbuf(st, "nr_t1", [128, 512], F32)) for _ in range(4)]

            ZW = 4358
            with ExitStack() as st:
                wl = sbuf(st, "wl", [128, 8, 128], BF16)
                bdf = sbuf(st, "bdf", [128, 4, 128], F32)
                bd = sbuf(st, "bd", [128, 4, 128], BF16)
                hb_t = [sbuf(st, "lhb%d" % i, [128, 8, 512], BF16) for i in range(2)]
                zl = sbuf(st, "zl", [128, ZW], F32)
                ul = sbuf(st, "ul", [128, TF], F32)
                ub = sbuf(st, "ub", [128, TF], BF16)
                ltmp = [tuple(sbuf(st, "l%s%d" % (nm, i), [128, 512], F32) for nm in "AT") for i in range(2)]
                lh = [sbuf(st, "lH%d" % i, [128, 512], F32) for i in range(3)]
                lctr = [0]
                Rall = sbuf(st, "Rall", [128, TF], F32)
                Iall = sbuf(st, "Iall", [128, TF], F32)
                Yall = sbuf(st, "Yall", [128, TF], F32)
                Yb = sbuf(st, "Yb", [128, TF], BF16)
                dma('pool', wl[:], wlru_v, (), ['wl'], 'wl')
                memset('pool', bdf[:], 0.0, ['bdf'])
                for g in range(2):
                    for d in range(2):
                        for hh in range(2):
                            dma('sp', bdf[hh * 64:(hh + 1) * 64, g * 2 + d, hh * 64:(hh + 1) * 64],
                                lruw_d[g, d, hh], ['bdf'], [('bdfq', g, d, hh)], 'bdf')
                cp('dve', bd[:], bdf[:], ['bdf'] + [('bdfq', g, d, hh) for g in range(2) for d in range(2) for hh in range(2)],
                   ['bd'])
                c = 0
                memset('pool', zl[:], 0.0, ['zl'])
                for bi, (c0, n, hk, isc) in enumerate(FB):
                    b = bi % 2
                    dma('sp', hb_t[b][:, :, 0:n], hxf_v[:, :, c0:c0 + n], [hk], [('lhb', b)], ('lhb', b))
                    pz, pzk = nps()
                    for kc in range(8):
                        mm(pz[:, 0:n], wl[:, kc, :], hb_t[b][:, kc, 0:n], kc == 0, kc == 7, ['wl', ('lhb', b)], [pzk])
                    zc0 = 2 if isc else 261 + (c0 - 256)
                    act(zl[:, zc0:zc0 + n], pz[:, 0:n], AF.Copy, [pzk], ['zl'])
                for (u0, z0, n) in ((0, 2, 256), (256, 261, SEQ)):
                    for j in range(4):
                        wj = vecs[:, V_CONVW + c * 4 + j:V_CONVW + c * 4 + j + 1]
                        zin = zl[:, z0 + j - 2:z0 + j - 2 + n]
                        if j == 0:
                            ts('dve', ul[:, u0:u0 + n], zin, wj, vecs[:, V_CONVB + c:V_CONVB + c + 1], ALU.mult, ALU.add,
                               ['zl', 'vecs'], ['ul'])
                        else:
                            stt(ul[:, u0:u0 + n], zin, wj, ul[:, u0:u0 + n], ALU.mult, ALU.add, ['zl', 'vecs', 'ul'],
                                ['ul'])
                for d in range(2):
                    order = list(range(9)) if d == 0 else [0] + list(range(8, 0, -1))
                    for bi in order:
                        c0, n, hk, isc = FB[bi]
                        if d == 0:
                            cp('pool', ub[:, c0:c0 + n], ul[:, c0:c0 + n], ['ul'], [('ub', bi)])
                        pr, prk = nps((0, 1, 2))
                        mm(pr[:, 0:n], bd[:, 0 * 2 + d, :], ub[:, c0:c0 + n], True, True, ['bd', ('ub', bi)], [prk])
                        pi_, pik = nps((3, 4, 5))
                        mm(pi_[:, 0:n], bd[:, 1 * 2 + d, :], ub[:, c0:c0 + n], True, True, ['bd', ('ub', bi)], [pik])
                        bcr = V_BR + d * 2 + c
                        bci = V_BI + d * 2 + c
                        act(Rall[:, c0:c0 + n], pr[:, 0:n], AF.Sigmoid, [prk, 'vecs'], [('Rall', bi)], scale=1.0,
                            bias=vecs[:, bcr:bcr + 1])
                        act(Iall[:, c0:c0 + n], pi_[:, 0:n], AF.Sigmoid, [pik, 'vecs'], [('Iall', bi)], scale=1.0,
                            bias=vecs[:, bci:bci + 1])
                    prev_h = None
                    for oi, bi in enumerate(order):
                        c0, n, hk, isc = FB[bi]
                        j = lctr[0] % 2
                        j3 = lctr[0] % 3
                        lctr[0] += 1
                        At, Tt = ltmp[j]
                        Rt = Rall[:, c0:c0 + n]
                        It = Iall[:, c0:c0 + n]
                        kR, kI, kA, kT = ('Rall', bi), ('Iall', bi), ('lA', j), ('lT', j)
                        if d == 0:
                            Ht, kH = Yall[:, c0:c0 + n], ('Yall', bi)
                        else:
                            Ht, kH = lh[j3][:, 0:n], ('lH', j3)
                        act(At[:, 0:n], Rt, AF.Exp, [kR, 'cs'], [kA], scale=cs[:, d * 2 + c:d * 2 + c + 1])
                        act(Tt[:, 0:n], At[:, 0:n], AF.Square, [kA], [kT])
                        act(Tt[:, 0:n], Tt[:, 0:n], AF.Ln, [kT], [kT], scale=-1.0, bias=1.0)
                        act(Tt[:, 0:n], Tt[:, 0:n], AF.Exp, [kT], [kT], scale=0.5)
                        tt('pool', It, It, ul[:, c0:c0 + n], ALU.mult, [kI, 'ul'], [kI])
                        tt('dve', It, It, Tt[:, 0:n], ALU.mult, [kI, kT], [kI])
                        if d == 0:
                            o_, da_, db_ = Ht, At[:, 0:n], It
                            init = 0.0 if prev_h is None else prev_h[0][:, prev_h[1] - 1:prev_h[1]]
                        else:
                            o_, da_, db_ = Ht[:, ::-1], At[:, 0:n][:, ::-1], It[:, ::-1]
                            init = 0.0 if prev_h is None else prev_h[0][:, 0:1]
                        rk = [kA, kI] + ([] if prev_h is None else [prev_h[2]])
                        S.op('dve', lambda e, o_=o_, da_=da_, db_=db_, init=init: e.tensor_tensor_scan(
                            out=o_, data0=da_, data1=db_, initial=init, op0=ALU.mult, op1=ALU.add), rk, [kH])
                        prev_h = (Ht, n, kH)
                        if d == 1:
                            S.op('dve', lambda e, c0=c0, n=n, Ht=Ht: e.tensor_tensor(
                                out=Yb[:, c0:c0 + n], in0=Yall[:, c0:c0 + n], in1=Ht, op=ALU.add),
                                [kH, ('Yall', bi)], [('Yb', bi)])
                ybk = [('Yb', bi) for bi in range(9)]
                t0_ = dma('sp', ydm[0], Yb[:, 0:HTF], ybk, [('ydm', l, 0)], 'ydm')
                t1_ = dma('sp', ydm[1], Yb[:, HTF:TF], ybk, [('ydm', l, 1)], 'ydm')
                S.wait_all('pool', [t0_, t1_])
                for i in range(2):
                    nc.gpsimd.collective_compute("AllGather", ALU.bypass, replica_groups=[[0, 1], [2, 3], [4, 5], [6, 7]],
                                                 ins=[ydm[i]], outs=[ydg[i]]).then_inc(ccy, 1)
            S.barrier()

            if os.environ.get("KSTOP") == "pA":
                return
            QB = [(i * 512, 512, hxo_v, i * 512, ('hxo', i), False) for i in range(4)]
            if update_ctx:
                QB.append((HALF, 256, hxf_v, 0, ('hxf', 0), True))

            fin_ctr = [0]
            fin_pend = []

            def finish_a(o_ps, o_key, odd, sink_col, ych, q0, n, scrs):
                si = fin_ctr[0] % 2
                fin_ctr[0] += 1
                osb, rden = scrs[si]
                ko, kr_ = ('osb', si), ('rden', si)
                if not odd:
                    drow, r0, r1 = 64, 0, 64
                    cp('dve', osb[0:65, 0:n], o_ps[0:65, 0:n], [o_key], [ko])
                else:
                    drow, r0, r1 = 0, 64, 128
                    cp('dve', osb[:, 0:n], o_ps[:, 0:n], [o_key], [ko])
                if sink_col is not None:
                    ts('dve', rden[drow:drow + 1, 0:n], osb[drow:drow + 1, 0:n], esink[drow:drow + 1, sink_col:sink_col + 1],
                       None, ALU.add, None, [ko, 'esink'], [kr_])
                    S.op('dve', lambda e: e.reciprocal(out=rden[drow:drow + 1, 0:n], in_=rden[drow:drow + 1, 0:n]),
                         [kr_], [kr_])
                else:
                    S.op('dve', lambda e: e.reciprocal(out=rden[drow:drow + 1, 0:n], in_=osb[drow:drow + 1, 0:n]),
                         [ko], [kr_])
                fin_pend.append((osb, rden, ko, kr_, drow, r0, r1, ych, q0, n))

            def finish_b():
                osb, rden, ko, kr_, drow, r0, r1, ych, q0, n = fin_pend.pop(0)
                pb, pbk = PS[5], ('ps', 5)
                mm(pb[0:r1, 0:n], onesf[drow:drow + 1, 0:r1], rden[drow:drow + 1, 0:n], True, True, [kr_, 'onesf'], [pbk])
                tt('dve', yT[r0:r1, ych, q0:q0 + n], osb[r0:r1, 0:n], pb[r0:r1, 0:n], ALU.mult, [ko, pbk], [('yT', ych, r0)])

            def finish_head(o_ps, o_key, odd, sink_col, ych, q0, n, scrs):
                finish_a(o_ps, o_key, odd, sink_col, ych, q0, n, scrs)
                while len(fin_pend) > 1:
                    finish_b()

            def finish_flush():
                while fin_pend:
                    finish_b()

            def attend(jobs, n, scale, scr_p):
                o_ps, o_key = nps((3, 4))
                pend = []
                first = [True]

                left = [len(jobs)]

                def flush_one():
                    (pt_t, ptk, vl, qlo, qhi, mrows, rd) = pend.pop(0)
                    left[0] -= 1
                    mm(o_ps[0:mrows, qlo:qhi], vl, pt_t[:, qlo:qhi], first[0], left[0] == 0, [ptk] + rd, [o_key])
                    first[0] = False

                for ji, (kl, rq, vl, mask, qlo, qhi, mrows, rd) in enumerate(jobs):
                    sp_t, spk = nps((0, 1, 2))
                    mm(sp_t[:, qlo:qhi], kl, rq, True, True, rd, [spk])
                    pi = ji % len(scr_p)
                    pt_t, ptk = scr_p[pi], ('pT', pi)
                    act(pt_t[:, qlo:qhi], sp_t[:, qlo:qhi], AF.Exp, [spk], [ptk], scale=scale)
                    if mask is not None:
                        tt('pool', pt_t[:, qlo:qhi], pt_t[:, qlo:qhi], mask, ALU.mult, [ptk, 'masks'], [ptk])
                    pend.append((pt_t, ptk, vl, qlo, qhi, mrows, rd))
                    if len(pend) > 2:
                        flush_one()
                while pend:
                    flush_one()
                return o_ps, o_key

            with ExitStack() as st:
                nr = alloc_nr(st)
                KmT = sbuf(st, "KmT", [128, 4, TF], BF16)
                Vm = sbuf(st, "Vm", [128, 34, 386], BF16)
                wkv1 = sbuf(st, "wkv1", [128, 8, 160], BF16)
                wkn = sbuf(st, "wkn", [128, 4, 96], BF16)
                wv = sbuf(st, "wv", [128, 4, 64], BF16)
                wcq = sbuf(st, "wcq", [128, 8, 256], BF16)
                wuq = sbuf(st, "wuq", [128, 2, 384], BF16)
                hb_t = [sbuf(st, "mhb%d" % i, [128, 8, 512], BF16) for i in range(2)]
                ckvn2 = [sbuf(st, "ckvn%d" % i, [128, 512], BF16) for i in range(2)]
                krT2 = [sbuf(st, "krT%d" % i, [32, 512], BF16) for i in range(2)]
                tabs = [sbuf(st, "mtab%d" % i, [128, 2, 512], F32) for i in range(2)]
                cqn = sbuf(st, "cqn", [128, 2, 512], BF16)
                QmT = sbuf(st, "QmT", [128, 4, 512], BF16)
                pTs = [sbuf(st, "mpT%d" % i, [128, 512], BF16) for i in range(4)]
                fscr = [(sbuf(st, "mosb", [128, 512], F32), sbuf(st, "mrden", [128, 512], F32)) for _ in range(2)]
                dma('pool', wkv1[:], win_v[:, :, C_CKV:C_CKV + 160], (), ['wkv1'], 'wkv1')
                memset('pool', wkn[:], 0.0, ['wkn'])
                wukv_h = wukv_d.rearrange("p (h n) -> p h n", h=4)
                dma('pool', wkn[:, :, 0:64], wukv_h[:, :, 0:64], ['wkn'], ['wkn2'], 'wkn')
                dma('pool', wv[:], wukv_h[:, :, 64:128], (), ['wv'], 'wv')
                dma('pool', wcq[:], win_v[:, :, C_CQ:C_CQ + 256], (), ['wcq'], 'wcq')
                dma('pool', wuq[:], wuq_d.rearrange("(c p) n -> p c n", p=128), (), ['wuq'], 'wuq')
                memset('pool', Vm[:], 0.0, ['Vm0'])
                for oc in (64, 65, 257, 258):
                    memset('pool', Vm[:, :, oc:oc + 1], 1.0, ['Vm0'])
                VMV = {0: (0, 65), 1: (65, 193), 2: (193, 258), 3: (258, 386)}
                def b1_front(bi):
                    c0, n, hk, isc = FB[bi]
                    b = bi % 2
                    dma('sp', hb_t[b][:, :, 0:n], hxf_v[:, :, c0:c0 + n], [hk], [('mhb', b)], ('mhb', b))
                    if not isc:
                        dma('sp', tabs[b][0:96, 0, :], cMf[0:96, c0 - 256:c0 - 256 + 512], (), [('mtab', b)], ('mtab', b))
                        dma('sp', tabs[b][0:96, 1, :], sMf[0:96, c0 - 256:c0 - 256 + 512], (), [('mtab', b, 1)], ('mtab', b))
                    pa, pak = nps()
                    for kc in range(8):
                        mm(pa[:, 0:n], wkv1[:, kc, 0:128], hb_t[b][:, kc, 0:n], kc == 0, kc == 7, ['wkv1', ('mhb', b)], [pak])
                    pk, pkk = nps()
                    for kc in range(8):
                        mm(pk[0:32, 0:n], wkv1[:, kc, 128:160], hb_t[b][:, kc, 0:n], kc == 0, kc == 7,
                           ['wkv1', ('mhb', b)], [pkk])
                    norm_rope(nr, pa, pak, 128, n, M_ONES128, 1.0 / 128, V_GCKV, ckvn2[b][:, 0:n], [('ckvn', b)])
                    cp('dve', krT2[b][:, 0:n], pk[0:32, 0:n], [pkk], [('krT', b)])

                def b1_back(bi):
                    c0, n, hk, isc = FB[bi]
                    b = bi % 2
                    ckvn, krT = ckvn2[b], krT2[b]
                    gens = []
                    for h in range(4):
                        pd, pdk = PS[(0, 1, 2, 5)[h]], ('ps', (0, 1, 2, 5)[h])
                        mm(pd[0:96, 0:n], wkn[:, h, :], ckvn[:, 0:n], True, False, ['wkn', 'wkn2', ('ckvn', b)], [pdk])
                        mm(pd[0:96, 0:n], M(M_EKR, 32, 96), krT[:, 0:n], False, True, [('krT', b), 'mats'], [pdk])
                        rope = None if isc else (M_SWAPM, tabs[b][0:96, 0, 0:n], tabs[b][0:96, 1, 0:n],
                                                 [('mtab', b), ('mtab', b, 1)])
                        gens.append(norm_rope_g(nr, pd, pdk, 96, n, M_ONES96, 1.0 / 96, V_GMK, KmT[0:96, h, c0:c0 + n],
                                                [('KmT', bi)], rope))
                    run_staged(gens)
                    for s in range(n // 128):
                        kt = c0 // 128 + s
                        pvv, pvk = nps()
                        mm(pvv[:, 0:256], ckvn[:, s * 128:(s + 1) * 128], wv[:].rearrange("p h n -> p (h n)"), True, True,
                           [('ckvn', b), 'wv'], [pvk])
                        vsrc = pvv[:, 0:256].rearrange("p (a b n) -> p a b n", a=2, b=2)
                        vdst = Vm[:, kt, :].rearrange("p (a c) -> p a c", a=2)
                        cp('dve', vdst[:, :, 0:64], vsrc[:, :, 0, :], [pvk, 'Vm0'], [('Vm', bi)])
                        cp('dve', vdst[:, :, 129:193], vsrc[:, :, 1, :], [pvk, 'Vm0'], [('Vm', bi, 1)])

                for bi in range(len(FB) + 1):
                    if bi < len(FB):
                        b1_front(bi)
                    if bi >= 1:
                        b1_back(bi - 1)
                allK = [('KmT', bi) for bi in range(9)] + [('Vm', bi) for bi in range(9)] + [('Vm', bi, 1) for bi in range(9)] + ['Vm0']
                for qi, (q0, n, hv, h0, hk, isc) in enumerate(QB):
                    b = qi % 2
                    dma('sp', hb_t[b][:, :, 0:n], hv[:, :, h0:h0 + n], [hk], [('mhb', b)], ('mhb', b))
                    if not isc:
                        dma('sp', tabs[b][0:96, 0, :], cMo[0:96, q0:q0 + 512], (), [('mtab', b)], ('mtab', b))
                        dma('sp', tabs[b][0:96, 1, :], sMo[0:96, q0:q0 + 512], (), [('mtab', b, 1)], ('mtab', b))
                    pcs = []
                    for c in range(2):
                        pc_, pck = nps((0, 1))
                        for kc in range(8):
                            mm(pc_[:, 0:n], wcq[:, kc, c * 128:(c + 1) * 128], hb_t[b][:, kc, 0:n], kc == 0, kc == 7,
                               ['wcq', ('mhb', b)], [pck])
                        pcs.append((pc_, pck))
                    sq_t, rstd_t, kn_t, t1_t = nr[0]
                    pq, pqk = PS[2], ('ps', 2)
                    for c in range(2):
                        act(sq_t[:, 0:n], pcs[c][0][:, 0:n], AF.Square, [pcs[c][1]], [('nr_sq', 0)])
                        mm(pq[:, 0:n], M(M_ONES128), sq_t[:, 0:n], c == 0, c == 1, [('nr_sq', 0), 'mats'], [pqk])
                    act(rstd_t[:, 0:n], pq[:, 0:n], AF.Ln, [pqk], [('nr_rstd', 0)], bias=EPS, scale=1.0 / 256)
                    act(rstd_t[:, 0:n], rstd_t[:, 0:n], AF.Exp, [('nr_rstd', 0)], [('nr_rstd', 0)], scale=-0.5)
                    for c in range(2):
                        stt(cqn[:, c, 0:n], pcs[c][0][:, 0:n], vecs[:, V_GCQ + c:V_GCQ + c + 1], rstd_t[:, 0:n], ALU.mult,
                            ALU.mult, [pcs[c][1], ('nr_rstd', 0), 'vecs'], ['cqn'])
                    gens = []
                    for h in range(4):
                        pd, pdk = PS[(0, 1, 2, 5)[h]], ('ps', (0, 1, 2, 5)[h])
                        for c in range(2):
                            mm(pd[0:96, 0:n], wuq[:, c, h * 96:(h + 1) * 96], cqn[:, c, 0:n], c == 0, c == 1, ['wuq', 'cqn'],
                               [pdk])
                        rope = None if isc else (M_SWAPM, tabs[b][0:96, 0, 0:n], tabs[b][0:96, 1, 0:n],
                                                 [('mtab', b), ('mtab', b, 1)])
                        gens.append(norm_rope_g(nr, pd, pdk, 96, n, M_ONES96, 1.0 / 96, V_GMQ, QmT[0:96, h, 0:n],
                                                [('QmT', h)], rope))
                    run_staged(gens)
                    kts = range(2) if isc else range(34)
                    for h in range(4):
                        odd = h % 2 == 1
                        jobs = []
                        for kt in kts:
                            vl = Vm[:, kt, VMV[h][0]:VMV[h][1]]
                            jobs.append((KmT[0:96, h, kt * 128:(kt + 1) * 128], QmT[0:96, h, 0:n], vl, None, 0, n,
                                         128 if odd else 65, allK + [('QmT', h)]))
                        o_ps, o_key = attend(jobs, n, 96.0 ** -0.5, pTs)
                        finish_head(o_ps, o_key, odd, None, h // 2, q0, n, fscr)
                finish_flush()
            S.barrier()

            if os.environ.get("KSTOP") == "pB1":
                return
            with ExitStack() as st:
                nr = alloc_nr(st)
                KaT = sbuf(st, "KaT", [128, TF], BF16)
                Va = sbuf(st, "Va", [128, 34, 2, 129], BF16)
                NSK = CTX + 256 + HALF
                KsT = sbuf(st, "KsT", [128, NSK], BF16)
                Vs = sbuf(st, "Vs", [128, 20, 2, 129], BF16)
                wk_ = sbuf(st, "wk_", [128, 8, 4, 128], BF16)
                wq_ = sbuf(st, "wq_", [128, 8, 2, 256], BF16)
                hb_t = [sbuf(st, "ahb%d" % i, [128, 8, 512], BF16) for i in range(2)]
                tabs = [sbuf(st, "atab%d" % i, [128, 2, 512], F32) for i in range(2)]
                QaT = sbuf(st, "QaT", [128, 2, 512], BF16)
                QsT = sbuf(st, "QsT", [128, 2, 512], BF16)
                Qz = sbuf(st, "Qz", [128, 2, 4, 512], BF16)
                memset('pool', Qz[:], 0.0, ['Qz0'])
                pTs = [sbuf(st, "apT%d" % i, [128, 512], BF16) for i in range(4)]
                fscr = [(sbuf(st, "aosb", [128, 512], F32), sbuf(st, "arden", [128, 512], F32)) for _ in range(2)]
                for i, cb in enumerate((C_AK, C_AV, C_SK, C_SV)):
                    dma('pool', wk_[:, :, i, :], win_v[:, :, cb:cb + 128], (), [('wk_', i)], 'wk_')
                for i, cb in enumerate((C_AQ, C_SQ)):
                    for pos, hq in enumerate((0, 2, 1, 3)):
                        dma('pool', wq_[:, :, i, pos * 64:(pos + 1) * 64], win_v[:, :, cb + hq * 64:cb + (hq + 1) * 64], (),
                            [('wq_', i, pos)], 'wq_')
                wkk = [('wk_', i) for i in range(4)]
                wqk = [('wq_', i, pos) for i in range(2) for pos in range(4)]
                for V_ in (Va, Vs):
                    memset('pool', V_[:], 0.0, ['V0'])
                    memset('pool', V_[:, :, :, 0:1], 1.0, ['V0'])
                    memset('pool', V_[:, :, :, 128:129], 1.0, ['V0'])

                def kv_block(b, n, wi_k, wi_v, KT_ap, kkeys, V_t, kt0, vkeys, rope):
                    pa, pak = nps()
                    for kc in range(8):
                        mm(pa[:, 0:n], wk_[:, kc, wi_k, :], hb_t[b][:, kc, 0:n], kc == 0, kc == 7, wkk + [('ahb', b)], [pak])
                    pvs = []
                    for s in range(n // 128):
                        pvv, pvk = nps((1, 2, 5) if s % 2 == 0 else (0, 2, 5))
                        if pvk == pak:
                            pvv, pvk = nps((1, 2, 5) if s % 2 == 0 else (0, 2, 5))
                        for kc in range(8):
                            mm(pvv[:, 0:128], hb_t[b][:, kc, s * 128:(s + 1) * 128], wk_[:, kc, wi_v, :], kc == 0, kc == 7,
                               wkk + [('ahb', b)], [pvk])
                        cp('dve', V_t[:, kt0 + s, :, 64:128], pvv[:, 0:128].rearrange("p (h n) -> p h n", h=2),
                           [pvk, 'V0'], vkeys)
                    norm_rope(nr, pa, pak, 128, n, M_ONES64, 1.0 / 64, V_GAK if wi_k == 0 else V_GSK, KT_ap, kkeys, rope)

                blk = 0
                for bi, (c0, n, hk, isc) in enumerate(FB):
                    b = blk % 2
                    blk += 1
                    dma('sp', hb_t[b][:, :, 0:n], hxf_v[:, :, c0:c0 + n], [hk], [('ahb', b)], ('ahb', b))
                    rope = None
                    if not isc:
                        dma('sp', tabs[b][:, 0, :], c64f[:, c0 - 256:c0 - 256 + 512], (), [('atab', b)], ('atab', b))
                        dma('sp', tabs[b][:, 1, :], s64f[:, c0 - 256:c0 - 256 + 512], (), [('atab', b, 1)], ('atab', b))
                        rope = (M_SWAP64, tabs[b][:, 0, 0:n], tabs[b][:, 1, 0:n], [('atab', b), ('atab', b, 1)])
                    kv_block(b, n, 0, 1, KaT[:, c0:c0 + n], [('KaT', bi)], Va, c0 // 128, [('Va', bi)], rope)
                    if isc:
                        kv_block(b, n, 2, 3, KsT[:, 0:256], [('KsT', 0)], Vs, 0, [('Vs', 0)], None)
                b = blk % 2
                blk += 1
                dma('sp', hb_t[b][:, :, 0:128], hxf_v[:, :, 256 + 1920:256 + 2048], [('hxf', 4)], [('ahb', b)], ('ahb', b))
                dma('sp', hb_t[b][:, :, 128:256], hxf_v[:, :, 256 + 2048:256 + 2176], [('hxf', 5)], [('ahb', b)], ('ahb', b))
                dma('sp', tabs[b][:, 0, 0:256], c64e[:, 0:256], (), [('atab', b)], ('atab', b))
                dma('sp', tabs[b][:, 1, 0:256], s64e[:, 0:256], (), [('atab', b, 1)], ('atab', b))
                kv_block(b, 256, 2, 3, KsT[:, 256:512], [('KsT', 1)], Vs, 2, [('Vs', 1)],
                         (M_SWAP64, tabs[b][:, 0, 0:256], tabs[b][:, 1, 0:256], [('atab', b), ('atab', b, 1)]))
                for i in range(4):
                    b = blk % 2
                    blk += 1
                    dma('sp', hb_t[b][:], hxo_v[:, :, i * 512:(i + 1) * 512], [('hxo', i)], [('ahb', b)], ('ahb', b))
                    dma('sp', tabs[b][:, 0, :], c64e[:, 256 + i * 512:256 + (i + 1) * 512], (), [('atab', b)], ('atab', b))
                    dma('sp', tabs[b][:, 1, :], s64e[:, 256 + i * 512:256 + (i + 1) * 512], (), [('atab', b, 1)], ('atab', b))
                    kv_block(b, 512, 2, 3, KsT[:, 512 + i * 512:512 + (i + 1) * 512], [('KsT', 2 + i)], Vs, 4 + i * 4,
                             [('Vs', 2 + i)], (M_SWAP64, tabs[b][:, 0, :], tabs[b][:, 1, :], [('atab', b), ('atab', b, 1)]))
                allKa = [('KaT', bi) for bi in range(9)] + [('Va', bi) for bi in range(9)] + ['V0']
                allKs = [('KsT', i) for i in range(6)] + [('Vs', i) for i in range(6)] + ['V0']
                for qi, (q0, n, hv, h0, hk, isc) in enumerate(QB):
                    b = blk % 2
                    blk += 1
                    dma('sp', hb_t[b][:, :, 0:n], hv[:, :, h0:h0 + n], [hk], [('ahb', b)], ('ahb', b))
                    rope = None
                    if not isc:
                        dma('sp', tabs[b][:, 0, :], c64e[:, 256 + q0:256 + q0 + 512], (), [('atab', b)], ('atab', b))
                        dma('sp', tabs[b][:, 1, :], s64e[:, 256 + q0:256 + q0 + 512], (), [('atab', b, 1)], ('atab', b))
                        rope = (M_SWAP64, tabs[b][:, 0, 0:n], tabs[b][:, 1, 0:n], [('atab', b), ('atab', b, 1)])
                    gens = []
                    for wi, (QT, gcol, qn) in enumerate(((QaT, V_GAQ, 'QaT'), (QsT, V_GSQ, 'QsT'))):
                        for tl in range(2):
                            bk = (0, 1, 2, 5)[wi * 2 + tl]
                            pa, pak = PS[bk], ('ps', bk)
                            for kc in range(8):
                                mm(pa[:, 0:n], wq_[:, kc, wi, tl * 128:(tl + 1) * 128], hb_t[b][:, kc, 0:n], kc == 0, kc == 7,
                                   wqk + [('ahb', b)], [pak])

                            def post(QT=QT, qn=qn, wi=wi, tl=tl):
                                cp('pool', Qz[0:64, wi, tl, 0:n], QT[0:64, tl, 0:n], [(qn, tl), 'Qz0'], [('Qz', wi, tl)])
                                cp('pool', Qz[64:128, wi, 2 + tl, 0:n], QT[64:128, tl, 0:n], [(qn, tl), 'Qz0'],
                                   [('Qz', wi, 2 + tl)])
                            gens.append(norm_rope_g(nr, pa, pak, 128, n, M_ONES64, 1.0 / 64, gcol, QT[:, tl, 0:n], [(qn, tl)],
                                                    rope, post))
                    run_staged(gens)
                    for hq in range(4):
                        hkv, g = hq // 2, hq % 2
                        odd = hq % 2 == 1
                        r0 = hkv * 64
                        kts = range(2) if isc else range(34)
                        jobs = []
                        for kt in kts:
                            vl = Va[:, kt, hkv, 0:128] if odd else Va[:, kt, hkv, 64:129]
                            jobs.append((KaT[:, kt * 128:(kt + 1) * 128], Qz[:, 0, hq, 0:n], vl, None, 0, n,
                                         128 if odd else 65, allKa + [('Qz', 0, hq), 'Qz0']))
                        o_ps, o_key = attend(jobs, n, 0.125, pTs)
                        finish_head(o_ps, o_key, odd, None, 4 + hq // 2, q0, n, fscr)
                        jobs = []
                        for kt in range(2):
                            vl = Vs[:, kt, hkv, 0:128] if odd else Vs[:, kt, hkv, 64:129]
                            jobs.append((KsT[:, kt * 128:(kt + 1) * 128], Qz[:, 1, hq, 0:n], vl, None, 0, n,
                                         128 if odd else 65, allKs + [('Qz', 1, hq), 'Qz0']))
                        if not isc:
                            jj0 = q0 // 128
                            for m in range(jj0, jj0 + 6):
                                lo, hi = max(m - 2, jj0), min(m, jj0 + 3)
                                if m == 0:
                                    kt, mask = 2, masks[:, 384:512]
                                elif m == 17:
                                    kt, mask = 3, masks[:, 512:640]
                                else:
                                    kt = 4 + (m - 1)
                                    mask = masks[:, (lo - (m - 2)) * 128:(hi - (m - 2) + 1) * 128]
                                qlo, qhi = (lo - jj0) * 128, (hi - jj0 + 1) * 128
                                vl = Vs[:, kt, hkv, 0:128] if odd else Vs[:, kt, hkv, 64:129]
                                jobs.append((KsT[:, kt * 128:(kt + 1) * 128], Qz[:, 1, hq, qlo:qhi], vl, mask,
                                             qlo, qhi, 128 if odd else 65, allKs + [('Qz', 1, hq), 'Qz0']))
                        o_ps, o_key = attend(jobs, n, 0.125, pTs)
                        finish_head(o_ps, o_key, odd, hq, 2 + hq // 2, q0, n, fscr)
                finish_flush()
            S.barrier()

            if os.environ.get("KSTOP") == "pB2":
                return
            yk = [('yT', c) for c in (6, 7)] + [('yT', c, 'c') for c in (6, 7)] + [('yT', c, r) for c in range(6) for r in (0, 64)]
            for e_ in S.eng.values():
                e_.wait_ge(ccy, 2 * (l + 1))
            with ExitStack() as st:
                YA = sbuf(st, "YA", [128, 2, HALF], BF16)
                YB = sbuf(st, "YB", [128, 2, HALF], BF16)
                for j in range(2):
                    rows_ = slice(j * 128, (j + 1) * 128)
                    dma('sp', YA[:, j, 0:HTF - 256], ydg[0][rows_, 256:HTF], (), [('YA', j)], ('YA', j))
                    dma('sp', YA[:, j, HTF - 256:HALF], ydg[1][rows_, 0:HALF - (HTF - 256)], (), [('YA', j, 1)], ('YA', j))
                    dma('sp', YB[:, j, :], ydg[1][rows_, HTF - HALF:HTF], (), [('YB', j)], ('YB', j))
                    if update_ctx:
                        dma('sp', yT[:, 6 + j, HALF:NQ], ydg[0][rows_, 0:CTX], (), [('yT', 6 + j, 'c')], ('yTc', j))
                    ts('dve', YA[:, j, :], YA[:, j, :], vecs[:, V_SEL:V_SEL + 1], None, ALU.mult, None,
                       [('YA', j), ('YA', j, 1), 'vecs'], [('YA', j), ('YA', j, 1)])
                    stt(yT[:, 6 + j, 0:HALF], YB[:, j, :], vecs[:, V_SEL + 1:V_SEL + 2], YA[:, j, :], ALU.mult, ALU.add,
                        [('YA', j), ('YA', j, 1), ('YB', j), 'vecs'], [('yT', 6 + j)])
            S.barrier()
            with ExitStack() as st0:
              mixT = sbuf(st0, "mixT", [128, 8, NQ], BF16)
              with ExitStack() as st:
                hxq = sbuf(st, "hxq", [128, 8, NQ], BF16)
                wg = [sbuf(st, "wg%d" % i, [128, 8, 128], BF16) for i in range(2)]
                wmg = [sbuf(st, "wmg%d" % i, [128, 8, 4, 128], BF16) for i in range(2)]
                wbr = [sbuf(st, "wbr%d" % i, [128, 2, 4, 128], BF16) for i in range(2)]
                sg = [sbuf(st, "sg%d" % i, [128, 512], BF16) for i in range(2)]
                sig = [sbuf(st, "sig%d" % i, [128, 512], F32) for i in range(2)]
                acc = [sbuf(st, "acc%d" % i, [128, 512], F32) for i in range(2)]
                for i in range(4):
                    dma('sp', hxq[:, :, i * 512:(i + 1) * 512], hxo_v[:, :, i * 512:(i + 1) * 512], [('hxo', i)], [('hxq', i)], 'hxq')
                if update_ctx:
                    dma('sp', hxq[:, :, HALF:NQ], hxf_v[:, :, 0:CTX], [('hxf', 0)], [('hxq', 4)], 'hxq')
                TB = [(i * 512, 512) for i in range(4)] + ([(HALF, 256)] if update_ctx else [])
                hxqk = [('hxq', i) for i in range(5 if update_ctx else 4)]
                for gc in range(8):
                    b = gc % 2
                    dma('pool', wg[b][:], win_v[:, :, C_GATE + gc * 128:C_GATE + (gc + 1) * 128], (), [('wg', b)], ('wg', b))
                    for ti_, (t0, n) in enumerate(TB):
                        pa, pak = nps()
                        for kc in range(8):
                            mm(pa[:, 0:n], wg[b][:, kc, :], hxq[:, kc, t0:t0 + n], kc == 0, kc == 7, [('wg', b)] + hxqk, [pak])
                        sb_ = ti_ % 2
                        act(sg[sb_][:, 0:n], pa[:, 0:n], AF.Silu, [pak], [('sg', sb_)])
                        tt('dve', yT[:, gc, t0:t0 + n], yT[:, gc, t0:t0 + n], sg[sb_][:, 0:n],
                           ALU.mult, yk + [('sg', sb_)], [('yg', gc, ti_)])
                ygk = [('yg', gc, ti_) for gc in range(8) for ti_ in range(len(TB))]
                it = 0
                for f in range(8):
                    b = f % 2
                    for k in range(4):
                        dma('pool', wmg[b][:, :, k, :], win_v[:, :, C_MERGE + k * 1024 + f * 128:C_MERGE + k * 1024 + (f + 1) * 128],
                            (), [('wmg', b, k)], ('wmg', b))
                        dma('pool', wbr[b][:, :, k, :], wbr_d[k].rearrange("(c p) n -> p c n", p=128)[:, :, f * 128:(f + 1) * 128],
                            (), [('wbr', b, k)], ('wbr', b))
                    wk4 = [('wmg', b, k) for k in range(4)] + [('wbr', b, k) for k in range(4)]
                    for (t0, n) in TB:
                        ab = it % 2
                        it += 1
                        for k in range(4):
                            pz, pzk = nps((0, 1, 2))
                            for kc in range(8):
                                mm(pz[:, 0:n], wmg[b][:, kc, k, :], hxq[:, kc, t0:t0 + n], kc == 0, kc == 7, wk4 + hxqk, [pzk])
                            pj, pjk = nps((3, 4, 5))
                            for kc in range(2):
                                mm(pj[:, 0:n], wbr[b][:, kc, k, :], yT[:, 2 * k + kc, t0:t0 + n], kc == 0, kc == 1, wk4 + ygk,
                                   [pjk])
                            sb_ = k % 2
                            act(sig[sb_][:, 0:n], pz[:, 0:n], AF.Sigmoid, [pzk], [('sig', sb_)])
                            if k == 0:
                                tt('dve', acc[ab][:, 0:n], sig[sb_][:, 0:n], pj[:, 0:n], ALU.mult, [('sig', sb_), pjk],
                                   [('acc', ab)])
                            else:
                                tt('dve', sig[sb_][:, 0:n], sig[sb_][:, 0:n], pj[:, 0:n], ALU.mult, [('sig', sb_), pjk],
                                   [('sig', sb_)])
                                if k < 3:
                                    tt('dve', acc[ab][:, 0:n], acc[ab][:, 0:n], sig[sb_][:, 0:n], ALU.add,
                                       [('sig', sb_), ('acc', ab)], [('acc', ab)])
                                else:
                                    tt('dve', mixT[:, f, t0:t0 + n], acc[ab][:, 0:n], sig[sb_][:, 0:n], ALU.add,
                                       [('sig', sb_), ('acc', ab)], [('mixT', f)])
              S.barrier()
              with ExitStack() as st:
                wo = sbuf(st, "wo", [128, 8, D], BF16)
                xt = [sbuf(st, "oxt%d" % i, [128, D], F32) for i in range(2)]
                t1 = [sbuf(st, "ot%d" % i, [128, D], F32) for i in range(2)]
                dma('pool', wo[:], wout_d.rearrange("(c p) n -> p c n", p=128), (), ['wo'], 'wo')
                mk = [('mixT', f) for f in range(8)]
                tiles = [(0, xo, i * 128, xn_out, i * 128, i * 128) for i in range(16)]
                if update_ctx:
                    tiles += [(1, cx, i * 128, cn_out, i * 128, HALF + i * 128) for i in range(2)]
                outs = []
                for ti_, (mj, src, r0, dst, d0, t0) in enumerate(tiles):
                    b = ti_ % 2
                    dma('sp', xt[b][:], src[r0:r0 + 128, :], (), [('oxt', b)], ('oxt', b))
                    for nb in range(2):
                        po, pok = nps((0, 1, 2, 3))
                        for kc in range(8):
                            mm(po[:], mixT[:, kc, t0:t0 + 128], wo[:, kc, nb * 512:(nb + 1) * 512], kc == 0, kc == 7,
                               mk + ['wo'], [pok])
                        tt('dve', t1[b][:, nb * 512:(nb + 1) * 512], po[:], G[:, mj, nb * 512:(nb + 1) * 512], ALU.mult,
                           [pok, 'G'], [('ot', b, nb)])
                        tt('pool', t1[b][:, nb * 512:(nb + 1) * 512], t1[b][:, nb * 512:(nb + 1) * 512],
                           xt[b][:, nb * 512:(nb + 1) * 512], ALU.add, [('ot', b, nb), ('oxt', b)], [('ot', b, nb)])
                    outs.append(dma('pool', dst[d0:d0 + 128, :], t1[b][:], [('ot', b, 0), ('ot', b, 1)], [('out', ti_)], ('ost', b)))
                if final:
                    S.wait_all('sp', outs)
        emit_layer(0, True, xf_in, xo_in, cx_in, x1o, c1, False)
        if os.environ.get("KSTOP"):
            P.nops = S.nops
            return P
        ccs = es.enter_context(nc.semaphore("ccs"))
        CH = 256
        nch = HALF // CH
        x1g = x1f.rearrange("(k q) n -> k q n", k=nch)
        for k in range(nch):
            nc.gpsimd.collective_compute("AllGather", ALU.bypass, replica_groups=[[0, 1], [2, 3], [4, 5], [6, 7]],
                                         ins=[x1o[k * CH:(k + 1) * CH]], outs=[x1g[k]]).then_inc(ccs, 1)

        def wait_cc():
            nc.sync.wait_ge(ccs, nch)

        def x1_rows(t0):
            hh, rem = t0 // HALF, t0 % HALF
            r = (rem // CH) * (2 * CH) + hh * CH + rem % CH
            return x1f[r:r + 128, :]

        emit_layer(1, False, x1_rows, x1o, c1, y_out, None, True, after_p0=wait_cc)
        P.nops = S.nops
    return P


def _rope_tables():
    rows = np.repeat(np.arange(SEQ // 64, dtype=np.float32), 64)
    cols = np.tile(np.arange(64, dtype=np.float32), SEQ // 64)

    def ang(rot_dim):
        q = rot_dim // 4
        fr = (np.float32(10000.0) ** (-np.arange(q, dtype=np.float32) / np.float32(q))).astype(np.float32)
        return np.concatenate([rows[:, None] * fr, cols[:, None] * fr], axis=-1).astype(np.float32)

    a64 = ang(64)
    aM = ang(32)
    c64 = np.zeros((128, SEQ), np.float32)
    s64 = np.zeros((128, SEQ), np.float32)
    for p in range(128):
        d = p % 64
        c64[p] = np.cos(a64[:, d % 32])
        s64[p] = np.sin(a64[:, d % 32]) * (-1.0 if d < 32 else 1.0)
    cM = np.zeros((128, SEQ), np.float32)
    sM = np.zeros((128, SEQ), np.float32)
    cM[0:64] = 1.0
    for p in range(64, 96):
        d = p - 64
        cM[p] = np.cos(aM[:, d % 16])
        sM[p] = np.sin(aM[:, d % 16]) * (-1.0 if d < 16 else 1.0)
    return c64, s64, cM, sM


def _const_mats():
    m = np.zeros((7, 128, 128), np.float32)
    m[M_ID] = np.eye(128)
    m[M_ONES64, 0:64, 0:64] = 1
    m[M_ONES64, 64:128, 64:128] = 1
    m[M_ONES96, 0:96, 0:96] = 1
    m[M_ONES128] = 1
    for p in range(128):
        d = p % 64
        m[M_SWAP64, (p - d) + (d + 32) % 64, p] = 1
    for p in range(64, 96):
        d = p - 64
        m[M_SWAPM, 64 + (d + 16) % 32, p] = 1
    for d in range(32):
        m[M_EKR, d, 64 + d] = 1
    return m


def _masks(h):
    k = np.arange(128)[:, None]
    q = np.arange(128)[None, :]
    prev = (k >= q).astype(np.float32)
    nxt = (k <= q).astype(np.float32)
    ones = np.ones((128, 128), np.float32)
    lv = 1.0 if h == 1 else 0.0
    rv = 1.0 if h == 0 else 0.0
    return np.concatenate([nxt, ones, prev, prev * lv, nxt * rv], axis=1)


def _vecs(p, l, h):
    v = np.zeros((128, NV), np.float32)
    v[:, V_NORMW:V_NORMW + 8] = p['norm_w'][l].reshape(8, 128).T
    v[:, V_BSHIFT:V_BSHIFT + 8] = p['b_mod'][l][0:D].reshape(8, 128).T
    v[:, V_BSCALE:V_BSCALE + 8] = p['b_mod'][l][D:2 * D].reshape(8, 128).T
    v[:, V_GCQ:V_GCQ + 2] = p['mla_cq_norm'][l].reshape(2, 128).T
    v[:, V_GCKV] = p['mla_ckv_norm'][l]
    v[0:96, V_GMQ] = p['mla_q_norm'][l]
    v[0:96, V_GMK] = p['mla_k_norm'][l]
    v[:, V_GSQ] = np.tile(p['swa_q_norm'][l], 2)
    v[:, V_GSK] = np.tile(p['swa_k_norm'][l], 2)
    v[:, V_GAQ] = np.tile(p['axa_q_norm'][l], 2)
    v[:, V_GAK] = np.tile(p['axa_k_norm'][l], 2)
    v[:, V_SINK:V_SINK + 4] = p['swa_sink'][l][None, :]
    ch = slice(h * 128, (h + 1) * 128)
    v[:, V_CONVW:V_CONVW + 4] = p['lru_conv_w'][l][:, ch].T
    v[:, V_CONVB] = p['lru_conv_b'][l][ch]
    for d in range(2):
        v[:, V_BR + d * 2] = p['lru_b_r'][l][d, ch]
        v[:, V_BI + d * 2] = p['lru_b_i'][l][d, ch]
        v[:, V_LAM + d * 2] = p['lru_lambda'][l][d, ch]
    v[:, V_SEL] = 1.0 if h == 0 else 0.0
    v[:, V_SEL + 1] = 1.0 if h == 1 else 0.0
    return v


_PROGS = {}
_CONST = {}


def kernel(**inputs):
    p = {k: np.asarray(v, dtype=np.float32) for k, v in inputs.items()}
    if 'prog' not in _PROGS:
        _PROGS['prog'] = build()
    P = _PROGS['prog']
    if 'tabs' not in _CONST:
        _CONST['tabs'] = _rope_tables()
        _CONST['mats'] = _const_mats()
    c64, s64, cM, sM = _CONST['tabs']
    A = np.ascontiguousarray
    x, ctx = p['x'], p['ctx']
    lruw = np.stack([p['lru_w_r'], p['lru_w_i']], axis=1)
    rows = A(np.stack([np.broadcast_to(p['b_mod'][l][2 * D:3 * D][None, :], (128, D)) for l in range(2)], axis=0))
    shared = {"wmod": A(p['w_mod']), "win": A(p['w_in']), "wuq": A(p['mla_w_uq']), "wukv": A(p['mla_w_ukv']),
              "wbr": A(p['w_branch']), "wout": A(p['w_out']), "rows": rows, "mats": _CONST['mats'],
              "c64f": c64, "s64f": s64, "cMf": cM, "sMf": sM}
    in_maps = []
    for core in range(8):
        b, h = core // 2, core % 2
        own = slice(h * HALF, (h + 1) * HALF)
        ext = np.concatenate([np.arange(1920, 2048), np.arange(2048, 2176), np.arange(h * HALF, (h + 1) * HALF)])
        ccv = np.stack([p['c'][b].reshape(8, 128).T, p['c_ctx'].reshape(8, 128).T], axis=-1)
        m = dict(shared)
        m.update({
            "xf": A(x[b]), "xo": A(x[b, own]), "cx": A(ctx[b]), "cc": A(ccv.astype(np.float32)),
            "vecs": A(np.stack([_vecs(p, l, h) for l in range(2)], axis=0)), "masks": _masks(h),
            "lruw": A(lruw[:, :, :, 2 * h:2 * h + 2]),
            "wlru": A(p['w_in'][:, :, C_LRU + h * 128:C_LRU + (h + 1) * 128]),
            "c64e": A(c64[:, ext]), "s64e": A(s64[:, ext]), "cMo": A(cM[:, own]), "sMo": A(sM[:, own]),
        })
        in_maps.append(m)
    res = run_bass_kernel_spmd(P.nc, in_maps, core_ids=list(range(8)))
    out = np.empty_like(x)
    for core in range(8):
        b, h = core // 2, core % 2
        out[b, h * HALF:(h + 1) * HALF] = res.results[core]["xn"]
    return out.astype(np.float32)
```
